# Optimizing a Trainium2 kernel written in Bass

```python
import math
import jax, jax.numpy as jnp
from jax import lax
import numpy as np

D_MODEL = 1024
BATCH = 16
SEQ = 256
DEPTH = 2
DEC_BATCH = 2
DEC_SEQ = 2048
PAST_LEN = 256

GRID_W = 64
ROPE_BASE = 10000.0
CONV_W = 512
CONV_K = 31
DIFF_HEADS = 4
DIFF_HD = 64
DIFF_W = DIFF_HEADS * 2 * DIFF_HD
MLA_HEADS = 8
MLA_NOPE = 64
MLA_ROPE = 32
MLA_QK = MLA_NOPE + MLA_ROPE
MLA_V = 64
MLA_Q_RANK = 384
MLA_KV_RANK = 256
MLA_W = MLA_HEADS * MLA_V
N_BRANCH = 3
D_FF = 4 * D_MODEL
IN_COLS = 2 * CONV_W + 3 * DIFF_W + MLA_Q_RANK + MLA_KV_RANK + MLA_ROPE + N_BRANCH * D_MODEL
Q_BLOCK = 128
EPS = 1e-6

kernel_name = 'hybrid_diffusion_conformer_diffattn_mla_step'


def _rms(x, g):
    xf = x.astype(jnp.float32)
    y = xf * lax.rsqrt(jnp.mean(xf * xf, axis=-1, keepdims=True) + EPS)
    return (y * g.astype(jnp.float32)).astype(x.dtype)


def _split_points():
    sizes = (2 * CONV_W, DIFF_W, DIFF_W, DIFF_W, MLA_Q_RANK, MLA_KV_RANK, MLA_ROPE)
    pts, acc = [], 0
    for s in sizes:
        acc += s
        pts.append(acc)
    return pts


def _axial_tables(seq_len, rot_dim):
    rows = seq_len // GRID_W
    row = jnp.repeat(jnp.arange(rows, dtype=jnp.float32), GRID_W)
    col = jnp.tile(jnp.arange(GRID_W, dtype=jnp.float32), rows)
    half = rot_dim // 2
    inv = ROPE_BASE ** (-jnp.arange(0, half, 2, dtype=jnp.float32) / half)
    ang = jnp.stack([row[:, None] * inv, col[:, None] * inv], axis=1)
    return jnp.cos(ang), jnp.sin(ang)


def _rope2d(x, tables):
    cos, sin = tables
    qd = x.shape[-1] // 4
    xr = x.reshape(x.shape[:-1] + (2, 2, qd)).astype(jnp.float32)
    x1, x2 = xr[..., 0, :], xr[..., 1, :]
    shape = (1, cos.shape[0]) + (1,) * (x.ndim - 3) + (2, qd)
    c = cos.reshape(shape)
    s = sin.reshape(shape)
    out = jnp.stack([x1 * c - x2 * s, x2 * c + x1 * s], axis=-2)
    return out.reshape(x.shape).astype(x.dtype)


def _sweep_queries(core, q):
    b, sq = q.shape[:2]
    nb = sq // Q_BLOCK
    qb = jnp.moveaxis(q.reshape((b, nb, Q_BLOCK) + q.shape[2:]), 1, 0)
    out = lax.map(core, qb)
    return jnp.moveaxis(out, 0, 1).reshape((b, sq) + out.shape[3:])


def _diff_attend(q, k, v, lam):
    scale = DIFF_HD ** -0.5
    def core(qb):
        s = jnp.einsum('bqhcd,bkhcd->bchqk', qb, k).astype(jnp.float32) * scale
        a = jax.nn.softmax(s, axis=-1)
        w = a[:, 0] - lam * a[:, 1]
        return jnp.einsum('bhqk,bkhe->bqhe', w.astype(v.dtype), v)
    return _sweep_queries(core, q)


def _mla_attend(q, k, v):
    scale = MLA_QK ** -0.5
    def core(qb):
        s = jnp.einsum('bqhd,bkhd->bhqk', qb, k).astype(jnp.float32) * scale
        a = jax.nn.softmax(s, axis=-1)
        return jnp.einsum('bhqk,bkhe->bqhe', a.astype(v.dtype), v)
    return _sweep_queries(core, q)


def _mla_expand(ckv, kpe, w_ukv, k_norm):
    b, l, _ = ckv.shape
    kv = (ckv @ w_ukv).reshape(b, l, MLA_HEADS, MLA_NOPE + MLA_V)
    k_nope, v = kv[..., :MLA_NOPE], kv[..., MLA_NOPE:]
    k = jnp.concatenate([k_nope, jnp.broadcast_to(kpe[:, :, None, :], (b, l, MLA_HEADS, MLA_ROPE))], axis=-1)
    return _rms(k, k_norm), v


def _conv_module(u, dw, bias, g, w_out):
    a, gate = jnp.split(u, 2, axis=-1)
    u = a * jax.nn.sigmoid(gate)
    y = lax.conv_general_dilated(u, dw[:, None, :], window_strides=(1,),
                                 padding=[(CONV_K // 2, CONV_K // 2)],
                                 dimension_numbers=('NWC', 'WIO', 'NWC'),
                                 feature_group_count=CONV_W) + bias
    return jax.nn.silu(_rms(y, g)) @ w_out


def _rope_tail(x, tables):
    return jnp.concatenate([x[..., :MLA_NOPE], _rope2d(x[..., MLA_NOPE:], tables)], axis=-1)


def _layer(x, cond, P, l, ctx=None, tabs=None):
    b, s, _ = x.shape
    mod = jax.nn.silu(cond) @ P['mod_w'][l] + P['mod_b'][l]
    sh1, sc1, g1, sh2, sc2, g2 = jnp.split(mod[:, None, :], 6, axis=-1)
    h = _rms(x, P['norm1_g'][l]) * (1 + sc1) + sh1
    proj = h @ P['w_in'][l]
    u_conv, dq, dk, dv, cq, ckv, kpe, gl = jnp.split(proj, _split_points(), axis=-1)

    o_conv = _conv_module(u_conv, P['conv_dw'][l], P['conv_b'][l], P['conv_norm_g'][l], P['w_conv_out'][l])

    q = _rms(dq.reshape(b, s, DIFF_HEADS, 2, DIFF_HD), P['diff_q_norm'][l])
    k = _rms(dk.reshape(b, s, DIFF_HEADS, 2, DIFF_HD), P['diff_k_norm'][l])
    v = dv.reshape(b, s, DIFF_HEADS, 2 * DIFF_HD)

    cq = _rms(cq, P['mla_q_a_norm'][l])
    qm = _rms((cq @ P['w_uq'][l]).reshape(b, s, MLA_HEADS, MLA_QK), P['mla_q_norm'][l])
    ckv = _rms(ckv, P['mla_kv_a_norm'][l])
    state = (k.reshape(b, s, DIFF_HEADS, 2 * DIFF_HD), v, ckv, kpe)
    km, vm = _mla_expand(ckv, kpe, P['w_ukv'][l], P['mla_k_norm'][l])

    if ctx is not None:
        ck, cv, cckv, ckpe = ctx
        q = _rope2d(q, tabs[0])
        k = _rope2d(k, tabs[0])
        qm = _rope_tail(qm, tabs[1])
        km = _rope_tail(km, tabs[1])
        lc = ck.shape[1]
        k = jnp.concatenate([ck.reshape(b, lc, DIFF_HEADS, 2, DIFF_HD), k], axis=1)
        v = jnp.concatenate([cv, v], axis=1)
        kmc, vmc = _mla_expand(cckv, ckpe, P['w_ukv'][l], P['mla_k_norm'][l])
        km = jnp.concatenate([kmc, km], axis=1)
        vm = jnp.concatenate([vmc, vm], axis=1)

    lam_init = 0.8 - 0.6 * math.exp(-0.3 * l)
    lp = P['diff_lambda'][l].astype(jnp.float32)
    lam = jnp.exp(jnp.sum(lp[0] * lp[1])) - jnp.exp(jnp.sum(lp[2] * lp[3])) + lam_init
    od = _rms(_diff_attend(q, k, v, lam), P['diff_subln'][l]) * (1 - lam_init)
    o_diff = od.reshape(b, s, DIFF_W) @ P['w_diff_out'][l]
    o_mla = _mla_attend(qm, km, vm).reshape(b, s, MLA_W) @ P['w_mla_out'][l]

    gates = jax.nn.sigmoid(gl).reshape(b, s, N_BRANCH, D_MODEL)
    merged = gates[..., 0, :] * o_conv + gates[..., 1, :] * o_diff + gates[..., 2, :] * o_mla
    x = x + g1 * (merged @ P['w_out'][l])

    h = _rms(x, P['norm2_g'][l]) * (1 + sc2) + sh2
    x = x + g2 * (jnp.square(jax.nn.relu(h @ P['w_up'][l])) @ P['w_down'][l])
    return x, state


def setup_inputs(seed: int = 0) -> dict:
    key = jax.random.key(seed)
    ks = jax.random.split(key, 40)
    f = jnp.float32
    def nrm(i, shape, scale=1.0):
        return jax.random.normal(ks[i], shape, f) * scale
    def gain(i, shape):
        return 1.0 + 0.01 * jax.random.normal(ks[i], shape, f)
    return {
        'x_prompt': nrm(0, (BATCH, SEQ, D_MODEL)),
        'x_sample': nrm(1, (DEC_BATCH, DEC_SEQ, D_MODEL)),
        'cache_diff_k': nrm(2, (DEC_BATCH, DEPTH, PAST_LEN, DIFF_HEADS, 2 * DIFF_HD)),
        'cache_diff_v': nrm(3, (DEC_BATCH, DEPTH, PAST_LEN, DIFF_HEADS, 2 * DIFF_HD)),
        'cache_mla_ckv': nrm(4, (DEC_BATCH, DEPTH, PAST_LEN, MLA_KV_RANK)),
        'cache_mla_kpe': nrm(5, (DEC_BATCH, DEPTH, PAST_LEN, MLA_ROPE)),
        'c': nrm(6, (DEC_BATCH, D_MODEL)),
        'c_ctx': nrm(7, (D_MODEL,)),
        'mod_w': nrm(8, (DEPTH, D_MODEL, 6 * D_MODEL), D_MODEL ** -0.5),
        'mod_b': nrm(9, (DEPTH, 6 * D_MODEL), 0.01),
        'norm1_g': gain(10, (DEPTH, D_MODEL)),
        'w_in': nrm(11, (DEPTH, D_MODEL, IN_COLS), D_MODEL ** -0.5),
        'conv_dw': nrm(12, (DEPTH, CONV_K, CONV_W), CONV_K ** -0.5),
        'conv_b': nrm(13, (DEPTH, CONV_W), 0.01),
        'conv_norm_g': gain(14, (DEPTH, CONV_W)),
        'w_conv_out': nrm(15, (DEPTH, CONV_W, D_MODEL), CONV_W ** -0.5),
        'diff_q_norm': gain(16, (DEPTH, DIFF_HD)),
        'diff_k_norm': gain(17, (DEPTH, DIFF_HD)),
        'diff_lambda': nrm(18, (DEPTH, 4, DIFF_HD), 0.1),
        'diff_subln': gain(19, (DEPTH, 2 * DIFF_HD)),
        'w_diff_out': nrm(20, (DEPTH, DIFF_W, D_MODEL), DIFF_W ** -0.5),
        'mla_q_a_norm': gain(21, (DEPTH, MLA_Q_RANK)),
        'mla_kv_a_norm': gain(22, (DEPTH, MLA_KV_RANK)),
        'w_uq': nrm(23, (DEPTH, MLA_Q_RANK, MLA_HEADS * MLA_QK), MLA_Q_RANK ** -0.5),
        'w_ukv': nrm(24, (DEPTH, MLA_KV_RANK, MLA_HEADS * (MLA_NOPE + MLA_V)), MLA_KV_RANK ** -0.5),
        'mla_q_norm': gain(25, (DEPTH, MLA_QK)),
        'mla_k_norm': gain(26, (DEPTH, MLA_QK)),
        'w_mla_out': nrm(27, (DEPTH, MLA_W, D_MODEL), MLA_W ** -0.5),
        'w_out': nrm(28, (DEPTH, D_MODEL, D_MODEL), D_MODEL ** -0.5),
        'norm2_g': gain(29, (DEPTH, D_MODEL)),
        'w_up': nrm(30, (DEPTH, D_MODEL, D_FF), D_MODEL ** -0.5),
        'w_down': nrm(31, (DEPTH, D_FF, D_MODEL), D_FF ** -0.5),
    }


def reference(x_prompt, x_sample, cache_diff_k, cache_diff_v, cache_mla_ckv, cache_mla_kpe, c, c_ctx,
              mod_w, mod_b, norm1_g, w_in, conv_dw, conv_b, conv_norm_g, w_conv_out,
              diff_q_norm, diff_k_norm, diff_lambda, diff_subln, w_diff_out,
              mla_q_a_norm, mla_kv_a_norm, w_uq, w_ukv, mla_q_norm, mla_k_norm, w_mla_out,
              w_out, norm2_g, w_up, w_down):
    P = dict(mod_w=mod_w, mod_b=mod_b, norm1_g=norm1_g, w_in=w_in, conv_dw=conv_dw, conv_b=conv_b,
             conv_norm_g=conv_norm_g, w_conv_out=w_conv_out, diff_q_norm=diff_q_norm,
             diff_k_norm=diff_k_norm, diff_lambda=diff_lambda, diff_subln=diff_subln,
             w_diff_out=w_diff_out, mla_q_a_norm=mla_q_a_norm, mla_kv_a_norm=mla_kv_a_norm,
             w_uq=w_uq, w_ukv=w_ukv, mla_q_norm=mla_q_norm, mla_k_norm=mla_k_norm,
             w_mla_out=w_mla_out, w_out=w_out, norm2_g=norm2_g, w_up=w_up, w_down=w_down)

    xp = x_prompt
    cond_ctx = c_ctx[None, :]
    st_k, st_v, st_ckv, st_kpe = [], [], [], []
    for l in range(DEPTH):
        xp, (sk, sv, sckv, skpe) = _layer(xp, cond_ctx, P, l)
        st_k.append(sk)
        st_v.append(sv)
        st_ckv.append(sckv)
        st_kpe.append(skpe)
    y_prompt = xp
    new_diff_k = jnp.stack(st_k, axis=1)
    new_diff_v = jnp.stack(st_v, axis=1)
    new_mla_ckv = jnp.stack(st_ckv, axis=1)
    new_mla_kpe = jnp.stack(st_kpe, axis=1)

    n_lat = x_sample.shape[1]
    tabs = (_axial_tables(n_lat, DIFF_HD), _axial_tables(n_lat, MLA_ROPE))
    xs = x_sample
    for l in range(DEPTH):
        ctx = (cache_diff_k[:, l], cache_diff_v[:, l], cache_mla_ckv[:, l], cache_mla_kpe[:, l])
        xs, _ = _layer(xs, c, P, l, ctx=ctx, tabs=tabs)
    y_sample = xs
    return (y_prompt, y_sample, new_diff_k, new_diff_v, new_mla_ckv, new_mla_kpe)
```

```python
import contextlib
import math
import numpy as np
import concourse.bass as bass
import concourse.mybir as mybir
from concourse.bass_utils import run_bass_kernel_spmd

F32 = mybir.dt.float32
BF16 = mybir.dt.bfloat16
AF = mybir.ActivationFunctionType
ALU = mybir.AluOpType

D = 1024
DEPTH = 2
T = 1024
HALF = 512
EPS = 1e-6
IN_COLS = 6304
NCOLS = 526
GATE0 = 3232


class Buf:
    __slots__ = ("name", "w", "r", "rd", "rpe")

    def __init__(self, name):
        self.name = name
        self.w = None
        self.r = {}
        self.rd = []
        self.rpe = []


class Op:
    __slots__ = ("eng", "kind", "fn", "deps", "gdeps", "flag", "sem", "val", "pidx", "ape", "T", "wb", "cols", "pos")

    def __init__(self, eng, kind, fn):
        self.eng = eng
        self.kind = kind
        self.fn = fn
        self.deps = []
        self.gdeps = []
        self.flag = False
        self.sem = None
        self.val = 0
        self.pidx = -1
        self.ape = -1
        self.T = None
        self.wb = ()
        self.cols = 512
        self.pos = -1


class Sched:
    ENGS = ("pe", "act", "dve", "pool", "sp")
    HND = {"pe": "tensor", "act": "scalar", "dve": "vector", "pool": "gpsimd", "sp": "sync"}
    REORDER = True

    def __init__(self, nc, n_dma_sems=14):
        self.nc = nc
        self.ops = {e: [] for e in self.ENGS}
        self.n_dma_sems = n_dma_sems
        self.npe = 0
        self.last_ape = {}

    def op(self, eng, fn, reads=(), writes=(), kind="c", cols=512):
        o = Op(eng, kind, fn)
        o.cols = cols
        is_pe = (eng == "pe" and kind == "c")
        deps = []
        for b in reads:
            if b.w is not None:
                deps.append(b.w)
        for b in writes:
            if b.w is not None:
                deps.append(b.w)
            deps.extend(b.r.values())
            deps.extend(b.rd)
            if b.rpe and not is_pe:
                o.gdeps.append(b.rpe)
        seen = set()
        ape = -1
        for d in deps:
            if d is o or id(d) in seen:
                continue
            seen.add(id(d))
            if is_pe and d.eng == "pe" and d.kind == "c":
                continue
            o.deps.append(d)
            ape = max(ape, d.pidx if (d.eng == "pe" and d.kind == "c") else d.ape)
        for g in o.gdeps:
            ape = max(ape, g[-1].pidx)
        if is_pe:
            o.pidx = self.npe
            self.npe += 1
            o.wb = tuple(writes)
        else:
            ape = max(ape, self.last_ape.get(eng, -1))
            self.last_ape[eng] = ape
        o.ape = ape
        for b in writes:
            b.w = o
            b.r = {}
            b.rd = []
            b.rpe = []
        for b in reads:
            if b.w is o:
                continue
            if is_pe:
                b.rpe.append(o)
            elif kind == "c":
                b.r[eng] = o
            else:
                b.rd.append(o)
        self.ops[eng].append(o)
        return o

    def reorder_pe(self):
        ops = self.ops["pe"]
        import os
        COST = {"c": float(os.environ.get("KCOST", "2.0")), "d": 4.0, "x": 30.0}
        fin = {}
        out = []
        D = []
        now = [0.0]
        inD = set()

        def Tof(o):
            if o.T is not None:
                return o.T
            stack = [o]
            while stack:
                x = stack[-1]
                if x.T is not None:
                    stack.pop()
                    continue
                t = 0.0
                pending = False
                for d in x.deps:
                    if d.eng == "pe" and d.kind == "c":
                        t = max(t, fin[d.pidx])
                    elif d.T is None:
                        stack.append(d)
                        pending = True
                    else:
                        t = max(t, d.T)
                if pending:
                    continue
                for g in x.gdeps:
                    t = max(t, max(fin[m.pidx] for m in g))
                x.T = t + COST[x.kind]
                stack.pop()
            return o.T

        def ready_time(y):
            t = 0.0
            for d in y.deps:
                t = max(t, Tof(d))
            return t

        def emit(y, forced=False):
            r = ready_time(y)
            now[0] = max(now[0], r) + max(0.035, y.cols / 1900.0 + 0.016)
            fin[y.pidx] = now[0]
            y.pos = len(out)
            out.append(y)

        def can_go(idx):
            x = D[idx]
            for e in D[:idx]:
                if x.ape >= e.pidx:
                    return False
                for b in x.wb:
                    if b in e.wb:
                        return False
            return True

        def drain(limit):
            while D:
                best = None
                cand = None
                for idx in range(len(D)):
                    if not can_go(idx):
                        continue
                    r = ready_time(D[idx])
                    if r <= now[0] + 0.05:
                        best = idx
                        break
                    if cand is None or r < cand[0]:
                        cand = (r, idx)
                if best is None:
                    if len(D) <= limit:
                        return
                    best = cand[1]
                emit(D.pop(best))

        CAP = int(os.environ.get("KCAP", "14")) if self.REORDER else 0
        for y in ops:
            D.append(y)
            drain(CAP)
        drain(0)
        assert len(out) == len(ops)
        self.ops["pe"] = out

    def emit(self, final_waits=()):
        nc = self.nc
        self.reorder_pe()
        for e in self.ENGS:
            for o in self.ops[e]:
                for g in o.gdeps:
                    o.deps.append(max(g, key=lambda m: m.pos))
                o.gdeps = []
        for e in self.ENGS:
            for o in self.ops[e]:
                for d in o.deps:
                    d.flag = True
        for o in final_waits:
            o.flag = True
        with contextlib.ExitStack() as es:
            esem = {e: es.enter_context(nc.semaphore("s_" + e)) for e in self.ENGS}
            dsem = {e: [es.enter_context(nc.semaphore(f"d_{e}{i}")) for i in range(self.n_dma_sems)]
                    for e in ("sp", "pool")}
            ncc = sum(1 for e in self.ENGS for o in self.ops[e] if o.kind == "x")
            csem = [es.enter_context(nc.semaphore(f"cc{i}")) for i in range(ncc)]
            ci = 0
            for e in self.ENGS:
                cnt = 0
                dcnt = [0] * self.n_dma_sems
                dlast = [None] * self.n_dma_sems
                di = 0
                for o in self.ops[e]:
                    if o.kind == "c":
                        if o.flag:
                            cnt += 1
                            o.sem = esem[e]
                            o.val = cnt
                    elif o.kind == "d":
                        k = di % self.n_dma_sems
                        di += 1
                        dcnt[k] += 16
                        o.sem = dsem[e][k]
                        o.val = dcnt[k]
                        if dlast[k] is not None:
                            o.deps.append(dlast[k])
                        dlast[k] = o
                    else:
                        o.sem = csem[ci]
                        o.val = 1
                        ci += 1
            block = es.enter_context(nc.Block())

            def make(e):
                def body(eng):
                    seen = {}
                    for o in self.ops[e]:
                        need = {}
                        for d in o.deps:
                            key = id(d.sem)
                            if seen.get(key, 0) >= d.val:
                                continue
                            if key not in need or need[key][1] < d.val:
                                need[key] = (d.sem, d.val)
                        for key, (s, v) in need.items():
                            eng.wait_ge(s, v)
                            seen[key] = v
                        ins = o.fn(eng)
                        if o.kind == "c":
                            if o.flag:
                                ins.then_inc(o.sem, 1)
                        elif o.kind == "d":
                            ins.then_inc(o.sem, 16)
                        else:
                            ins.then_inc(o.sem)
                    if e == "sp":
                        for o in final_waits:
                            if seen.get(id(o.sem), 0) < o.val:
                                eng.wait_ge(o.sem, o.val)
                                seen[id(o.sem)] = o.val
                return body

            for e in self.ENGS:
                getattr(block, self.HND[e])(make(e))


class Grid:
    def __init__(self, name):
        self.name = name
        self.d = {}

    def __getitem__(self, key):
        b = self.d.get(key)
        if b is None:
            b = self.d[key] = Buf(f"{self.name}{key}")
        return b

    def all(self):
        return list(self.d.values())


class KB:
    def __init__(self, dbg=()):
        self.dbg = set(dbg)
        self.nc = bass.Bass("TRN2", target_bir_lowering=False)
        self.S = Sched(self.nc)
        self.es = contextlib.ExitStack()
        self.outs = []

    def MM(self, out, lhsT, rhs, st, sp, R, W, **kw):
        return self.S.op("pe", lambda e: e.matmul(out, lhsT=lhsT, rhs=rhs, start=st, stop=sp, **kw), R, W, cols=int(rhs.shape[-1]))

    def ACT(self, out, in_, func, R, W, bias=None, scale=None):
        kw = {}
        if bias is not None:
            kw["bias"] = bias
        if scale is not None:
            kw["scale"] = scale
        return self.S.op("act", lambda e: e.activation(out=out, in_=in_, func=func, **kw), R, W)

    def TS(self, out, in0, s1, s2, op0, op1, R, W, eng="dve"):
        if s2 is None:
            return self.S.op(eng, lambda e: e.tensor_scalar(out=out, in0=in0, scalar1=s1, scalar2=None, op0=op0), R, W)
        return self.S.op(eng, lambda e: e.tensor_scalar(out=out, in0=in0, scalar1=s1, scalar2=s2, op0=op0, op1=op1), R, W)

    def TT(self, out, in0, in1, op, R, W, eng="dve"):
        return self.S.op(eng, lambda e: e.tensor_tensor(out=out, in0=in0, in1=in1, op=op), R, W)

    def STT(self, out, in0, scalar, in1, op0, op1, R, W, eng="dve"):
        return self.S.op(eng, lambda e: e.scalar_tensor_tensor(out=out, in0=in0, scalar=scalar, in1=in1, op0=op0, op1=op1), R, W)

    def CP(self, out, in_, R, W, eng="dve"):
        return self.S.op(eng, lambda e: e.tensor_copy(out=out, in_=in_), R, W)

    def MS(self, ap, val, W, eng="dve"):
        return self.S.op(eng, lambda e: e.memset(ap, val), (), W)

    def DMA(self, out, in_, R, W, q="sp"):
        return self.S.op(q, lambda e: e.dma_start(out=out, in_=in_), R, W, kind="d")

    def dump(self, name, ap, R, dt=F32):
        import os
        if os.environ.get("KDUMP", "") == "":
            return
        shape = list(ap.shape)
        t = self.nc.dram_tensor("dbg_" + name, shape, dt, kind="ExternalOutput").ap()
        self.outs.append(self.DMA(t, ap, R, []))
        self.dbg_names = getattr(self, "dbg_names", []) + ["dbg_" + name]

    def FENCE(self, R, W):
        sc = self.scratch
        return self.S.op("dve", lambda e: e.memset(sc[0:1, 0:1], 0.0), (), list(R) + list(W) + [self.b_scratch])

    def sb(self, name, shape, dt):
        return self.es.enter_context(self.nc.sbuf_tensor(name, shape, dt))

    def dram_in(self, name, shape, dt=F32):
        return self.nc.dram_tensor(name, list(shape), dt, kind="ExternalInput").ap()

    def dram_out(self, name, shape, dt=F32):
        return self.nc.dram_tensor(name, list(shape), dt, kind="ExternalOutput").ap()

    def tf(self):
        i = self._tfi % len(self.TF)
        self._tfi += 1
        return self.TF[i], self.b_TF[i]

    def tb(self):
        i = self._tbi % len(self.TB)
        self._tbi += 1
        return self.TB[i], self.b_TB[i]

    def psn(self, pool):
        lst = self.pspools[pool]
        i = self._psi.get(pool, 0)
        self._psi[pool] = i + 1
        b = lst[i % len(lst)]
        return self.PS[b], self.b_PS[b]

    def wnext(self, name, hold=0):
        i = self._wi
        self._wi += 1
        spec = self.wlist[i]
        assert spec[0] == name, (spec[0], name)
        nslot = len(self.WS)
        while self._wissued < min(len(self.wlist), i + nslot - hold):
            j = self._wissued
            nm, src, view = self.wlist[j]
            s = j % nslot
            if isinstance(src, list):
                for bi, sp_ in enumerate(src):
                    self.DMA(view(self.WS[s])[:, :, bi, :], sp_, (), [self.b_WS[s]], q="pool")
            else:
                self.DMA(view(self.WS[s]), src, (), [self.b_WS[s]], q="pool")
            self._wissued += 1
        s = i % nslot
        return spec[2](self.WS[s]), self.b_WS[s]

    def build(self):
        nc = self.nc
        self.xT = self.dram_in("xT", [D, T])
        self.cond = self.dram_in("cond", [128, 8, 2])
        self.cols_d = self.dram_in("cols", [128, DEPTH, NCOLS])
        self.constm = self.dram_in("constm", [128, 6, 128])
        self.rope = self.dram_in("rope", [128, 4, HALF])
        self.sel_d = self.dram_in("sel", [128, 8])
        self.ckT = self.dram_in("ckT", [DEPTH, 512, 256])
        self.cv = self.dram_in("cv", [DEPTH, 256, 512])
        self.cckvT = self.dram_in("cckvT", [DEPTH, 256, 256])
        self.ckpeT = self.dram_in("ckpeT", [DEPTH, 32, 256])
        self.mod_w = self.dram_in("mod_w", [DEPTH, D, 6 * D])
        self.w_in = self.dram_in("w_in", [DEPTH, D, IN_COLS])
        self.w_conv_out = self.dram_in("w_conv_out", [DEPTH, 512, D])
        self.w_diff_out = self.dram_in("w_diff_out", [DEPTH, 512, D])
        self.w_uq = self.dram_in("w_uq", [DEPTH, 384, 768])
        self.w_ukv = self.dram_in("w_ukv", [DEPTH, 256, 1024])
        self.w_mla_out = self.dram_in("w_mla_out", [DEPTH, 512, D])
        self.w_out = self.dram_in("w_out", [DEPTH, D, D])
        self.w_up = self.dram_in("w_up", [DEPTH, D, 4 * D])
        self.w_down = self.dram_in("w_down", [DEPTH, 4 * D, D])

        self.yT = self.dram_out("yT", [D, T])
        self.o_dk = self.dram_out("o_dk", [DEPTH, 512, HALF])
        self.o_dv = self.dram_out("o_dv", [DEPTH, HALF, 512])
        self.o_ckv = self.dram_out("o_ckv", [DEPTH, 256, HALF])
        self.o_kpe = self.dram_out("o_kpe", [DEPTH, 32, HALF])

        def dint(name, shape):
            return nc.dram_tensor(name, list(shape), BF16)
        self.kx_in = [dint(f"kx_in{l}", [512, 512]) for l in range(DEPTH)]
        self.kx_out = [dint(f"kx_out{l}", [2048, 512]) for l in range(DEPTH)]
        self.vx_in = [dint(f"vx_in{l}", [512, 512]) for l in range(DEPTH)]
        self.vx_out = [dint(f"vx_out{l}", [2048, 512]) for l in range(DEPTH)]
        self.kmx_in = [dint(f"kmx_in{l}", [768, 512]) for l in range(DEPTH)]
        self.kmx_out = [dint(f"kmx_out{l}", [3072, 512]) for l in range(DEPTH)]
        self.vmx_in = [dint(f"vmx_in{l}", [512, 512]) for l in range(DEPTH)]
        self.vmx_out = [dint(f"vmx_out{l}", [2048, 512]) for l in range(DEPTH)]
        self.hb_in = [dint(f"hb_in{l}", [512, 32]) for l in range(DEPTH)]
        self.hb_out = [dint(f"hb_out{l}", [2048, 32]) for l in range(DEPTH)]
        self.b_x = {n: [Buf(f"{n}_in{l}") for l in range(DEPTH)] for n in ("kx", "vx", "kmx", "vmx", "hb")}
        self.b_xo = {n: [Buf(f"{n}_out{l}") for l in range(DEPTH)] for n in ("kx", "vx", "kmx", "vmx", "hb")}

        sb = self.sb
        self.X = sb("X", [128, 8, T], F32)
        self.bX = Grid("X")
        self.HT = sb("HT", [128, 8, T], BF16)
        self.bHT = Grid("HT")
        self.RB = sb("RB", [128, 8192], BF16)
        RB = self.RB
        self.ODT = RB[:, 0:4096].rearrange("p (c t) -> p c t", c=4)
        self.OMT = RB[:, 4096:8192].rearrange("p (c t) -> p c t", c=4)
        self.bODT = Grid("ODT")
        self.bOMT = Grid("OMT")
        NSLOT = 3
        self.WS = [sb(f"WS{i}", [128, 4096], BF16) for i in range(NSLOT)]
        self.b_WS = [Buf(f"WS{i}") for i in range(NSLOT)]
        self.TF = [sb(f"TF{i}", [128, HALF], F32) for i in range(7)]
        self.b_TF = [Buf(f"TF{i}") for i in range(7)]
        self.TB = [sb(f"TB{i}", [128, HALF], BF16) for i in range(7)]
        self.b_TB = [Buf(f"TB{i}") for i in range(7)]
        self._tfi = self._tbi = 0
        self.CACC = sb("CACC", [128, T], F32)
        self.bACC = [Buf("ACC0"), Buf("ACC1")]
        self.DA = [sb(f"DA{i}", [128, HALF], F32) for i in range(2)]
        self.b_DA = [Buf(f"DA{i}") for i in range(2)]
        self.RS = [sb(f"RS{i}", [128, HALF], F32) for i in range(2)]
        self.b_RS = [Buf(f"RS{i}") for i in range(2)]
        self.CB = sb("CB", [128, 6, 128], BF16)
        self.bCB = Buf("CB")
        self.ROPE = sb("ROPE", [128, 4, HALF], F32)
        self.bROPE = Buf("ROPE")
        self.SEL = sb("SEL", [128, 8], F32)
        self.bSEL = Buf("SEL")
        self.COLS = sb("COLS", [128, DEPTH, NCOLS], F32)
        self.bCOLS = Buf("COLS")
        self.CONDF = sb("CONDF", [128, 8, 2], F32)
        self.CONDB = sb("CONDB", [128, 8, 2], BF16)
        self.bCOND = Buf("COND")
        self.bCONDB = Buf("CONDB")
        self.MODC_ = sb("MODC", [128, DEPTH, 48, 2], F32)
        self.bMC = Grid("MC")
        self.MODA_ = sb("MODA", [128, DEPTH, 2, 8, 2], F32)
        self.bMA = Grid("MA")
        self.LAM_ = sb("LAM", [128, DEPTH, 8], F32)
        self.bLAMg = Grid("LAM")
        self.LTMP = sb("LTMP", [128, 2, 64], F32)
        self.bLT = Buf("LTMP")
        self.scratch = sb("scr", [128, 2], F32)
        self.b_scratch = Buf("scr")
        self.WK = sb("WK", [128, 2, 8, 96], BF16)
        self.WV = sb("WV", [128, 2, 512], BF16)
        self.bWK = Buf("WK")
        self.bWV = Buf("WV")
        self.UP = sb("UP", [128, 4, 2, 286], BF16)
        self.US = sb("US", [128, 4, 542], BF16)
        self.bUP = Grid("UP")
        self.bUS = Grid("US")
        self.HL = sb("HL", [128, 4, 4, 32], BF16)
        self.bHL = Buf("HL")
        self.HLF = sb("HLF", [128, 4, 2, 16], F32)
        self.bHLF = Buf("HLF")

        o = 0
        self.CQN = RB[:, o:o + 3072].rearrange("p (c t) -> p c t", c=3); o += 3072
        self.CKVN = RB[:, o:o + 2048].rearrange("p (c t) -> p c t", c=2); o += 2048
        self.KPE = RB[:, o:o + 1024]; o += 1024
        self.CC = RB[:, o:o + 512].rearrange("p (c t) -> p c t", c=2); o += 512
        self.CKPE = RB[:, o:o + 256]; o += 256
        RA_N = 28544
        self.RA = sb("RA", [128, RA_N], BF16)
        RA = self.RA
        off = [0]

        def carve(n):
            a = off[0]
            off[0] += n
            return a
        o = carve(4096); self.QT = RA[:, o:o + 4096].rearrange("p (c t) -> p c t", c=4)
        o = carve(8192); self.QMT = RA[:, o:o + 8192].rearrange("p (c t) -> p c t", c=8)
        o = carve(2048); self.KMTC = RA[:, o:o + 2048].rearrange("p (c t) -> p c t", c=8)
        o = carve(1024); self.VMC = RA[:, o:o + 1024].rearrange("p (c t) -> p c t", c=2)
        o = carve(1024); self.KC = RA[:, o:o + 1024].rearrange("p (c t) -> p c t", c=4)
        o = carve(1024); self.VC = RA[:, o:o + 1024].rearrange("p (c t) -> p c t", c=2)
        un = carve(10240)
        o = un
        self.KTP = RA[:, o:o + 2048].rearrange("p (c t) -> p c t", c=4); o += 2048
        self.VP = RA[:, o:o + 2048].rearrange("p (c t) -> p c t", c=4); o += 2048
        self.KMTP = RA[:, o:o + 4096].rearrange("p (c t) -> p c t", c=8); o += 4096
        self.VMP = RA[:, o:o + 2048].rearrange("p (c t) -> p c t", c=4); o += 2048
        self.KS = [RA[:, un + i * 2048: un + (i + 1) * 2048] for i in range(2)]
        self.VS = [RA[:, un + 4096 + i * 2048: un + 4096 + (i + 1) * 2048].rearrange("p (n e) -> p n e", n=16) for i in range(2)]
        life1_end = off[0]
        o2 = 0
        self.ST = RA[:, o2:o2 + 4096].rearrange("p (c t) -> p c t", c=4); o2 += 4096
        self.DG = RA[:, o2:o2 + 3968].rearrange("p (k c) -> p k c", k=31); o2 += 3968
        self.MT = RA[:, o2:o2 + 8192].rearrange("p (c t) -> p c t", c=8); o2 += 8192
        self.AT = RA[:, o2:o2 + 8192].rearrange("p (c t) -> p c t", c=8)
        self.WCO = RA[:, o2:o2 + 4096].rearrange("p (k c) -> p k c", k=4); o2 += 4096
        self.WDO = RA[:, o2:o2 + 4096].rearrange("p (k c) -> p k c", k=4); o2 += 4096
        self.WMO = RA[:, o2:o2 + 4096].rearrange("p (k c) -> p k c", k=4); o2 += 4096
        assert o2 <= RA_N and life1_end <= RA_N, (o2, life1_end)
        self.bQT = Grid("QT"); self.bKTP = Grid("KTP"); self.bVP = Grid("VP"); self.bQMT = Grid("QMT")
        self.bKMTP = Grid("KMTP"); self.bVMP = Grid("VMP"); self.bKMTC = Grid("KMTC"); self.bVMC = Grid("VMC")
        self.bKC = Buf("KC"); self.bVC = Buf("VC")
        self.bKS = [Buf("KS0"), Buf("KS1")]; self.bVS = [Buf("VS0"), Buf("VS1")]
        self.bCQN = Grid("CQN"); self.bCKVN = Grid("CKVN"); self.bKPE = Grid("KPE"); self.bCC = Buf("CC"); self.bCKPE = Buf("CKPE")
        self.bST = Grid("ST"); self.bDG = Buf("DG"); self.bMT = Grid("MT"); self.bAT = Grid("AT")
        self.bWCO = Buf("WCO"); self.bWDO = Buf("WDO"); self.bWMO = Buf("WMO")
        for g, n1, n2 in ((self.bQT, 4, 2), (self.bQMT, 8, 2), (self.bODT, 4, 2), (self.bOMT, 8, 2), (self.bCQN, 3, 2), (self.bCKVN, 2, 2),
                          (self.bST, 4, 2), (self.bMT, 8, 2), (self.bAT, 8, 2)):
            for a in range(n1):
                for b_ in range(n2):
                    g[a, b_]
        for g, n1 in ((self.bKTP, 4), (self.bVP, 4), (self.bKMTP, 8), (self.bVMP, 4), (self.bKMTC, 8), (self.bVMC, 2), (self.bKPE, 2)):
            for a in range(n1):
                g[a]

        self.PS = [self.es.enter_context(nc.psum_tensor(f"ps{i}", [128, HALF], F32)) for i in range(8)]
        self.b_PS = [Buf(f"ps{i}") for i in range(8)]
        self.pspools = {"d": [0, 1, 2, 3, 4, 5], "s": [6, 7], "a": [0, 1, 2, 3]}
        self._psi = {}

        self.wlist = []

        def v3(nk, ncol):
            return lambda slot: slot[:, 0:nk * ncol].rearrange("p (k c) -> p k c", k=nk)
        specs = {}
        for l in range(DEPTH):
            mw = self.mod_w[l].rearrange("(k p) c -> p k c", p=128)
            wi = self.w_in[l].rearrange("(k p) c -> p k c", p=128)
            for j in range(12):
                specs[f"mod{l}_{j}"] = (mw[:, :, 512 * j:512 * (j + 1)], v3(8, 512))
            for nm, a, b in (("ua", 0, 512), ("ug", 512, 1024), ("dq", 1024, 1536), ("dk", 1536, 2048),
                             ("dv", 2048, 2560), ("cq", 2560, 2944), ("ckv", 2944, 3232)):
                specs[f"{nm}{l}"] = (wi[:, :, a:b], v3(8, b - a))
            specs[f"uq{l}"] = (self.w_uq[l].rearrange("(k p) c -> p k c", p=128), v3(3, 768))
            wg = self.w_in[l][:, GATE0:IN_COLS].rearrange("(k p) (b j c) -> p k b j c", p=128, b=3, j=8)
            for j in range(8):
                specs[f"mg{l}_{j}"] = ([wg[:, :, b_, j, :] for b_ in range(3)],
                                       lambda slot: slot[:, 0:8 * 384].rearrange("p (k b c) -> p k b c", k=8, b=3))
            wo = self.w_out[l].rearrange("(k p) c -> p k c", p=128)
            for j in range(2):
                specs[f"wo{l}_{j}"] = (wo[:, :, 512 * j:512 * (j + 1)], v3(8, 512))
            wu = self.w_up[l].rearrange("(k p) c -> p k c", p=128)
            wd = self.w_down[l].rearrange("(q k p) c -> q p k c", q=4, p=128)
            for q in range(4):
                for j in range(2):
                    specs[f"up{l}_{q}_{j}"] = (wu[:, :, 1024 * q + 512 * j: 1024 * q + 512 * (j + 1)], v3(8, 512))
                for j in range(2):
                    specs[f"dn{l}_{q}_{j}"] = (wd[q][:, :, 512 * j:512 * (j + 1)], v3(8, 512))
        order = []
        for l in range(DEPTH):
            if l == 0:
                order += [f"mod0_{j}" for j in range(4)]
            order += [f"ua{l}", f"ug{l}", f"dq{l}", f"dk{l}", f"dv{l}", f"cq{l}", f"ckv{l}", f"uq{l}"]
            order += [f"mod{l}_{j}" for j in range(4, 12)]
            order += [f"mg{l}_{j}" for j in range(8)] + [f"wo{l}_0", f"wo{l}_1"]
            for q in range(4):
                order += [f"up{l}_{q}_0", f"up{l}_{q}_1", f"dn{l}_{q}_0", f"dn{l}_{q}_1"]
                if l + 1 < DEPTH and q < 2:
                    order += [f"mod{l + 1}_{2 * q}", f"mod{l + 1}_{2 * q + 1}"]
        self.wlist = [(nm,) + specs[nm] for nm in order]
        self._wi = 0
        self._wissued = 0

        self.prologue()
        try:
            for l in range(DEPTH):
                self.layer(l)
        except StopIteration:
            pass
        finals = self.epilogue()
        self.S.emit(final_waits=finals + self.outs)
        return nc

    def prologue(self):
        self.DMA(self.CONDF[:], self.cond, (), [self.bCOND])
        self.DMA(self.COLS[:], self.cols_d, (), [self.bCOLS])
        for k in range(8):
            for h in range(2):
                self.DMA(self.X[:, k, h * HALF:(h + 1) * HALF], self.xT[k * 128:(k + 1) * 128, h * HALF:(h + 1) * HALF],
                         (), [self.bX[k, h]])
        self.DMA(self.ROPE[:], self.rope, (), [self.bROPE])
        self.DMA(self.SEL[:], self.sel_d, (), [self.bSEL])
        self.DMA(self.CB[:], self.constm, (), [self.bCB], q="pool")
        self.ACT(self.CONDB[:], self.CONDF[:], AF.Silu, [self.bCOND], [self.bCONDB])
        for j in range(4):
            self.MS(self.UP[:, j], 0.0, [self.bUP[j]], eng="pool")
        self.MS(self.WK[:], 0.0, [self.bWK], eng="pool")
        self.MS(self.HLF[:], 0.0, [self.bHLF], eng="pool")

    def col(self, l, a, n=1):
        return self.COLS[:, l, a:a + n]

    def layer(self, l):
        mla_t = self.bCQN.all() + self.bCKVN.all() + self.bKPE.all() + [self.bCC, self.bCKPE]
        ponly = self.bKTP.all() + self.bVP.all() + self.bKMTP.all() + self.bVMP.all()
        life1 = (self.bQT.all() + self.bQMT.all() + self.bKMTC.all() + self.bVMC.all() + [self.bKC, self.bVC]
                 + self.bKS + self.bVS + ponly)
        wo3 = [self.bWCO, self.bWDO, self.bWMO]
        life2 = self.bST.all() + [self.bDG] + self.bMT.all() + wo3
        obr = self.bODT.all() + self.bOMT.all()
        import os
        stop = os.environ.get("KSTOP", "")

        def chk(name):
            if stop == f"{name}{l}":
                raise StopIteration
        self.chk = chk
        self.stage_lam(l)
        if l == 0:
            for j in range(4):
                self.mod_block(0, j)
        chk("mod")
        self.stage_norm(l, 0)
        if l == 0:
            self.dump("ht", self.HT[:].rearrange("p k t -> p (k t)"), self.bHT.all(), BF16)
        chk("norm")
        self.stage_convproj(l)
        if l == 0:
            self.dump("us", self.US[:, :, 15:527], [self.bUS[j] for j in range(4)], BF16)
            self.dump("up", self.UP[:, :, :, 15:271], [self.bUP[j] for j in range(4)], BF16)
        chk("convproj")
        self.stage_diffproj(l)
        if l == 0:
            self.dump("qt", self.QT, self.bQT.all(), BF16)
            self.dump("ktp", self.KTP, self.bKTP.all(), BF16)
            self.dump("vp", self.VP, self.bVP.all(), BF16)
            self.dump("kxin", self.kx_in[l].ap(), [self.b_x["kx"][l]], BF16)
            self.dump("vxin", self.vx_in[l].ap(), [self.b_x["vx"][l]], BF16)
        chk("diffproj")
        self.stage_mlaproj(l)
        if l == 0:
            self.dump("qmt", self.QMT[0:96], self.bQMT.all(), BF16)
            self.dump("kmtp", self.KMTP[0:96], self.bKMTP.all(), BF16)
            self.dump("vmp", self.VMP, self.bVMP.all(), BF16)
            self.dump("cqn", self.CQN, self.bCQN.all(), BF16)
            self.dump("kmxin", self.kmx_in[l].ap(), [self.b_x["kmx"][l]], BF16)
            self.dump("vmxin", self.vmx_in[l].ap(), [self.b_x["vmx"][l]], BF16)
            self.dump("kmtc", self.KMTC[0:96], self.bKMTC.all(), BF16)
            self.dump("vmc", self.VMC, self.bVMC.all(), BF16)
        chk("mlaproj")
        self.FENCE(mla_t, obr)
        self.stage_attn_P(l); chk("attnP")
        self.FENCE(ponly, self.bKS + self.bVS)
        self.stage_attn_S(l)
        if l == 0:
            self.dump("odt", self.ODT, self.bODT.all(), BF16)
            self.dump("omt", self.OMT, self.bOMT.all(), BF16)
        chk("attnS")
        self.FENCE(life1, life2)
        self.stage_conv(l)
        if l == 0:
            self.dump("st", self.ST, self.bST.all(), BF16)
        chk("conv")
        self.stage_merge(l)
        if l == 0:
            self.dump("mt", self.MT, self.bMT.all(), BF16)
            self.dump("xmid", self.X[:], self.bX.all())
        chk("merge")
        self.FENCE(wo3, self.bAT.all())
        self.stage_norm(l, 1)
        self.stage_mlp(l); chk("mlp")
        if l + 1 < DEPTH:
            self.FENCE(life2 + self.bAT.all() + obr, life1 + mla_t)

    C_MODB = 0
    C_N1G = 96
    C_N2G = 112
    C_DW = 128
    C_CB = 252
    C_CNG = 256
    C_QG = 260
    C_KG = 261
    C_SLG = 262
    C_CQG = 263
    C_KVG = 266
    C_QMG = 268
    C_KMG = 269
    C_LAM = 270

    def mod_block(self, l, j, pool="s"):
        pm, bpm = self.psn(pool)
        slot, bs = self.wnext(f"mod{l}_{j}")
        for mi in range(4):
            for k in range(8):
                self.MM(pm[:, 2 * mi:2 * mi + 2], slot[:, k, mi * 128:(mi + 1) * 128], self.CONDB[:, k, :],
                        k == 0, k == 7, [bs, self.bCONDB], [bpm])
        self.TT(self.MODC_[:, l, 4 * j:4 * j + 4, :].rearrange("p m c -> p (m c)"), pm[:, 0:8],
                self.COLS[:, l, self.C_MODB + 8 * j:self.C_MODB + 8 * j + 8], ALU.add, [bpm, self.bCOLS], [self.bMC[l, j]])
        for i, (sc0, gcol, last) in enumerate(((8, self.C_N1G, 3), (32, self.C_N2G, 9))):
            if j == last:
                self.STT(self.MODA_[:, l, i].rearrange("p k c -> p (k c)"),
                         self.MODC_[:, l, sc0:sc0 + 8, :].rearrange("p k c -> p (k c)"), 1.0,
                         self.COLS[:, l, gcol:gcol + 16], ALU.add, ALU.mult, [self.bMC[l, j - 1], self.bMC[l, j], self.bCOLS], [self.bMA[l, i]])

    def stage_lam(self, l):
        LAM = self.LAM_[:, l, :]
        bL = self.bLAMg[l]
        lam_init = 0.8 - 0.6 * math.exp(-0.3 * l)
        lp = self.COLS[:, l, self.C_LAM:self.C_LAM + 256].rearrange("p (r d) -> p r d", r=4)
        self.TT(self.LTMP[:, 0, :], lp[:, 0, :], lp[:, 1, :], ALU.mult, [self.bCOLS], [self.bLT])
        self.TT(self.LTMP[:, 1, :], lp[:, 2, :], lp[:, 3, :], ALU.mult, [self.bCOLS, self.bLT], [self.bLT])
        self.S.op("dve", lambda e: e.tensor_reduce(out=LAM[:, 0:2], in_=self.LTMP[:], axis=mybir.AxisListType.X, op=ALU.add),
                  [self.bLT], [bL])
        self.ACT(LAM[:, 2:4], LAM[:, 0:2], AF.Exp, [bL], [bL])
        self.TT(LAM[:, 4:5], LAM[:, 3:4], LAM[:, 2:3], ALU.subtract, [bL], [bL])
        self.TS(LAM[:, 5:6], LAM[:, 4:5], -lam_init, None, ALU.add, None, [bL], [bL])
        self.TS(LAM[:, 6:7], self.col(l, self.C_SLG), 1.0 - lam_init, None, ALU.mult, None, [bL, self.bCOLS], [bL])

    def modcol(self, l, part, k, h):
        return self.MODC_[:, l, part * 8 + k, h:h + 1]

    def bmod(self, l, part):
        return [self.bMC[l, 2 * part], self.bMC[l, 2 * part + 1]]

    def rstd_from(self, stats_ps, bps, npart=128, n=HALF, dst=None):
        r, br = dst if dst is not None else self.tf()
        self.ACT(r[0:npart, 0:n], stats_ps[0:npart, 0:n], AF.Ln, [bps], [br], bias=EPS)
        self.ACT(r[0:npart, 0:n], r[0:npart, 0:n], AF.Exp, [br], [br], scale=-0.5)
        return r, br

    def ident(self):
        return self.CB[:, 0, :]

    def ones(self):
        return self.CB[:, 1, :]

    def stage_norm(self, l, which):
        part_sh, part_sc = (0, 1) if which == 0 else (3, 4)
        for h in range(2):
            ts = slice(h * HALF, (h + 1) * HALF)
            pst, bpst = self.psn("s")
            for k in range(8):
                sq, bsq = self.tb()
                self.ACT(sq[:], self.X[:, k, ts], AF.Square, [self.bX[k, h]], [bsq], scale=1.0 / 32.0)
                self.MM(pst[:], self.ones(), sq[:], k == 0, k == 7, [self.bCB, bsq], [bpst])
            r, br = self.rstd_from(pst, bpst, dst=(self.RS[h], self.b_RS[h]))
            for k in range(8):
                t, bt = self.tf()
                self.STT(t[:], self.X[:, k, ts], self.MODA_[:, l, which, k, h:h + 1], r[:], ALU.mult, ALU.mult,
                         [self.bX[k, h], self.bMA[l, which], br], [bt])
                self.ACT(self.HT[:, k, ts], t[:], AF.Identity, [bt] + self.bmod(l, part_sh), [self.bHT[k, h]],
                         bias=self.modcol(l, part_sh, k, h))

    def proj(self, slot, bs, c0, m, h, pool="d", nk=8, src=None, bsrc=None):
        ps, bps = self.psn(pool)
        ts = slice(h * HALF, (h + 1) * HALF)
        for k in range(nk):
            if src is None:
                rhs, br = self.HT[:, k, ts], self.bHT[k, h]
            else:
                rhs, br = src[:, k, ts], bsrc[k, h]
            self.MM(ps[0:m, :], slot[:, k, c0:c0 + m], rhs, k == 0, k == nk - 1, [bs, br], [bps])
        return ps, bps

    def stage_convproj(self, l):
        sa, bsa = self.wnext(f"ua{l}")
        sg, bsg = self.wnext(f"ug{l}", hold=1)
        for j in range(4):
            for h in range(2):
                pa, bpa = self.proj(sa, bsa, j * 128, 128, h)
                pg, bpg = self.proj(sg, bsg, j * 128, 128, h)
                sgm, bsgm = self.tf()
                self.ACT(sgm[:], pg[:], AF.Sigmoid, [bpg], [bsgm])
                if h == 0:
                    for s in range(2):
                        self.TT(self.UP[:, j, s, 15:271], pa[:, s * 256:(s + 1) * 256], sgm[:, s * 256:(s + 1) * 256], ALU.mult,
                                [bpa, bsgm], [self.bUP[j]])
                else:
                    self.TT(self.US[:, j, 15:527], pa[:], sgm[:], ALU.mult, [bpa, bsgm], [self.bUS[j]])
        hb = self.hb_in[l].ap().rearrange("(j p) c -> p j c", p=128)
        o1 = self.DMA(hb[:, :, 0:16], self.US[:, :, 15:31], [self.bUS[j] for j in range(4)], [self.b_x["hb"][l]])
        o2 = self.DMA(hb[:, :, 16:32], self.US[:, :, 511:527], [self.bUS[j] for j in range(4)], [self.b_x["hb"][l]])
        self.allgather("hb", l)

    def allgather(self, name, l):
        src = getattr(self, name + "_in")[l]
        dst = getattr(self, name + "_out")[l]
        self.S.op("pool", lambda e: e.collective_compute("AllGather", ALU.bypass, replica_groups=[[0, 1, 2, 3], [4, 5, 6, 7]],
                                                         ins=[src.ap().opt()], outs=[dst.ap().opt()]),
                  [self.b_x[name][l]], [self.b_xo[name][l]], kind="x")

    def blocknorm(self, ps, bps, npart, ones_ap, inv_sqrt_n, gcol, bgs, n=HALF):
        sq, bsq = self.tb()
        self.ACT(sq[0:npart, 0:n], ps[0:npart, 0:n], AF.Square, [bps], [bsq], scale=inv_sqrt_n)
        pst, bpst = self.psn("s")
        self.MM(pst[0:npart, 0:n], ones_ap, sq[0:npart, 0:n], True, True, [self.bCB, bsq], [bpst])
        r, br = self.rstd_from(pst, bpst, npart, n)
        t, bt = self.tf()
        self.STT(t[0:npart, 0:n], ps[0:npart, 0:n], gcol, r[0:npart, 0:n], ALU.mult, ALU.mult, [bps, br] + bgs, [bt])
        return t, bt

    def rope_from(self, tb_, btb, npart, perm_ap, ctab, stab, out_ap, bout):
        psw, bpsw = self.psn("d")
        self.MM(psw[0:npart, :], perm_ap, tb_[0:npart, :], True, True, [self.bCB, btb], [bpsw])
        t1, bt1 = self.tf()
        self.TT(t1[0:npart, :], tb_[0:npart, :], ctab, ALU.mult, [btb, self.bROPE], [bt1])
        t2, bt2 = self.tf()
        self.TT(t2[0:npart, :], psw[0:npart, :], stab, ALU.mult, [bpsw, self.bROPE], [bt2])
        self.TT(out_ap, t1[0:npart, :], t2[0:npart, :], ALU.add, [bt1, bt2], bout)

    def bn_run(self, units):
        def A(u):
            ps, bps = u["proj"]()
            npart, n = u["npart"], u.get("n", HALF)
            sq, bsq = self.tb()
            self.ACT(sq[0:npart, 0:n], ps[0:npart, 0:n], AF.Square, [bps], [bsq], scale=u["isn"])
            u["c"] = (ps, bps, sq, bsq)

        def B(u):
            ps, bps, sq, bsq = u["c"]
            npart, n = u["npart"], u.get("n", HALF)
            pst, bpst = self.psn("s")
            self.MM(pst[0:npart, 0:n], u["ones"], sq[0:npart, 0:n], True, True, [self.bCB, bsq], [bpst])
            u["r"] = self.rstd_from(pst, bpst, npart, n)

        def C(u):
            ps, bps, sq, bsq = u["c"]
            r, br = u["r"]
            npart, n = u["npart"], u.get("n", HALF)

            def norm(dst, bdst):
                self.STT(dst, ps[0:npart, 0:n], u["gcol"], r[0:npart, 0:n], ALU.mult, ALU.mult, [bps, br] + u["bgs"], bdst)
            u["post"](norm)
        nu = len(units)
        for i in range(nu + 2):
            if i < nu:
                A(units[i])
            if 0 <= i - 1 < nu:
                B(units[i - 1])
            if 0 <= i - 2 < nu:
                C(units[i - 2])

    def stage_diffproj(self, l):
        blk64 = self.CB[:, 2, :]
        Pd = self.CB[:, 3, :]
        sq_, bsq_ = self.wnext(f"dq{l}")
        sk, bsk = self.wnext(f"dk{l}", hold=1)
        kx = self.kx_in[l].ap().rearrange("(j p) t -> p j t", p=128)
        units = []
        for which in range(2):
            for j in range(4):
                for h in range(2):
                    def post(norm, which=which, j=j, h=h):
                        if which == 0:
                            if h == 0:
                                norm(self.QT[:, j, 0:HALF], [self.bQT[j, 0]])
                            else:
                                tb_, btb = self.tb()
                                norm(tb_[:], [btb])
                                self.rope_from(tb_, btb, 128, Pd, self.ROPE[:, 0, :], self.ROPE[:, 1, :], self.QT[:, j, HALF:T], [self.bQT[j, 1]])
                        else:
                            if h == 0:
                                t, bt = self.tf()
                                norm(t[:], [bt])
                                self.CP(self.KTP[:, j, :], t[:], [bt], [self.bKTP[j]])
                                self.outs.append(self.DMA(self.o_dk[l, j * 128:(j + 1) * 128, :], t[:], [bt], []))
                            else:
                                tb_, btb = self.tb()
                                norm(tb_[:], [btb])
                                st, bst = self.tb()
                                self.rope_from(tb_, btb, 128, Pd, self.ROPE[:, 0, :], self.ROPE[:, 1, :], st[:], [bst])
                                self.DMA(kx[:, j, :], st[:], [bst], [self.b_x["kx"][l]])
                    slot, bslot = (sq_, bsq_) if which == 0 else (sk, bsk)
                    units.append(dict(proj=(lambda slot=slot, bslot=bslot, j=j, h=h: self.proj(slot, bslot, j * 128, 128, h)),
                                      npart=128, ones=blk64, isn=0.125,
                                      gcol=self.col(l, self.C_QG if which == 0 else self.C_KG), bgs=[self.bCOLS], post=post))
        self.bn_run(units)
        self.chk("dq")
        self.chk("dk")
        sv, bsv = self.wnext(f"dv{l}")
        vx = self.vx_in[l].ap().rearrange("(n p) e -> p n e", p=128)
        for tt in range(8):
            ps, bps = self.psn("d")
            h = tt // 4
            for k in range(8):
                self.MM(ps[:], self.HT[:, k, tt * 128:(tt + 1) * 128], sv[:, k, :], k == 0, k == 7, [bsv, self.bHT[k, h]], [bps])
            if tt < 4:
                vf, bvf = self.tf()
                self.ACT(vf[:], ps[:], AF.Copy, [bps], [bvf])
                self.outs.append(self.DMA(self.o_dv[l, tt * 128:(tt + 1) * 128, :], vf[:], [bvf], []))
                self.CP(self.VP[:, tt, :], vf[:], [bvf], [self.bVP[tt]])
            else:
                st, bst = self.tb()
                self.ACT(st[:], ps[:], AF.Copy, [bps], [bst])
                self.DMA(vx[:, tt - 4, :], st[:], [bst], [self.b_x["vx"][l]])
        self.chk("dv")
        self.allgather("kx", l)
        self.allgather("vx", l)
        self.chk("dag")
        self.DMA(self.KC, self.ckT[l].rearrange("(j p) t -> p j t", p=128), (), [self.bKC], q="pool")
        self.DMA(self.VC, self.cv[l].rearrange("(n p) e -> p n e", p=128), (), [self.bVC], q="pool")

    def stage_mlaproj(self, l):
        ones = self.ones()
        Pm = self.CB[0:96, 4, 0:96]
        EXT = self.CB[0:32, 5, 0:96]
        wk_src = self.w_ukv[l].rearrange("(k p) (h c) -> p k h c", p=128, h=8)
        for k in range(2):
            self.DMA(self.WK[:, k, :, 0:64], wk_src[:, k, :, 0:64], (), [self.bWK], q="pool")
            self.DMA(self.WV[:, k, :].rearrange("p (h c) -> p h c", h=8), wk_src[:, k, :, 64:128], (), [self.bWV], q="pool")
        self.DMA(self.CC, self.cckvT[l].rearrange("(k p) t -> p k t", p=128), (), [self.bCC], q="pool")
        self.DMA(self.CKPE[0:32, :], self.ckpeT[l], (), [self.bCKPE], q="pool")
        scq, bscq = self.wnext(f"cq{l}")
        for h in range(2):
            ts = slice(h * HALF, (h + 1) * HALF)
            raws = []
            pst, bpst = self.psn("s")
            for m in range(3):
                ps, bps = self.psn("a")
                for k in range(8):
                    self.MM(ps[:], scq[:, k, m * 128:(m + 1) * 128], self.HT[:, k, ts], k == 0, k == 7, [bscq, self.bHT[k, h]], [bps])
                sq, bsq = self.tb()
                self.ACT(sq[:], ps[:], AF.Square, [bps], [bsq], scale=384.0 ** -0.5)
                self.MM(pst[:], ones, sq[:], m == 0, m == 2, [self.bCB, bsq], [bpst])
                raws.append((ps, bps))
            r, br = self.rstd_from(pst, bpst)
            for m in range(3):
                ps, bps = raws[m]
                self.STT(self.CQN[:, m, ts], ps[:], self.col(l, self.C_CQG + m), r[:], ALU.mult, ALU.mult,
                         [bps, br, self.bCOLS], [self.bCQN[m, h]])
        sc, bsc = self.wnext(f"ckv{l}")
        for h in range(2):
            ts = slice(h * HALF, (h + 1) * HALF)
            raws = []
            pst, bpst = self.psn("s")
            for m in range(2):
                ps, bps = self.psn("a")
                for k in range(8):
                    self.MM(ps[:], sc[:, k, m * 128:(m + 1) * 128], self.HT[:, k, ts], k == 0, k == 7, [bsc, self.bHT[k, h]], [bps])
                sq, bsq = self.tb()
                self.ACT(sq[:], ps[:], AF.Square, [bps], [bsq], scale=1.0 / 16.0)
                self.MM(pst[:], ones, sq[:], m == 0, m == 1, [self.bCB, bsq], [bpst])
                raws.append((ps, bps))
            r, br = self.rstd_from(pst, bpst)
            for m in range(2):
                ps, bps = raws[m]
                t, bt = self.tf()
                self.STT(t[:], ps[:], self.col(l, self.C_KVG + m), r[:], ALU.mult, ALU.mult, [bps, br, self.bCOLS], [bt])
                self.ACT(self.CKVN[:, m, ts], t[:], AF.Copy, [bt], [self.bCKVN[m, h]])
                if h == 0:
                    self.outs.append(self.DMA(self.o_ckv[l, m * 128:(m + 1) * 128, :], t[:], [bt], []))
            ps, bps = self.psn("a")
            for k in range(8):
                self.MM(ps[0:32, :], sc[:, k, 256:288], self.HT[:, k, ts], k == 0, k == 7, [bsc, self.bHT[k, h]], [bps])
            t, bt = self.tf()
            self.ACT(t[0:32, :], ps[0:32, :], AF.Copy, [bps], [bt])
            self.CP(self.KPE[0:32, ts], t[0:32, :], [bt], [self.bKPE[h]])
            if h == 0:
                self.outs.append(self.DMA(self.o_kpe[l], t[0:32, :], [bt], []))
        suq, bsuq = self.wnext(f"uq{l}")
        kmx = self.kmx_in[l].ap().rearrange("(h p) t -> p h t", p=96)
        kmg = self.COLS[0:96, l, self.C_KMG:self.C_KMG + 1]
        qmg = self.COLS[0:96, l, self.C_QMG:self.C_QMG + 1]
        ones96 = ones[0:96, 0:96]
        units = []
        for hd in range(8):
            for h in range(2):
                ts = slice(h * HALF, (h + 1) * HALF)

                def projq(hd=hd, h=h, ts=ts):
                    ps, bps = self.psn("d")
                    for k in range(3):
                        self.MM(ps[0:96, :], suq[:, k, hd * 96:(hd + 1) * 96], self.CQN[:, k, ts], k == 0, k == 2,
                                [bsuq, self.bCQN[k, h]], [bps])
                    return ps, bps

                def postq(norm, hd=hd, h=h, ts=ts):
                    if h == 0:
                        norm(self.QMT[0:96, hd, ts], [self.bQMT[hd, 0]])
                    else:
                        tb_, btb = self.tb()
                        norm(tb_[0:96, :], [btb])
                        self.rope_from(tb_, btb, 96, Pm, self.ROPE[0:96, 2, :], self.ROPE[0:96, 3, :], self.QMT[0:96, hd, ts], [self.bQMT[hd, 1]])
                units.append(dict(proj=projq, npart=96, ones=ones96, isn=96.0 ** -0.5, gcol=qmg, bgs=[self.bCOLS], post=postq))
        for hd in range(8):
            for seg in range(3):
                n = HALF if seg < 2 else 256

                def projk(hd=hd, seg=seg, n=n):
                    ps, bps = self.psn("d")
                    for k in range(2):
                        if seg < 2:
                            rhs, br = self.CKVN[:, k, seg * HALF:(seg + 1) * HALF], self.bCKVN[k, seg]
                        else:
                            rhs, br = self.CC[:, k, :], self.bCC
                        self.MM(ps[0:96, 0:n], self.WK[:, k, hd, :], rhs, k == 0, False, [self.bWK, br], [bps])
                    if seg < 2:
                        rhs, br = self.KPE[0:32, seg * HALF:(seg + 1) * HALF], self.bKPE[seg]
                    else:
                        rhs, br = self.CKPE[0:32, :], self.bCKPE
                    self.MM(ps[0:96, 0:n], EXT, rhs, False, True, [self.bCB, br], [bps])
                    return ps, bps

                def postk(norm, hd=hd, seg=seg):
                    if seg == 0:
                        norm(self.KMTP[0:96, hd, :], [self.bKMTP[hd]])
                    elif seg == 1:
                        tb_, btb = self.tb()
                        norm(tb_[0:96, :], [btb])
                        st, bst = self.tb()
                        self.rope_from(tb_, btb, 96, Pm, self.ROPE[0:96, 2, :], self.ROPE[0:96, 3, :], st[0:96, :], [bst])
                        self.DMA(kmx[:, hd, :], st[0:96, :], [bst], [self.b_x["kmx"][l]])
                    else:
                        norm(self.KMTC[0:96, hd, :], [self.bKMTC[hd]])
                units.append(dict(proj=projk, npart=96, ones=ones96, isn=96.0 ** -0.5, gcol=kmg, bgs=[self.bCOLS], n=n, post=postk))
        self.bn_run(units)
        vmx = self.vmx_in[l].ap().rearrange("(n p) e -> p n e", p=128)
        for tt in range(10):
            ps, bps = self.psn("d")
            for k in range(2):
                if tt < 8:
                    lhsT, br = self.CKVN[:, k, tt * 128:(tt + 1) * 128], self.bCKVN[k, tt // 4]
                else:
                    lhsT, br = self.CC[:, k, (tt - 8) * 128:(tt - 7) * 128], self.bCC
                self.MM(ps[:], lhsT, self.WV[:, k, :], k == 0, k == 1, [self.bWV, br], [bps])
            if tt < 4:
                self.ACT(self.VMP[:, tt, :], ps[:], AF.Copy, [bps], [self.bVMP[tt]])
            elif tt < 8:
                st, bst = self.tb()
                self.ACT(st[:], ps[:], AF.Copy, [bps], [bst])
                self.DMA(vmx[:, tt - 4, :], st[:], [bst], [self.b_x["vmx"][l]])
            else:
                self.ACT(self.VMC[:, tt - 8, :], ps[:], AF.Copy, [bps], [self.bVMC[tt - 8]])
        self.allgather("kmx", l)
        self.allgather("vmx", l)

    def attn_unit(self, qap, bq, keytiles, scale, e, po, bpo, pz, bpz, nq):
        nt = len(keytiles)
        for i, (kT, bk, v, bv) in enumerate(keytiles):
            psc, bpsc = self.psn("a")
            self.MM(psc[:, 0:nq], kT, qap, True, True, [bk, bq], [bpsc])
            pt, bpt = self.tb()
            self.ACT(pt[:, 0:nq], psc[:, 0:nq], AF.Exp, [bpsc], [bpt], scale=scale)
            self.MM(po, v, pt[:, 0:nq], i == 0, i == nt - 1, [bv, bpt], [bpo])
            self.MM(pz, self.CB[:, 1, 0:e], pt[:, 0:nq], i == 0, i == nt - 1, [self.bCB, bpt], [bpz])

    def diff_fin(self, l, j, h, q0, nq):
        P = self.PS
        b = self.b_PS
        DA, bDA = self.DA, self.b_DA
        r0, br0 = self.tf()
        self.ACT(r0[:, 0:nq], P[5][:, 0:nq], AF.Ln, [b[5]], [br0])
        self.ACT(r0[:, 0:nq], r0[:, 0:nq], AF.Exp, [br0], [br0], scale=-1.0)
        self.TT(DA[0][:, 0:nq], P[4][:, 0:nq], r0[:, 0:nq], ALU.mult, [b[4], br0], [bDA[0]])
        r1, br1 = self.tf()
        self.ACT(r1[:, 0:nq], P[7][:, 0:nq], AF.Ln, [b[7]], [br1])
        self.ACT(r1[:, 0:nq], r1[:, 0:nq], AF.Exp, [br1], [br1], scale=-1.0)
        self.TT(DA[1][:, 0:nq], P[6][:, 0:nq], r1[:, 0:nq], ALU.mult, [b[6], br1], [bDA[1]])

        def tail():
            o = DA[0]
            self.STT(o[:, 0:nq], DA[1][:, 0:nq], self.LAM_[:, l, 5:6], DA[0][:, 0:nq], ALU.mult, ALU.add,
                     [bDA[1], bDA[0], self.bLAMg[l]], [bDA[0]])
            sq, bsq = self.tb()
            self.ACT(sq[:, 0:nq], o[:, 0:nq], AF.Square, [bDA[0]], [bsq], scale=128.0 ** -0.5)
            pst, bpst = self.psn("a")
            self.MM(pst[:, 0:nq], self.ones(), sq[:, 0:nq], True, True, [self.bCB, bsq], [bpst])
            r, br = self.rstd_from(pst, bpst, 128, nq)
            self.STT(self.ODT[:, j, q0:q0 + nq], o[:, 0:nq], self.LAM_[:, l, 6:7], r[:, 0:nq], ALU.mult, ALU.mult,
                     [bDA[0], br, self.bLAMg[l]], [self.bODT[j, h]])
        return tail

    def mla_fin(self, hd, h, q0, nq):
        def tail():
            P = self.PS
            b = self.b_PS
            ba = 4 + 2 * (hd % 2)
            pr = slice((hd % 2) * 64, (hd % 2) * 64 + 64)
            r0, br0 = self.tf()
            self.ACT(r0[pr, 0:nq], P[ba + 1][pr, 0:nq], AF.Ln, [b[ba + 1]], [br0])
            self.ACT(r0[pr, 0:nq], r0[pr, 0:nq], AF.Exp, [br0], [br0], scale=-1.0)
            self.TT(self.OMT[pr, hd // 2, q0:q0 + nq], P[ba][pr, 0:nq], r0[pr, 0:nq], ALU.mult, [b[ba], br0], [self.bOMT[hd, h]])
        return tail

    def attn_stream(self, units, defer, bg=None):
        pending = []
        for u in units:
            tiles = u["tiles"]
            nt = len(tiles)
            nq, e = u["nq"], u["e"]
            if u.get("pre") is not None:
                u["pre"]()
            wbanks = set()
            for L in (u.get("lanes") or [u]):
                wbanks.add(L["bpo"])
                wbanks.add(L["bpz"])
            for p in list(pending):
                if any(bb in wbanks for bb in p[2]):
                    pending.remove(p)
                    p[1]()
            lanes = u.get("lanes") or [u]
            for i in range(nt):
                scs = []
                for L in lanes:
                    kT, bk, v, bv = L["tiles"][i]
                    psc, bpsc = self.psn("a")
                    self.MM(psc[:, 0:nq], kT, L["q"], True, True, [bk, L["bq"]], [bpsc])
                    scs.append((psc, bpsc))
                pts = []
                for L, (psc, bpsc) in zip(lanes, scs):
                    pt, bpt = self.tb()
                    self.ACT(pt[:, 0:nq], psc[:, 0:nq], AF.Exp, [bpsc], [bpt], scale=u["scale"])
                    pts.append((pt, bpt))
                for L, (pt, bpt) in zip(lanes, pts):
                    kT, bk, v, bv = L["tiles"][i]
                    self.MM(L["po"], v, pt[:, 0:nq], i == 0, i == nt - 1, [bv, bpt], [L["bpo"]])
                    self.MM(L["pz"], self.CB[:, 1, 0:e], pt[:, 0:nq], i == 0, i == nt - 1, [self.bCB, bpt], [L["bpz"]])
                for p in pending:
                    p[0] -= 1
                while pending and pending[0][0] <= 0:
                    pending.pop(0)[1]()
                if bg:
                    bg.pop(0)()
            if u.get("fin") is not None:
                t = u["fin"]()
                if t is not None:
                    pending.append([defer, t, u.get("fin_banks", ())])
        for p in pending:
            p[1]()
        while bg:
            bg.pop(0)()

    def stage_attn_P(self, l):
        P, b = self.PS, self.b_PS
        units = []
        for s in range(2):
            q0 = s * 256
            for j in range(4):
                for c in range(2):
                    pr = slice(c * 64, (c + 1) * 64)
                    kts = [(self.KTP[pr, j, q0 + i * 128:q0 + (i + 1) * 128], self.bKTP[j],
                            self.VP[:, 2 * s + i, j * 128:(j + 1) * 128], self.bVP[2 * s + i]) for i in range(2)]
                    units.append(dict(tiles=kts, q=self.QT[pr, j, q0:q0 + 256], bq=self.bQT[j, 0], scale=0.125, e=128,
                                      po=P[4 + 2 * c][:, 0:256], bpo=b[4 + 2 * c], pz=P[5 + 2 * c][:, 0:256], bpz=b[5 + 2 * c], nq=256,
                                      fin=(lambda j=j, q0=q0: self.diff_fin(l, j, 0, q0, 256)) if c == 1 else None))
            for hd in range(8):
                kts = [(self.KMTP[0:96, hd, q0 + i * 128:q0 + (i + 1) * 128], self.bKMTP[hd],
                        self.VMP[:, 2 * s + i, hd * 64:(hd + 1) * 64], self.bVMP[2 * s + i]) for i in range(2)]
                pr = slice((hd % 2) * 64, (hd % 2) * 64 + 64)
                ba = 4 + 2 * (hd % 2)
                units.append(dict(tiles=kts, q=self.QMT[0:96, hd, q0:q0 + 256], bq=self.bQMT[hd, 0], scale=96.0 ** -0.5, e=64,
                                  po=P[ba][pr, 0:256], bpo=b[ba], pz=P[ba + 1][pr, 0:256], bpz=b[ba + 1], nq=256,
                                  fin=(lambda hd=hd, q0=q0: self.mla_fin(hd, 0, q0, 256)), fin_banks=(b[ba], b[ba + 1])))
        self.attn_stream(units, 3)

    def stage_attn_S(self, l):
        P, b = self.PS, self.b_PS
        q0 = HALF
        units = []
        slot = [0]

        import os
        PAIR = os.environ.get("KPAIR", "0") == "1"
        for j in range(4):
            s = slot[0] % 2
            slot[0] += 1
            lanes = []
            for c in range(2):
                pr = slice(c * 64, (c + 1) * 64)
                kts = [(self.KC[pr, j, i * 128:(i + 1) * 128], self.bKC, self.VC[:, i, j * 128:(j + 1) * 128], self.bVC) for i in range(2)]
                kts += [(self.KS[s][pr, i * 128:(i + 1) * 128], self.bKS[s], self.VS[s][:, i, :], self.bVS[s]) for i in range(16)]
                lanes.append(dict(tiles=kts, q=self.QT[pr, j, q0:q0 + HALF], bq=self.bQT[j, 1], scale=0.125, e=128,
                                  po=P[4 + 2 * c][:, :], bpo=b[4 + 2 * c], pz=P[5 + 2 * c][:, :], bpz=b[5 + 2 * c], nq=HALF))
            if PAIR:
                u = dict(lanes[0])
                u["lanes"] = lanes
                u["fin"] = (lambda j=j: self.diff_fin(l, j, 1, q0, HALF))
                u["load"] = ("d", j, s)
                units.append(u)
            else:
                lanes[0]["load"] = ("d", j, s)
                lanes[0]["fin"] = None
                lanes[1]["fin"] = (lambda j=j: self.diff_fin(l, j, 1, q0, HALF))
                units += lanes
        for hd in range(8):
            s = slot[0] % 2
            slot[0] += 1
            pr = slice((hd % 2) * 64, (hd % 2) * 64 + 64)
            ba = 4 + 2 * (hd % 2)
            kts = [(self.KMTC[0:96, hd, i * 128:(i + 1) * 128], self.bKMTC[hd], self.VMC[:, i, hd * 64:(hd + 1) * 64], self.bVMC[i]) for i in range(2)]
            kts += [(self.KS[s][0:96, i * 128:(i + 1) * 128], self.bKS[s], self.VS[s][:, i, 0:64], self.bVS[s]) for i in range(16)]
            units.append(dict(tiles=kts, q=self.QMT[0:96, hd, q0:q0 + HALF], bq=self.bQMT[hd, 1], scale=96.0 ** -0.5, e=64,
                              po=P[ba][pr, :], bpo=b[ba], pz=P[ba + 1][pr, :], bpz=b[ba + 1], nq=HALF,
                              fin=(lambda hd=hd: self.mla_fin(hd, 1, q0, HALF)), fin_banks=(b[ba], b[ba + 1]), load=("m", hd, s)))
        kxo = self.kx_out[l].ap().rearrange("(r f) t -> f r t", r=4)
        vxo = self.vx_out[l].ap().rearrange("(n p) e -> p n e", p=128)
        kmo = self.kmx_out[l].ap().rearrange("(r f) t -> f r t", r=4)
        vmo = self.vmx_out[l].ap().rearrange("(n p) e -> p n e", p=128)

        def do_load(ld):
            kind, idx, s = ld
            if kind == "d":
                self.DMA(self.KS[s].rearrange("p (r t) -> p r t", r=4), kxo[idx * 128:(idx + 1) * 128], [self.b_xo["kx"][l]], [self.bKS[s]])
                self.DMA(self.VS[s], vxo[:, :, idx * 128:(idx + 1) * 128], [self.b_xo["vx"][l]], [self.bVS[s]])
            else:
                self.DMA(self.KS[s][0:96, :].rearrange("p (r t) -> p r t", r=4), kmo[idx * 96:(idx + 1) * 96], [self.b_xo["kmx"][l]], [self.bKS[s]])
                self.DMA(self.VS[s][:, :, 0:64], vmo[:, :, idx * 64:(idx + 1) * 64], [self.b_xo["vmx"][l]], [self.bVS[s]])
        for u in units:
            if u.get("load") is not None:
                u["pre"] = (lambda ld=u["load"]: do_load(ld))
        bg = self.conv_bg(l)
        step = len(bg) // 9
        for n_, j in enumerate(range(4, 12)):
            bg.insert((n_ + 1) * step + n_, (lambda j=j: self.mod_block(l, j, pool="a")))
        self.attn_stream(units, 4, bg=bg)

    def conv_bg(self, l):
        ops = []
        hbo = self.hb_out[l].ap().rearrange("(r j p) c -> p j r c", r=4, p=128)

        def halo():
            for j in range(4):
                self.DMA(self.HL[:, j], hbo[:, j], [self.b_xo["hb"][l]], [self.bHL])
            for side, c0, dst0 in ((0, 17, 0), (1, 0, 527)):
                for r in range(4):
                    if r == 0:
                        self.TS(self.HLF[:, :, side, 0:15], self.HL[:, :, r, c0:c0 + 15], self.SEL[:, side * 4 + r:side * 4 + r + 1], None,
                                ALU.mult, None, [self.bHL, self.bSEL], [self.bHLF])
                    else:
                        self.STT(self.HLF[:, :, side, 0:15], self.HL[:, :, r, c0:c0 + 15], self.SEL[:, side * 4 + r:side * 4 + r + 1],
                                 self.HLF[:, :, side, 0:15], ALU.mult, ALU.add, [self.bHL, self.bSEL, self.bHLF], [self.bHLF])
                self.CP(self.US[:, :, dst0:dst0 + 15], self.HLF[:, :, side, 0:15], [self.bHLF], [self.bUS[j] for j in range(4)])
        ops.append(halo)
        accP = self.CACC[:, 0:HALF].rearrange("p (s t) -> p s t", s=2)
        accS = self.CACC[:, HALF:T]
        for j in range(4):
            for kk in range(31):
                dwc = self.col(l, self.C_DW + j * 31 + kk)

                def tapP(j=j, kk=kk, dwc=dwc):
                    src = self.UP[:, j, :, kk:kk + 256]
                    if kk == 0:
                        self.TS(accP, src, dwc, None, ALU.mult, None, [self.bUP[j], self.bCOLS], [self.bACC[0]])
                    else:
                        self.STT(accP, src, dwc, accP, ALU.mult, ALU.add, [self.bUP[j], self.bCOLS, self.bACC[0]], [self.bACC[0]])

                def tapS(j=j, kk=kk, dwc=dwc):
                    src = self.US[:, j, kk:kk + HALF]
                    if kk == 0:
                        self.TS(accS, src, dwc, None, ALU.mult, None, [self.bUS[j], self.bCOLS], [self.bACC[1]])
                    else:
                        self.STT(accS, src, dwc, accS, ALU.mult, ALU.add, [self.bUS[j], self.bCOLS, self.bACC[1]], [self.bACC[1]])
                ops.append(tapP)
                ops.append(tapS)

            def fin(j=j):
                cb = self.col(l, self.C_CB + j)
                self.TS(self.UP[:, j, :, 15:271], accP, cb, None, ALU.add, None, [self.bACC[0], self.bCOLS], [self.bUP[j]])
                self.TS(self.US[:, j, 15:527], accS, cb, None, ALU.add, None, [self.bACC[1], self.bCOLS], [self.bUS[j]])
            ops.append(fin)
        return ops

    def stage_conv(self, l):
        P, b = self.PS, self.b_PS
        stats = [(P[6], b[6]), (P[7], b[7])]

        def ysrc(j, h):
            if h == 0:
                return self.UP[:, j, :, 15:271], self.bUP[j]
            return self.US[:, j, 15:527], self.bUS[j]
        for j in range(4):
            for h in range(2):
                y, by = ysrc(j, h)
                sq, bsq = self.tb()
                sqv = sq[:].rearrange("p (s t) -> p s t", s=2) if h == 0 else sq[:]
                self.ACT(sqv, y, AF.Square, [by], [bsq], scale=512.0 ** -0.5)
                pst, bpst = stats[h]
                self.MM(pst[:], self.ones(), sq[:], j == 0, j == 3, [self.bCB, bsq], [bpst])
        for h in range(2):
            ts = slice(h * HALF, (h + 1) * HALF)
            r, br = self.rstd_from(stats[h][0], stats[h][1])
            for j in range(4):
                y, by = ysrc(j, h)
                z, bz = self.tf()
                zv = z[:].rearrange("p (s t) -> p s t", s=2) if h == 0 else z[:]
                rv = r[:].rearrange("p (s t) -> p s t", s=2) if h == 0 else r[:]
                self.STT(zv, y, self.col(l, self.C_CNG + j), rv, ALU.mult, ALU.mult, [by, self.bCOLS, br], [bz])
                self.ACT(self.ST[:, j, ts], z[:], AF.Silu, [bz], [self.bST[j, h]])

    def stage_merge(self, l):
        self.DMA(self.WCO, self.w_conv_out[l].rearrange("(k p) c -> p k c", p=128), (), [self.bWCO], q="pool")
        self.DMA(self.WDO, self.w_diff_out[l].rearrange("(k p) c -> p k c", p=128), (), [self.bWDO], q="pool")
        self.DMA(self.WMO, self.w_mla_out[l].rearrange("(k p) c -> p k c", p=128), (), [self.bWMO], q="pool")
        branches = ((self.WCO, self.bWCO, self.ST, self.bST), (self.WDO, self.bWDO, self.ODT, self.bODT),
                    (self.WMO, self.bWMO, self.OMT, self.bOMT))
        for j in range(8):
            sg, bsg = self.wnext(f"mg{l}_{j}")
            for h in range(2):
                ts = slice(h * HALF, (h + 1) * HALF)
                acc = None
                for bi, (wt, bwt, src, bsrc) in enumerate(branches):
                    pg, bpg = self.psn("d")
                    for k in range(8):
                        self.MM(pg[:], sg[:, k, bi, :], self.HT[:, k, ts], k == 0, k == 7, [bsg, self.bHT[k, h]], [bpg])
                    po, bpo = self.psn("d")
                    for k in range(4):
                        if bi == 2:
                            rb = [self.bOMT[2 * k, h], self.bOMT[2 * k + 1, h]]
                        else:
                            rb = [bsrc[k, h]]
                        self.MM(po[:], wt[:, k, j * 128:(j + 1) * 128], src[:, k, ts], k == 0, k == 3, [bwt] + rb, [bpo])
                    sgm, bsgm = self.tf()
                    self.ACT(sgm[:], pg[:], AF.Sigmoid, [bpg], [bsgm])
                    if bi == 0:
                        acc, bacc = self.tf()
                        self.TT(acc[:], po[:], sgm[:], ALU.mult, [bpo, bsgm], [bacc])
                    elif bi == 1:
                        t1, bt1 = self.tf()
                        self.TT(t1[:], po[:], sgm[:], ALU.mult, [bpo, bsgm], [bt1])
                        self.TT(acc[:], acc[:], t1[:], ALU.add, [bacc, bt1], [bacc])
                    else:
                        t1, bt1 = self.tf()
                        self.TT(t1[:], po[:], sgm[:], ALU.mult, [bpo, bsgm], [bt1])
                        self.TT(self.MT[:, j, ts], acc[:], t1[:], ALU.add, [bacc, bt1], [self.bMT[j, h]])
        for blk in range(2):
            so, bso = self.wnext(f"wo{l}_{blk}")
            for jj in range(4):
                j = blk * 4 + jj
                for h in range(2):
                    ts = slice(h * HALF, (h + 1) * HALF)
                    po, bpo = self.proj(so, bso, jj * 128, 128, h, src=self.MT, bsrc=self.bMT)
                    self.STT(self.X[:, j, ts], po[:], self.modcol(l, 2, j, h), self.X[:, j, ts], ALU.mult, ALU.add,
                             [bpo, self.bX[j, h]] + self.bmod(l, 2), [self.bX[j, h]])

    def stage_mlp(self, l):
        for q in range(4):
            for blk in range(2):
                su, bsu = self.wnext(f"up{l}_{q}_{blk}")
                for jj in range(4):
                    jq = blk * 4 + jj
                    for h in range(2):
                        ts = slice(h * HALF, (h + 1) * HALF)
                        pu, bpu = self.proj(su, bsu, jj * 128, 128, h)
                        t, bt = self.tf()
                        self.ACT(t[:], pu[:], AF.Relu, [bpu], [bt])
                        self.TT(self.AT[:, jq, ts], t[:], t[:], ALU.mult, [bt], [self.bAT[jq, h]])
            for blk in range(2):
                sd, bsd = self.wnext(f"dn{l}_{q}_{blk}")
                for jj in range(4):
                    j = blk * 4 + jj
                    for h in range(2):
                        ts = slice(h * HALF, (h + 1) * HALF)
                        pd, bpd = self.proj(sd, bsd, jj * 128, 128, h, src=self.AT, bsrc=self.bAT)
                        self.STT(self.X[:, j, ts], pd[:], self.modcol(l, 5, j, h), self.X[:, j, ts], ALU.mult, ALU.add,
                                 [bpd, self.bX[j, h]] + self.bmod(l, 5), [self.bX[j, h]])
            if l + 1 < DEPTH and q < 2:
                self.mod_block(l + 1, 2 * q)
                self.mod_block(l + 1, 2 * q + 1)

    def epilogue(self):
        fin = []
        for k in range(8):
            fin.append(self.DMA(self.yT[k * 128:(k + 1) * 128, :], self.X[:, k, :], [self.bX[k, 0], self.bX[k, 1]], []))
        return fin


def _rope_tables(rank):
    t = np.arange(rank * 512, rank * 512 + 512)
    pos = [(t // 64).astype(np.float64), (t % 64).astype(np.float64)]
    out = np.zeros((128, 4, 512), np.float64)
    inv = 10000.0 ** (-np.arange(0, 32, 2, dtype=np.float64) / 32.0)
    for c in range(2):
        for a in range(2):
            for p in range(2):
                for i in range(16):
                    d = c * 64 + a * 32 + p * 16 + i
                    ang = pos[a] * inv[i]
                    out[d, 0] = np.cos(ang)
                    out[d, 1] = np.sin(ang) * (-1.0 if p == 0 else 1.0)
    inv = 10000.0 ** (-np.arange(0, 16, 2, dtype=np.float64) / 16.0)
    out[0:64, 2] = 1.0
    for a in range(2):
        for p in range(2):
            for i in range(8):
                d = 64 + a * 16 + p * 8 + i
                ang = pos[a] * inv[i]
                out[d, 2] = np.cos(ang)
                out[d, 3] = np.sin(ang) * (-1.0 if p == 0 else 1.0)
    return out.astype(np.float32)


def _const_mats():
    m = np.zeros((128, 6, 128), np.float32)
    m[:, 0, :] = np.eye(128)
    m[:, 1, :] = 1.0
    m[0:64, 2, 0:64] = 1.0
    m[64:128, 2, 64:128] = 1.0
    for c in range(2):
        for a in range(2):
            for p in range(2):
                for i in range(16):
                    d = c * 64 + a * 32 + p * 16 + i
                    ds = c * 64 + a * 32 + (1 - p) * 16 + i
                    m[ds, 3, d] = 1.0
    for d in range(64):
        m[d, 4, d] = 1.0
    for a in range(2):
        for p in range(2):
            for i in range(8):
                d = 64 + a * 16 + p * 8 + i
                ds = 64 + a * 16 + (1 - p) * 8 + i
                m[ds, 4, d] = 1.0
    for i in range(32):
        m[i, 5, 64 + i] = 1.0
    return m


def _pack_cols(inp):
    cols = np.zeros((128, DEPTH, NCOLS), np.float32)

    def colform(v, nchunk):
        return np.ascontiguousarray(v.reshape(nchunk, 128).T)
    for l in range(DEPTH):
        cols[:, l, 0:96] = np.repeat(colform(inp["mod_b"][l], 48)[:, :, None], 2, axis=2).reshape(128, 96)
        cols[:, l, 96:112] = np.repeat(colform(inp["norm1_g"][l], 8)[:, :, None], 2, axis=2).reshape(128, 16)
        cols[:, l, 112:128] = np.repeat(colform(inp["norm2_g"][l], 8)[:, :, None], 2, axis=2).reshape(128, 16)
        dw = inp["conv_dw"][l]
        cols[:, l, 128:252] = dw.T.reshape(4, 128, 31).transpose(1, 0, 2).reshape(128, 124)
        cols[:, l, 252:256] = colform(inp["conv_b"][l], 4)
        cols[:, l, 256:260] = colform(inp["conv_norm_g"][l], 4)
        cols[:, l, 260] = np.tile(inp["diff_q_norm"][l], 2)
        cols[:, l, 261] = np.tile(inp["diff_k_norm"][l], 2)
        cols[:, l, 262] = inp["diff_subln"][l]
        cols[:, l, 263:266] = colform(inp["mla_q_a_norm"][l], 3)
        cols[:, l, 266:268] = colform(inp["mla_kv_a_norm"][l], 2)
        cols[0:96, l, 268] = inp["mla_q_norm"][l]
        cols[0:96, l, 269] = inp["mla_k_norm"][l]
        cols[:, l, 270:526] = np.broadcast_to(inp["diff_lambda"][l].reshape(1, 256), (128, 256))
    return cols


_NC_CACHE = {}


def get_nc(dbg=()):
    key = tuple(dbg)
    if key not in _NC_CACHE:
        kb = KB(dbg)
        kb.build()
        _NC_CACHE[key] = kb
    return _NC_CACHE[key]


def make_in_maps(inp):
    inp = {k: np.asarray(v) for k, v in inp.items()}
    cols = _pack_cols(inp)
    constm = _const_mats()
    shared = {k: np.ascontiguousarray(inp[k], dtype=np.float32) for k in
              ("mod_w", "w_in", "w_conv_out", "w_diff_out", "w_uq", "w_ukv", "w_mla_out", "w_out", "w_up", "w_down")}
    maps = []
    for c in range(8):
        g, r = c // 4, c % 4
        xp = inp["x_prompt"][2 * c:2 * c + 2].reshape(512, D)
        xs = inp["x_sample"][g, r * 512:(r + 1) * 512]
        xT = np.ascontiguousarray(np.concatenate([xp, xs], 0).T)
        cond = np.stack([inp["c_ctx"].reshape(8, 128).T, inp["c"][g].reshape(8, 128).T], axis=2)
        sel = np.zeros((128, 8), np.float32)
        if r > 0:
            sel[:, r - 1] = 1.0
        if r < 3:
            sel[:, 4 + r + 1] = 1.0
        m = dict(shared)
        m.update({
            "xT": xT, "cond": np.ascontiguousarray(cond, dtype=np.float32), "cols": cols, "constm": constm,
            "rope": _rope_tables(r), "sel": sel,
            "ckT": np.ascontiguousarray(inp["cache_diff_k"][g].reshape(DEPTH, 256, 512).transpose(0, 2, 1)),
            "cv": np.ascontiguousarray(inp["cache_diff_v"][g].reshape(DEPTH, 256, 512)),
            "cckvT": np.ascontiguousarray(inp["cache_mla_ckv"][g].transpose(0, 2, 1)),
            "ckpeT": np.ascontiguousarray(inp["cache_mla_kpe"][g].transpose(0, 2, 1)),
        })
        maps.append(m)
    return maps


def assemble(results):
    y_prompt = np.zeros((16, 256, D), np.float32)
    y_sample = np.zeros((2, 2048, D), np.float32)
    ndk = np.zeros((16, DEPTH, 256, 4, 128), np.float32)
    ndv = np.zeros((16, DEPTH, 256, 4, 128), np.float32)
    nckv = np.zeros((16, DEPTH, 256, 256), np.float32)
    nkpe = np.zeros((16, DEPTH, 256, 32), np.float32)
    for c in range(8):
        g, r = c // 4, c % 4
        res = results[c]
        yT = res["yT"]
        y_prompt[2 * c:2 * c + 2] = yT[:, 0:512].T.reshape(2, 256, D)
        y_sample[g, r * 512:(r + 1) * 512] = yT[:, 512:].T
        for l in range(DEPTH):
            ndk[2 * c:2 * c + 2, l] = res["o_dk"][l].T.reshape(2, 256, 4, 128)
            ndv[2 * c:2 * c + 2, l] = res["o_dv"][l].reshape(2, 256, 4, 128)
            nckv[2 * c:2 * c + 2, l] = res["o_ckv"][l].T.reshape(2, 256, 256)
            nkpe[2 * c:2 * c + 2, l] = res["o_kpe"][l].T.reshape(2, 256, 32)
    return (y_prompt, y_sample, ndk, ndv, nckv, nkpe)


def kernel(**inputs):
    kb = get_nc()
    maps = make_in_maps(inputs)
    res = run_bass_kernel_spmd(kb.nc, maps, core_ids=list(range(8)))
    return assemble(res.results)
```

```python
import contextlib
import math
import numpy as np
import concourse.bass as bass
import concourse.mybir as mybir
from concourse.bass_utils import run_bass_kernel_spmd

F32 = mybir.dt.float32
BF16 = mybir.dt.bfloat16
AF = mybir.ActivationFunctionType
ALU = mybir.AluOpType

D = 1024
DEPTH = 2
T = 1024
HALF = 512
EPS = 1e-6
IN_COLS = 6304
NCOLS = 526
GATE0 = 3232


class Buf:
    __slots__ = ("name", "w", "r", "rd", "rpe")

    def __init__(self, name):
        self.name = name
        self.w = None
        self.r = {}
        self.rd = []
        self.rpe = []


class Op:
    __slots__ = ("eng", "kind", "fn", "deps", "gdeps", "flag", "sem", "val", "pidx", "ape", "T", "wb", "cols", "pos")

    def __init__(self, eng, kind, fn):
        self.eng = eng
        self.kind = kind
        self.fn = fn
        self.deps = []
        self.gdeps = []
        self.flag = False
        self.sem = None
        self.val = 0
        self.pidx = -1
        self.ape = -1
        self.T = None
        self.wb = ()
        self.cols = 512
        self.pos = -1


class Sched:
    ENGS = ("pe", "act", "dve", "pool", "sp")
    HND = {"pe": "tensor", "act": "scalar", "dve": "vector", "pool": "gpsimd", "sp": "sync"}
    REORDER = True

    def __init__(self, nc, n_dma_sems=14):
        self.nc = nc
        self.ops = {e: [] for e in self.ENGS}
        self.n_dma_sems = n_dma_sems
        self.npe = 0
        self.last_ape = {}

    def op(self, eng, fn, reads=(), writes=(), kind="c", cols=512):
        o = Op(eng, kind, fn)
        o.cols = cols
        is_pe = (eng == "pe" and kind == "c")
        deps = []
        for b in reads:
            if b.w is not None:
                deps.append(b.w)
        for b in writes:
            if b.w is not None:
                deps.append(b.w)
            deps.extend(b.r.values())
            deps.extend(b.rd)
            if b.rpe and not is_pe:
                o.gdeps.append(b.rpe)
        seen = set()
        ape = -1
        for d in deps:
            if d is o or id(d) in seen:
                continue
            seen.add(id(d))
            if is_pe and d.eng == "pe" and d.kind == "c":
                continue
            o.deps.append(d)
            ape = max(ape, d.pidx if (d.eng == "pe" and d.kind == "c") else d.ape)
        for g in o.gdeps:
            ape = max(ape, g[-1].pidx)
        if is_pe:
            o.pidx = self.npe
            self.npe += 1
            o.wb = tuple(writes)
        else:
            ape = max(ape, self.last_ape.get(eng, -1))
            self.last_ape[eng] = ape
        o.ape = ape
        for b in writes:
            b.w = o
            b.r = {}
            b.rd = []
            b.rpe = []
        for b in reads:
            if b.w is o:
                continue
            if is_pe:
                b.rpe.append(o)
            elif kind == "c":
                b.r[eng] = o
            else:
                b.rd.append(o)
        self.ops[eng].append(o)
        return o

    def reorder_pe(self):
        ops = self.ops["pe"]
        import os
        COST = {"c": float(os.environ.get("KCOST", "2.0")), "d": 4.0, "x": 30.0}
        fin = {}
        out = []
        D = []
        now = [0.0]
        inD = set()

        def Tof(o):
            if o.T is not None:
                return o.T
            stack = [o]
            while stack:
                x = stack[-1]
                if x.T is not None:
                    stack.pop()
                    continue
                t = 0.0
                pending = False
                for d in x.deps:
                    if d.eng == "pe" and d.kind == "c":
                        t = max(t, fin[d.pidx])
                    elif d.T is None:
                        stack.append(d)
                        pending = True
                    else:
                        t = max(t, d.T)
                if pending:
                    continue
                for g in x.gdeps:
                    t = max(t, max(fin[m.pidx] for m in g))
                x.T = t + COST[x.kind]
                stack.pop()
            return o.T

        def ready_time(y):
            t = 0.0
            for d in y.deps:
                t = max(t, Tof(d))
            return t

        def emit(y, forced=False):
            r = ready_time(y)
            now[0] = max(now[0], r) + max(0.035, y.cols / 1900.0 + 0.016)
            fin[y.pidx] = now[0]
            y.pos = len(out)
            out.append(y)

        def can_go(idx):
            x = D[idx]
            for e in D[:idx]:
                if x.ape >= e.pidx:
                    return False
                for b in x.wb:
                    if b in e.wb:
                        return False
            return True

        def drain(limit):
            while D:
                best = None
                cand = None
                for idx in range(len(D)):
                    if not can_go(idx):
                        continue
                    r = ready_time(D[idx])
                    if r <= now[0] + 0.05:
                        best = idx
                        break
                    if cand is None or r < cand[0]:
                        cand = (r, idx)
                if best is None:
                    if len(D) <= limit:
                        return
                    best = cand[1]
                emit(D.pop(best))

        CAP = int(os.environ.get("KCAP", "14")) if self.REORDER else 0
        for y in ops:
            D.append(y)
            drain(CAP)
        drain(0)
        assert len(out) == len(ops)
        self.ops["pe"] = out

    def emit(self, final_waits=()):
        nc = self.nc
        self.reorder_pe()
        for e in self.ENGS:
            for o in self.ops[e]:
                for g in o.gdeps:
                    o.deps.append(max(g, key=lambda m: m.pos))
                o.gdeps = []
        for e in self.ENGS:
            for o in self.ops[e]:
                for d in o.deps:
                    d.flag = True
        for o in final_waits:
            o.flag = True
        with contextlib.ExitStack() as es:
            esem = {e: es.enter_context(nc.semaphore("s_" + e)) for e in self.ENGS}
            dsem = {e: [es.enter_context(nc.semaphore(f"d_{e}{i}")) for i in range(self.n_dma_sems)]
                    for e in ("sp", "pool")}
            ncc = sum(1 for e in self.ENGS for o in self.ops[e] if o.kind == "x")
            csem = [es.enter_context(nc.semaphore(f"cc{i}")) for i in range(ncc)]
            ci = 0
            for e in self.ENGS:
                cnt = 0
                dcnt = [0] * self.n_dma_sems
                dlast = [None] * self.n_dma_sems
                di = 0
                for o in self.ops[e]:
                    if o.kind == "c":
                        if o.flag:
                            cnt += 1
                            o.sem = esem[e]
                            o.val = cnt
                    elif o.kind == "d":
                        k = di % self.n_dma_sems
                        di += 1
                        dcnt[k] += 16
                        o.sem = dsem[e][k]
                        o.val = dcnt[k]
                        if dlast[k] is not None:
                            o.deps.append(dlast[k])
                        dlast[k] = o
                    else:
                        o.sem = csem[ci]
                        o.val = 1
                        ci += 1
            block = es.enter_context(nc.Block())

            def make(e):
                def body(eng):
                    seen = {}
                    for o in self.ops[e]:
                        need = {}
                        for d in o.deps:
                            key = id(d.sem)
                            if seen.get(key, 0) >= d.val:
                                continue
                            if key not in need or need[key][1] < d.val:
                                need[key] = (d.sem, d.val)
                        for key, (s, v) in need.items():
                            eng.wait_ge(s, v)
                            seen[key] = v
                        ins = o.fn(eng)
                        if o.kind == "c":
                            if o.flag:
                                ins.then_inc(o.sem, 1)
                        elif o.kind == "d":
                            ins.then_inc(o.sem, 16)
                        else:
                            ins.then_inc(o.sem)
                    if e == "sp":
                        for o in final_waits:
                            if seen.get(id(o.sem), 0) < o.val:
                                eng.wait_ge(o.sem, o.val)
                                seen[id(o.sem)] = o.val
                return body

            for e in self.ENGS:
                getattr(block, self.HND[e])(make(e))


class Grid:
    def __init__(self, name):
        self.name = name
        self.d = {}

    def __getitem__(self, key):
        b = self.d.get(key)
        if b is None:
            b = self.d[key] = Buf(f"{self.name}{key}")
        return b

    def all(self):
        return list(self.d.values())


class KB:
    def __init__(self, dbg=()):
        self.dbg = set(dbg)
        self.nc = bass.Bass("TRN2", target_bir_lowering=False)
        self.S = Sched(self.nc)
        self.es = contextlib.ExitStack()
        self.outs = []

    def MM(self, out, lhsT, rhs, st, sp, R, W, **kw):
        return self.S.op("pe", lambda e: e.matmul(out, lhsT=lhsT, rhs=rhs, start=st, stop=sp, **kw), R, W, cols=int(rhs.shape[-1]))

    def ACT(self, out, in_, func, R, W, bias=None, scale=None):
        kw = {}
        if bias is not None:
            kw["bias"] = bias
        if scale is not None:
            kw["scale"] = scale
        return self.S.op("act", lambda e: e.activation(out=out, in_=in_, func=func, **kw), R, W)

    def TS(self, out, in0, s1, s2, op0, op1, R, W, eng="dve"):
        if s2 is None:
            return self.S.op(eng, lambda e: e.tensor_scalar(out=out, in0=in0, scalar1=s1, scalar2=None, op0=op0), R, W)
        return self.S.op(eng, lambda e: e.tensor_scalar(out=out, in0=in0, scalar1=s1, scalar2=s2, op0=op0, op1=op1), R, W)

    def TT(self, out, in0, in1, op, R, W, eng="dve"):
        return self.S.op(eng, lambda e: e.tensor_tensor(out=out, in0=in0, in1=in1, op=op), R, W)

    def STT(self, out, in0, scalar, in1, op0, op1, R, W, eng="dve"):
        return self.S.op(eng, lambda e: e.scalar_tensor_tensor(out=out, in0=in0, scalar=scalar, in1=in1, op0=op0, op1=op1), R, W)

    def CP(self, out, in_, R, W, eng="dve"):
        return self.S.op(eng, lambda e: e.tensor_copy(out=out, in_=in_), R, W)

    def MS(self, ap, val, W, eng="dve"):
        return self.S.op(eng, lambda e: e.memset(ap, val), (), W)

    def DMA(self, out, in_, R, W, q="sp"):
        return self.S.op(q, lambda e: e.dma_start(out=out, in_=in_), R, W, kind="d")

    def dump(self, name, ap, R, dt=F32):
        import os
        if os.environ.get("KDUMP", "") == "":
            return
        shape = list(ap.shape)
        t = self.nc.dram_tensor("dbg_" + name, shape, dt, kind="ExternalOutput").ap()
        self.outs.append(self.DMA(t, ap, R, []))
        self.dbg_names = getattr(self, "dbg_names", []) + ["dbg_" + name]

    def FENCE(self, R, W):
        sc = self.scratch
        return self.S.op("dve", lambda e: e.memset(sc[0:1, 0:1], 0.0), (), list(R) + list(W) + [self.b_scratch])

    def sb(self, name, shape, dt):
        return self.es.enter_context(self.nc.sbuf_tensor(name, shape, dt))

    def dram_in(self, name, shape, dt=F32):
        return self.nc.dram_tensor(name, list(shape), dt, kind="ExternalInput").ap()

    def dram_out(self, name, shape, dt=F32):
        return self.nc.dram_tensor(name, list(shape), dt, kind="ExternalOutput").ap()

    def tf(self):
        i = self._tfi % len(self.TF)
        self._tfi += 1
        return self.TF[i], self.b_TF[i]

    def tb(self):
        i = self._tbi % len(self.TB)
        self._tbi += 1
        return self.TB[i], self.b_TB[i]

    def psn(self, pool):
        lst = self.pspools[pool]
        i = self._psi.get(pool, 0)
        self._psi[pool] = i + 1
        b = lst[i % len(lst)]
        return self.PS[b], self.b_PS[b]

    def wnext(self, name, hold=0):
        i = self._wi
        self._wi += 1
        spec = self.wlist[i]
        assert spec[0] == name, (spec[0], name)
        nslot = len(self.WS)
        while self._wissued < min(len(self.wlist), i + nslot - hold):
            j = self._wissued
            nm, src, view = self.wlist[j]
            s = j % nslot
            if isinstance(src, list):
                for bi, sp_ in enumerate(src):
                    self.DMA(view(self.WS[s])[:, :, bi, :], sp_, (), [self.b_WS[s]], q="pool")
            else:
                self.DMA(view(self.WS[s]), src, (), [self.b_WS[s]], q="pool")
            self._wissued += 1
        s = i % nslot
        return spec[2](self.WS[s]), self.b_WS[s]

    def build(self):
        nc = self.nc
        self.xT = self.dram_in("xT", [D, T])
        self.cond = self.dram_in("cond", [128, 8, 2])
        self.cols_d = self.dram_in("cols", [128, DEPTH, NCOLS])
        self.constm = self.dram_in("constm", [128, 6, 128])
        self.rope = self.dram_in("rope", [128, 4, HALF])
        self.sel_d = self.dram_in("sel", [128, 8])
        self.ckT = self.dram_in("ckT", [DEPTH, 512, 256])
        self.cv = self.dram_in("cv", [DEPTH, 256, 512])
        self.cckvT = self.dram_in("cckvT", [DEPTH, 256, 256])
        self.ckpeT = self.dram_in("ckpeT", [DEPTH, 32, 256])
        self.mod_w = self.dram_in("mod_w", [DEPTH, D, 6 * D])
        self.w_in = self.dram_in("w_in", [DEPTH, D, IN_COLS])
        self.w_conv_out = self.dram_in("w_conv_out", [DEPTH, 512, D])
        self.w_diff_out = self.dram_in("w_diff_out", [DEPTH, 512, D])
        self.w_uq = self.dram_in("w_uq", [DEPTH, 384, 768])
        self.w_ukv = self.dram_in("w_ukv", [DEPTH, 256, 1024])
        self.w_mla_out = self.dram_in("w_mla_out", [DEPTH, 512, D])
        self.w_out = self.dram_in("w_out", [DEPTH, D, D])
        self.w_up = self.dram_in("w_up", [DEPTH, D, 4 * D])
        self.w_down = self.dram_in("w_down", [DEPTH, 4 * D, D])

        self.yT = self.dram_out("yT", [D, T])
        self.o_dk = self.dram_out("o_dk", [DEPTH, 512, HALF])
        self.o_dv = self.dram_out("o_dv", [DEPTH, HALF, 512])
        self.o_ckv = self.dram_out("o_ckv", [DEPTH, 256, HALF])
        self.o_kpe = self.dram_out("o_kpe", [DEPTH, 32, HALF])

        def dint(name, shape):
            return nc.dram_tensor(name, list(shape), BF16)
        self.kx_in = [dint(f"kx_in{l}", [512, 512]) for l in range(DEPTH)]
        self.kx_out = [dint(f"kx_out{l}", [2048, 512]) for l in range(DEPTH)]
        self.vx_in = [dint(f"vx_in{l}", [512, 512]) for l in range(DEPTH)]
        self.vx_out = [dint(f"vx_out{l}", [2048, 512]) for l in range(DEPTH)]
        self.kmx_in = [dint(f"kmx_in{l}", [768, 512]) for l in range(DEPTH)]
        self.kmx_out = [dint(f"kmx_out{l}", [3072, 512]) for l in range(DEPTH)]
        self.vmx_in = [dint(f"vmx_in{l}", [512, 512]) for l in range(DEPTH)]
        self.vmx_out = [dint(f"vmx_out{l}", [2048, 512]) for l in range(DEPTH)]
        self.hb_in = [dint(f"hb_in{l}", [512, 32]) for l in range(DEPTH)]
        self.hb_out = [dint(f"hb_out{l}", [2048, 32]) for l in range(DEPTH)]
        self.b_x = {n: [Buf(f"{n}_in{l}") for l in range(DEPTH)] for n in ("kx", "vx", "kmx", "vmx", "hb")}
        self.b_xo = {n: [Buf(f"{n}_out{l}") for l in range(DEPTH)] for n in ("kx", "vx", "kmx", "vmx", "hb")}

        sb = self.sb
        self.X = sb("X", [128, 8, T], F32)
        self.bX = Grid("X")
        self.HT = sb("HT", [128, 8, T], BF16)
        self.bHT = Grid("HT")
        self.RB = sb("RB", [128, 8192], BF16)
        RB = self.RB
        self.ODT = RB[:, 0:4096].rearrange("p (c t) -> p c t", c=4)
        self.OMT = RB[:, 4096:8192].rearrange("p (c t) -> p c t", c=4)
        self.bODT = Grid("ODT")
        self.bOMT = Grid("OMT")
        NSLOT = 3
        self.WS = [sb(f"WS{i}", [128, 4096], BF16) for i in range(NSLOT)]
        self.b_WS = [Buf(f"WS{i}") for i in range(NSLOT)]
        self.TF = [sb(f"TF{i}", [128, HALF], F32) for i in range(7)]
        self.b_TF = [Buf(f"TF{i}") for i in range(7)]
        self.TB = [sb(f"TB{i}", [128, HALF], BF16) for i in range(7)]
        self.b_TB = [Buf(f"TB{i}") for i in range(7)]
        self._tfi = self._tbi = 0
        self.CACC = sb("CACC", [128, T], F32)
        self.bACC = [Buf("ACC0"), Buf("ACC1")]
        self.DA = [sb(f"DA{i}", [128, HALF], F32) for i in range(2)]
        self.b_DA = [Buf(f"DA{i}") for i in range(2)]
        self.RS = [sb(f"RS{i}", [128, HALF], F32) for i in range(2)]
        self.b_RS = [Buf(f"RS{i}") for i in range(2)]
        self.CB = sb("CB", [128, 6, 128], BF16)
        self.bCB = Buf("CB")
        self.ROPE = sb("ROPE", [128, 4, HALF], F32)
        self.bROPE = Buf("ROPE")
        self.SEL = sb("SEL", [128, 8], F32)
        self.bSEL = Buf("SEL")
        self.COLS = sb("COLS", [128, DEPTH, NCOLS], F32)
        self.bCOLS = Buf("COLS")
        self.CONDF = sb("CONDF", [128, 8, 2], F32)
        self.CONDB = sb("CONDB", [128, 8, 2], BF16)
        self.bCOND = Buf("COND")
        self.bCONDB = Buf("CONDB")
        self.MODC_ = sb("MODC", [128, DEPTH, 48, 2], F32)
        self.bMC = Grid("MC")
        self.MODA_ = sb("MODA", [128, DEPTH, 2, 8, 2], F32)
        self.bMA = Grid("MA")
        self.LAM_ = sb("LAM", [128, DEPTH, 8], F32)
        self.bLAMg = Grid("LAM")
        self.LTMP = sb("LTMP", [128, 2, 64], F32)
        self.bLT = Buf("LTMP")
        self.scratch = sb("scr", [128, 2], F32)
        self.b_scratch = Buf("scr")
        self.WK = sb("WK", [128, 2, 8, 96], BF16)
        self.WV = sb("WV", [128, 2, 512], BF16)
        self.bWK = Buf("WK")
        self.bWV = Buf("WV")
        self.UP = sb("UP", [128, 4, 2, 286], BF16)
        self.US = sb("US", [128, 4, 542], BF16)
        self.bUP = Grid("UP")
        self.bUS = Grid("US")
        self.HL = sb("HL", [128, 4, 4, 32], BF16)
        self.bHL = Buf("HL")
        self.HLF = sb("HLF", [128, 4, 2, 16], F32)
        self.bHLF = Buf("HLF")

        o = 0
        self.CQN = RB[:, o:o + 3072].rearrange("p (c t) -> p c t", c=3); o += 3072
        self.CKVN = RB[:, o:o + 2048].rearrange("p (c t) -> p c t", c=2); o += 2048
        self.KPE = RB[:, o:o + 1024]; o += 1024
        self.CC = RB[:, o:o + 512].rearrange("p (c t) -> p c t", c=2); o += 512
        self.CKPE = RB[:, o:o + 256]; o += 256
        RA_N = 28544
        self.RA = sb("RA", [128, RA_N], BF16)
        RA = self.RA
        off = [0]

        def carve(n):
            a = off[0]
            off[0] += n
            return a
        o = carve(4096); self.QT = RA[:, o:o + 4096].rearrange("p (c t) -> p c t", c=4)
        o = carve(8192); self.QMT = RA[:, o:o + 8192].rearrange("p (c t) -> p c t", c=8)
        o = carve(2048); self.KMTC = RA[:, o:o + 2048].rearrange("p (c t) -> p c t", c=8)
        o = carve(1024); self.VMC = RA[:, o:o + 1024].rearrange("p (c t) -> p c t", c=2)
        o = carve(1024); self.KC = RA[:, o:o + 1024].rearrange("p (c t) -> p c t", c=4)
        o = carve(1024); self.VC = RA[:, o:o + 1024].rearrange("p (c t) -> p c t", c=2)
        un = carve(10240)
        o = un
        self.KTP = RA[:, o:o + 2048].rearrange("p (c t) -> p c t", c=4); o += 2048
        self.VP = RA[:, o:o + 2048].rearrange("p (c t) -> p c t", c=4); o += 2048
        self.KMTP = RA[:, o:o + 4096].rearrange("p (c t) -> p c t", c=8); o += 4096
        self.VMP = RA[:, o:o + 2048].rearrange("p (c t) -> p c t", c=4); o += 2048
        self.KS = [RA[:, un + i * 2048: un + (i + 1) * 2048] for i in range(2)]
        self.VS = [RA[:, un + 4096 + i * 2048: un + 4096 + (i + 1) * 2048].rearrange("p (n e) -> p n e", n=16) for i in range(2)]
        life1_end = off[0]
        o2 = 0
        self.ST = RA[:, o2:o2 + 4096].rearrange("p (c t) -> p c t", c=4); o2 += 4096
        self.DG = RA[:, o2:o2 + 3968].rearrange("p (k c) -> p k c", k=31); o2 += 3968
        self.MT = RA[:, o2:o2 + 8192].rearrange("p (c t) -> p c t", c=8); o2 += 8192
        self.AT = RA[:, o2:o2 + 8192].rearrange("p (c t) -> p c t", c=8)
        self.WCO = RA[:, o2:o2 + 4096].rearrange("p (k c) -> p k c", k=4); o2 += 4096
        self.WDO = RA[:, o2:o2 + 4096].rearrange("p (k c) -> p k c", k=4); o2 += 4096
        self.WMO = RA[:, o2:o2 + 4096].rearrange("p (k c) -> p k c", k=4); o2 += 4096
        assert o2 <= RA_N and life1_end <= RA_N, (o2, life1_end)
        self.bQT = Grid("QT"); self.bKTP = Grid("KTP"); self.bVP = Grid("VP"); self.bQMT = Grid("QMT")
        self.bKMTP = Grid("KMTP"); self.bVMP = Grid("VMP"); self.bKMTC = Grid("KMTC"); self.bVMC = Grid("VMC")
        self.bKC = Buf("KC"); self.bVC = Buf("VC")
        self.bKS = [Buf("KS0"), Buf("KS1")]; self.bVS = [Buf("VS0"), Buf("VS1")]
        self.bCQN = Grid("CQN"); self.bCKVN = Grid("CKVN"); self.bKPE = Grid("KPE"); self.bCC = Buf("CC"); self.bCKPE = Buf("CKPE")
        self.bST = Grid("ST"); self.bDG = Buf("DG"); self.bMT = Grid("MT"); self.bAT = Grid("AT")
        self.bWCO = Buf("WCO"); self.bWDO = Buf("WDO"); self.bWMO = Buf("WMO")
        for g, n1, n2 in ((self.bQT, 4, 2), (self.bQMT, 8, 2), (self.bODT, 4, 2), (self.bOMT, 8, 2), (self.bCQN, 3, 2), (self.bCKVN, 2, 2),
                          (self.bST, 4, 2), (self.bMT, 8, 2), (self.bAT, 8, 2)):
            for a in range(n1):
                for b_ in range(n2):
                    g[a, b_]
        for g, n1 in ((self.bKTP, 4), (self.bVP, 4), (self.bKMTP, 8), (self.bVMP, 4), (self.bKMTC, 8), (self.bVMC, 2), (self.bKPE, 2)):
            for a in range(n1):
                g[a]

        self.PS = [self.es.enter_context(nc.psum_tensor(f"ps{i}", [128, HALF], F32)) for i in range(8)]
        self.b_PS = [Buf(f"ps{i}") for i in range(8)]
        self.pspools = {"d": [0, 1, 2, 3, 4, 5], "s": [6, 7], "a": [0, 1, 2, 3], "a6": [0, 1, 2, 3, 6, 7]}
        self._psi = {}

        self.wlist = []

        def v3(nk, ncol):
            return lambda slot: slot[:, 0:nk * ncol].rearrange("p (k c) -> p k c", k=nk)
        specs = {}
        for l in range(DEPTH):
            mw = self.mod_w[l].rearrange("(k p) c -> p k c", p=128)
            wi = self.w_in[l].rearrange("(k p) c -> p k c", p=128)
            for j in range(12):
                specs[f"mod{l}_{j}"] = (mw[:, :, 512 * j:512 * (j + 1)], v3(8, 512))
            for nm, a, b in (("ua", 0, 512), ("ug", 512, 1024), ("dq", 1024, 1536), ("dk", 1536, 2048),
                             ("dv", 2048, 2560), ("cq", 2560, 2944), ("ckv", 2944, 3232)):
                specs[f"{nm}{l}"] = (wi[:, :, a:b], v3(8, b - a))
            specs[f"uq{l}"] = (self.w_uq[l].rearrange("(k p) c -> p k c", p=128), v3(3, 768))
            wg = self.w_in[l][:, GATE0:IN_COLS].rearrange("(k p) (b j c) -> p k b j c", p=128, b=3, j=8)
            for j in range(8):
                specs[f"mg{l}_{j}"] = ([wg[:, :, b_, j, :] for b_ in range(3)],
                                       lambda slot: slot[:, 0:8 * 384].rearrange("p (k b c) -> p k b c", k=8, b=3))
            wo = self.w_out[l].rearrange("(k p) c -> p k c", p=128)
            for j in range(2):
                specs[f"wo{l}_{j}"] = (wo[:, :, 512 * j:512 * (j + 1)], v3(8, 512))
            wu = self.w_up[l].rearrange("(k p) c -> p k c", p=128)
            wd = self.w_down[l].rearrange("(q k p) c -> q p k c", q=4, p=128)
            for q in range(4):
                for j in range(2):
                    specs[f"up{l}_{q}_{j}"] = (wu[:, :, 1024 * q + 512 * j: 1024 * q + 512 * (j + 1)], v3(8, 512))
                for j in range(2):
                    specs[f"dn{l}_{q}_{j}"] = (wd[q][:, :, 512 * j:512 * (j + 1)], v3(8, 512))
        order = []
        for l in range(DEPTH):
            if l == 0:
                order += [f"mod0_{j}" for j in range(4)]
            order += [f"ua{l}", f"ug{l}", f"dq{l}", f"dk{l}", f"dv{l}", f"cq{l}", f"ckv{l}", f"uq{l}"]
            order += [f"mod{l}_{j}" for j in range(4, 12)]
            order += [f"mg{l}_{j}" for j in range(8)] + [f"wo{l}_0", f"wo{l}_1"]
            for q in range(4):
                order += [f"up{l}_{q}_0", f"up{l}_{q}_1", f"dn{l}_{q}_0", f"dn{l}_{q}_1"]
                if l + 1 < DEPTH and q < 2:
                    order += [f"mod{l + 1}_{2 * q}", f"mod{l + 1}_{2 * q + 1}"]
        self.wlist = [(nm,) + specs[nm] for nm in order]
        self._wi = 0
        self._wissued = 0

        self.prologue()
        try:
            for l in range(DEPTH):
                self.layer(l)
        except StopIteration:
            pass
        finals = self.epilogue()
        self.S.emit(final_waits=finals + self.outs)
        return nc

    def prologue(self):
        for k in range(8):
            for h in range(2):
                self.DMA(self.X[:, k, h * HALF:(h + 1) * HALF], self.xT[k * 128:(k + 1) * 128, h * HALF:(h + 1) * HALF],
                         (), [self.bX[k, h]])
        self.DMA(self.CONDF[:], self.cond, (), [self.bCOND])
        self.DMA(self.COLS[:], self.cols_d, (), [self.bCOLS])
        self.DMA(self.ROPE[:], self.rope, (), [self.bROPE])
        self.DMA(self.SEL[:], self.sel_d, (), [self.bSEL])
        self.DMA(self.CB[:], self.constm, (), [self.bCB], q="pool")
        self.ACT(self.CONDB[:], self.CONDF[:], AF.Silu, [self.bCOND], [self.bCONDB])
        for j in range(4):
            self.MS(self.UP[:, j], 0.0, [self.bUP[j]], eng="pool")
        self.MS(self.WK[:], 0.0, [self.bWK], eng="pool")
        self.MS(self.HLF[:], 0.0, [self.bHLF], eng="pool")

    def col(self, l, a, n=1):
        return self.COLS[:, l, a:a + n]

    def layer(self, l):
        mla_t = self.bCQN.all() + self.bCKVN.all() + self.bKPE.all() + [self.bCC, self.bCKPE]
        ponly = self.bKTP.all() + self.bVP.all() + self.bKMTP.all() + self.bVMP.all()
        life1 = (self.bQT.all() + self.bQMT.all() + self.bKMTC.all() + self.bVMC.all() + [self.bKC, self.bVC]
                 + self.bKS + self.bVS + ponly)
        wo3 = [self.bWCO, self.bWDO, self.bWMO]
        life2 = self.bST.all() + [self.bDG] + self.bMT.all() + wo3
        obr = self.bODT.all() + self.bOMT.all()
        import os
        stop = os.environ.get("KSTOP", "")

        def chk(name):
            if stop == f"{name}{l}":
                raise StopIteration
        self.chk = chk
        self.stage_lam(l)
        if l == 0:
            for j in range(4):
                self.mod_block(0, j)
        chk("mod")
        self.stage_norm(l, 0)
        if l == 0:
            self.dump("ht", self.HT[:].rearrange("p k t -> p (k t)"), self.bHT.all(), BF16)
        chk("norm")
        self.stage_convproj(l)
        if l == 0:
            self.dump("us", self.US[:, :, 15:527], [self.bUS[j] for j in range(4)], BF16)
            self.dump("up", self.UP[:, :, :, 15:271], [self.bUP[j] for j in range(4)], BF16)
        chk("convproj")
        self.stage_diffproj(l)
        if l == 0:
            self.dump("qt", self.QT, self.bQT.all(), BF16)
            self.dump("ktp", self.KTP, self.bKTP.all(), BF16)
            self.dump("vp", self.VP, self.bVP.all(), BF16)
            self.dump("kxin", self.kx_in[l].ap(), [self.b_x["kx"][l]], BF16)
            self.dump("vxin", self.vx_in[l].ap(), [self.b_x["vx"][l]], BF16)
        chk("diffproj")
        self.stage_mlaproj(l)
        if l == 0:
            self.dump("qmt", self.QMT[0:96], self.bQMT.all(), BF16)
            self.dump("kmtp", self.KMTP[0:96], self.bKMTP.all(), BF16)
            self.dump("vmp", self.VMP, self.bVMP.all(), BF16)
            self.dump("cqn", self.CQN, self.bCQN.all(), BF16)
            self.dump("kmxin", self.kmx_in[l].ap(), [self.b_x["kmx"][l]], BF16)
            self.dump("vmxin", self.vmx_in[l].ap(), [self.b_x["vmx"][l]], BF16)
            self.dump("kmtc", self.KMTC[0:96], self.bKMTC.all(), BF16)
            self.dump("vmc", self.VMC, self.bVMC.all(), BF16)
        chk("mlaproj")
        self.FENCE(mla_t, obr)
        self.stage_attn_P(l); chk("attnP")
        self.FENCE(ponly, self.bKS + self.bVS)
        self.stage_attn_S(l)
        if l == 0:
            self.dump("odt", self.ODT, self.bODT.all(), BF16)
            self.dump("omt", self.OMT, self.bOMT.all(), BF16)
        chk("attnS")
        self.FENCE(life1, life2)
        self.stage_conv(l)
        if l == 0:
            self.dump("st", self.ST, self.bST.all(), BF16)
        chk("conv")
        self.stage_merge(l)
        if l == 0:
            self.dump("mt", self.MT, self.bMT.all(), BF16)
            self.dump("xmid", self.X[:], self.bX.all())
        chk("merge")
        self.FENCE(wo3, self.bAT.all())
        self.stage_norm(l, 1)
        self.stage_mlp(l); chk("mlp")
        if l + 1 < DEPTH:
            self.FENCE(life2 + self.bAT.all() + obr, life1 + mla_t)

    C_MODB = 0
    C_N1G = 96
    C_N2G = 112
    C_DW = 128
    C_CB = 252
    C_CNG = 256
    C_QG = 260
    C_KG = 261
    C_SLG = 262
    C_CQG = 263
    C_KVG = 266
    C_QMG = 268
    C_KMG = 269
    C_LAM = 270

    def mod_block(self, l, j, pool="s"):
        pm, bpm = self.psn(pool)
        slot, bs = self.wnext(f"mod{l}_{j}")
        for mi in range(4):
            for k in range(8):
                self.MM(pm[:, 2 * mi:2 * mi + 2], slot[:, k, mi * 128:(mi + 1) * 128], self.CONDB[:, k, :],
                        k == 0, k == 7, [bs, self.bCONDB], [bpm])
        self.TT(self.MODC_[:, l, 4 * j:4 * j + 4, :].rearrange("p m c -> p (m c)"), pm[:, 0:8],
                self.COLS[:, l, self.C_MODB + 8 * j:self.C_MODB + 8 * j + 8], ALU.add, [bpm, self.bCOLS], [self.bMC[l, j]])
        for i, (sc0, gcol, last) in enumerate(((8, self.C_N1G, 3), (32, self.C_N2G, 9))):
            if j == last:
                self.STT(self.MODA_[:, l, i].rearrange("p k c -> p (k c)"),
                         self.MODC_[:, l, sc0:sc0 + 8, :].rearrange("p k c -> p (k c)"), 1.0,
                         self.COLS[:, l, gcol:gcol + 16], ALU.add, ALU.mult, [self.bMC[l, j - 1], self.bMC[l, j], self.bCOLS], [self.bMA[l, i]])

    def stage_lam(self, l):
        LAM = self.LAM_[:, l, :]
        bL = self.bLAMg[l]
        lam_init = 0.8 - 0.6 * math.exp(-0.3 * l)
        lp = self.COLS[:, l, self.C_LAM:self.C_LAM + 256].rearrange("p (r d) -> p r d", r=4)
        self.TT(self.LTMP[:, 0, :], lp[:, 0, :], lp[:, 1, :], ALU.mult, [self.bCOLS], [self.bLT])
        self.TT(self.LTMP[:, 1, :], lp[:, 2, :], lp[:, 3, :], ALU.mult, [self.bCOLS, self.bLT], [self.bLT])
        self.S.op("dve", lambda e: e.tensor_reduce(out=LAM[:, 0:2], in_=self.LTMP[:], axis=mybir.AxisListType.X, op=ALU.add),
                  [self.bLT], [bL])
        self.ACT(LAM[:, 2:4], LAM[:, 0:2], AF.Exp, [bL], [bL])
        self.TT(LAM[:, 4:5], LAM[:, 3:4], LAM[:, 2:3], ALU.subtract, [bL], [bL])
        self.TS(LAM[:, 5:6], LAM[:, 4:5], -lam_init, None, ALU.add, None, [bL], [bL])
        self.TS(LAM[:, 6:7], self.col(l, self.C_SLG), 1.0 - lam_init, None, ALU.mult, None, [bL, self.bCOLS], [bL])

    def modcol(self, l, part, k, h):
        return self.MODC_[:, l, part * 8 + k, h:h + 1]

    def bmod(self, l, part):
        return [self.bMC[l, 2 * part], self.bMC[l, 2 * part + 1]]

    def rstd_from(self, stats_ps, bps, npart=128, n=HALF, dst=None):
        r, br = dst if dst is not None else self.tf()
        self.ACT(r[0:npart, 0:n], stats_ps[0:npart, 0:n], AF.Ln, [bps], [br], bias=EPS)
        self.ACT(r[0:npart, 0:n], r[0:npart, 0:n], AF.Exp, [br], [br], scale=-0.5)
        return r, br

    def ident(self):
        return self.CB[:, 0, :]

    def ones(self):
        return self.CB[:, 1, :]

    def stage_norm(self, l, which):
        part_sh, part_sc = (0, 1) if which == 0 else (3, 4)
        for h in range(2):
            ts = slice(h * HALF, (h + 1) * HALF)
            pst, bpst = self.psn("s")
            for k in range(8):
                sq, bsq = self.tb()
                self.ACT(sq[:], self.X[:, k, ts], AF.Square, [self.bX[k, h]], [bsq], scale=1.0 / 32.0)
                self.MM(pst[:], self.ones(), sq[:], k == 0, k == 7, [self.bCB, bsq], [bpst])
            r, br = self.rstd_from(pst, bpst, dst=(self.RS[h], self.b_RS[h]))
            for k in range(8):
                t, bt = self.tf()
                self.STT(t[:], self.X[:, k, ts], self.MODA_[:, l, which, k, h:h + 1], r[:], ALU.mult, ALU.mult,
                         [self.bX[k, h], self.bMA[l, which], br], [bt])
                self.ACT(self.HT[:, k, ts], t[:], AF.Identity, [bt] + self.bmod(l, part_sh), [self.bHT[k, h]],
                         bias=self.modcol(l, part_sh, k, h))

    def proj(self, slot, bs, c0, m, h, pool="d", nk=8, src=None, bsrc=None):
        ps, bps = self.psn(pool)
        ts = slice(h * HALF, (h + 1) * HALF)
        for k in range(nk):
            if src is None:
                rhs, br = self.HT[:, k, ts], self.bHT[k, h]
            else:
                rhs, br = src[:, k, ts], bsrc[k, h]
            self.MM(ps[0:m, :], slot[:, k, c0:c0 + m], rhs, k == 0, k == nk - 1, [bs, br], [bps])
        return ps, bps

    def stage_convproj(self, l):
        sa, bsa = self.wnext(f"ua{l}")
        sg, bsg = self.wnext(f"ug{l}", hold=1)
        for j in range(4):
            for h in range(2):
                pa, bpa = self.proj(sa, bsa, j * 128, 128, h)
                pg, bpg = self.proj(sg, bsg, j * 128, 128, h)
                sgm, bsgm = self.tf()
                self.ACT(sgm[:], pg[:], AF.Sigmoid, [bpg], [bsgm])
                if h == 0:
                    for s in range(2):
                        self.TT(self.UP[:, j, s, 15:271], pa[:, s * 256:(s + 1) * 256], sgm[:, s * 256:(s + 1) * 256], ALU.mult,
                                [bpa, bsgm], [self.bUP[j]])
                else:
                    self.TT(self.US[:, j, 15:527], pa[:], sgm[:], ALU.mult, [bpa, bsgm], [self.bUS[j]])
        hb = self.hb_in[l].ap().rearrange("(j p) c -> p j c", p=128)
        o1 = self.DMA(hb[:, :, 0:16], self.US[:, :, 15:31], [self.bUS[j] for j in range(4)], [self.b_x["hb"][l]])
        o2 = self.DMA(hb[:, :, 16:32], self.US[:, :, 511:527], [self.bUS[j] for j in range(4)], [self.b_x["hb"][l]])
        self.allgather("hb", l)

    def allgather(self, name, l):
        src = getattr(self, name + "_in")[l]
        dst = getattr(self, name + "_out")[l]
        self.S.op("pool", lambda e: e.collective_compute("AllGather", ALU.bypass, replica_groups=[[0, 1, 2, 3], [4, 5, 6, 7]],
                                                         ins=[src.ap().opt()], outs=[dst.ap().opt()]),
                  [self.b_x[name][l]], [self.b_xo[name][l]], kind="x")

    def blocknorm(self, ps, bps, npart, ones_ap, inv_sqrt_n, gcol, bgs, n=HALF):
        sq, bsq = self.tb()
        self.ACT(sq[0:npart, 0:n], ps[0:npart, 0:n], AF.Square, [bps], [bsq], scale=inv_sqrt_n)
        pst, bpst = self.psn("s")
        self.MM(pst[0:npart, 0:n], ones_ap, sq[0:npart, 0:n], True, True, [self.bCB, bsq], [bpst])
        r, br = self.rstd_from(pst, bpst, npart, n)
        t, bt = self.tf()
        self.STT(t[0:npart, 0:n], ps[0:npart, 0:n], gcol, r[0:npart, 0:n], ALU.mult, ALU.mult, [bps, br] + bgs, [bt])
        return t, bt

    def rope_from(self, tb_, btb, npart, perm_ap, ctab, stab, out_ap, bout):
        psw, bpsw = self.psn("d")
        self.MM(psw[0:npart, :], perm_ap, tb_[0:npart, :], True, True, [self.bCB, btb], [bpsw])
        t1, bt1 = self.tf()
        self.TT(t1[0:npart, :], tb_[0:npart, :], ctab, ALU.mult, [btb, self.bROPE], [bt1])
        t2, bt2 = self.tf()
        self.TT(t2[0:npart, :], psw[0:npart, :], stab, ALU.mult, [bpsw, self.bROPE], [bt2])
        self.TT(out_ap, t1[0:npart, :], t2[0:npart, :], ALU.add, [bt1, bt2], bout)

    def bn_run(self, units):
        def A(u):
            ps, bps = u["proj"]()
            npart, n = u["npart"], u.get("n", HALF)
            sq, bsq = self.tb()
            self.ACT(sq[0:npart, 0:n], ps[0:npart, 0:n], AF.Square, [bps], [bsq], scale=u["isn"])
            u["c"] = (ps, bps, sq, bsq)

        def B(u):
            ps, bps, sq, bsq = u["c"]
            npart, n = u["npart"], u.get("n", HALF)
            pst, bpst = self.psn("s")
            self.MM(pst[0:npart, 0:n], u["ones"], sq[0:npart, 0:n], True, True, [self.bCB, bsq], [bpst])
            u["r"] = self.rstd_from(pst, bpst, npart, n)

        def C(u):
            ps, bps, sq, bsq = u["c"]
            r, br = u["r"]
            npart, n = u["npart"], u.get("n", HALF)

            def norm(dst, bdst):
                self.STT(dst, ps[0:npart, 0:n], u["gcol"], r[0:npart, 0:n], ALU.mult, ALU.mult, [bps, br] + u["bgs"], bdst)
            u["post"](norm)
        nu = len(units)
        for i in range(nu + 2):
            if i < nu:
                A(units[i])
            if 0 <= i - 1 < nu:
                B(units[i - 1])
            if 0 <= i - 2 < nu:
                C(units[i - 2])

    def stage_diffproj(self, l):
        blk64 = self.CB[:, 2, :]
        Pd = self.CB[:, 3, :]
        sq_, bsq_ = self.wnext(f"dq{l}")
        sk, bsk = self.wnext(f"dk{l}", hold=1)
        kx = self.kx_in[l].ap().rearrange("(j p) t -> p j t", p=128)
        units = []
        for which in range(2):
            for j in range(4):
                for h in range(2):
                    def post(norm, which=which, j=j, h=h):
                        if which == 0:
                            if h == 0:
                                norm(self.QT[:, j, 0:HALF], [self.bQT[j, 0]])
                            else:
                                tb_, btb = self.tb()
                                norm(tb_[:], [btb])
                                self.rope_from(tb_, btb, 128, Pd, self.ROPE[:, 0, :], self.ROPE[:, 1, :], self.QT[:, j, HALF:T], [self.bQT[j, 1]])
                        else:
                            if h == 0:
                                t, bt = self.tf()
                                norm(t[:], [bt])
                                self.CP(self.KTP[:, j, :], t[:], [bt], [self.bKTP[j]])
                                self.outs.append(self.DMA(self.o_dk[l, j * 128:(j + 1) * 128, :], t[:], [bt], []))
                            else:
                                tb_, btb = self.tb()
                                norm(tb_[:], [btb])
                                st, bst = self.tb()
                                self.rope_from(tb_, btb, 128, Pd, self.ROPE[:, 0, :], self.ROPE[:, 1, :], st[:], [bst])
                                self.DMA(kx[:, j, :], st[:], [bst], [self.b_x["kx"][l]])
                    slot, bslot = (sq_, bsq_) if which == 0 else (sk, bsk)
                    units.append(dict(proj=(lambda slot=slot, bslot=bslot, j=j, h=h: self.proj(slot, bslot, j * 128, 128, h)),
                                      npart=128, ones=blk64, isn=0.125,
                                      gcol=self.col(l, self.C_QG if which == 0 else self.C_KG), bgs=[self.bCOLS], post=post))
        self.bn_run(units)
        self.chk("dq")
        self.chk("dk")
        sv, bsv = self.wnext(f"dv{l}")
        vx = self.vx_in[l].ap().rearrange("(n p) e -> p n e", p=128)
        for tt in range(8):
            ps, bps = self.psn("d")
            h = tt // 4
            for k in range(8):
                self.MM(ps[:], self.HT[:, k, tt * 128:(tt + 1) * 128], sv[:, k, :], k == 0, k == 7, [bsv, self.bHT[k, h]], [bps])
            if tt < 4:
                vf, bvf = self.tf()
                self.ACT(vf[:], ps[:], AF.Copy, [bps], [bvf])
                self.outs.append(self.DMA(self.o_dv[l, tt * 128:(tt + 1) * 128, :], vf[:], [bvf], []))
                self.CP(self.VP[:, tt, :], vf[:], [bvf], [self.bVP[tt]])
            else:
                st, bst = self.tb()
                self.ACT(st[:], ps[:], AF.Copy, [bps], [bst])
                self.DMA(vx[:, tt - 4, :], st[:], [bst], [self.b_x["vx"][l]])
        self.chk("dv")
        self.allgather("kx", l)
        self.allgather("vx", l)
        self.chk("dag")
        self.DMA(self.KC, self.ckT[l].rearrange("(j p) t -> p j t", p=128), (), [self.bKC], q="pool")
        self.DMA(self.VC, self.cv[l].rearrange("(n p) e -> p n e", p=128), (), [self.bVC], q="pool")

    def stage_mlaproj(self, l):
        ones = self.ones()
        Pm = self.CB[0:96, 4, 0:96]
        EXT = self.CB[0:32, 5, 0:96]
        wk_src = self.w_ukv[l].rearrange("(k p) (h c) -> p k h c", p=128, h=8)
        for k in range(2):
            self.DMA(self.WK[:, k, :, 0:64], wk_src[:, k, :, 0:64], (), [self.bWK], q="pool")
            self.DMA(self.WV[:, k, :].rearrange("p (h c) -> p h c", h=8), wk_src[:, k, :, 64:128], (), [self.bWV], q="pool")
        self.DMA(self.CC, self.cckvT[l].rearrange("(k p) t -> p k t", p=128), (), [self.bCC], q="pool")
        self.DMA(self.CKPE[0:32, :], self.ckpeT[l], (), [self.bCKPE], q="pool")
        scq, bscq = self.wnext(f"cq{l}")
        for h in range(2):
            ts = slice(h * HALF, (h + 1) * HALF)
            raws = []
            pst, bpst = self.psn("s")
            for m in range(3):
                ps, bps = self.psn("a")
                for k in range(8):
                    self.MM(ps[:], scq[:, k, m * 128:(m + 1) * 128], self.HT[:, k, ts], k == 0, k == 7, [bscq, self.bHT[k, h]], [bps])
                sq, bsq = self.tb()
                self.ACT(sq[:], ps[:], AF.Square, [bps], [bsq], scale=384.0 ** -0.5)
                self.MM(pst[:], ones, sq[:], m == 0, m == 2, [self.bCB, bsq], [bpst])
                raws.append((ps, bps))
            r, br = self.rstd_from(pst, bpst)
            for m in range(3):
                ps, bps = raws[m]
                self.STT(self.CQN[:, m, ts], ps[:], self.col(l, self.C_CQG + m), r[:], ALU.mult, ALU.mult,
                         [bps, br, self.bCOLS], [self.bCQN[m, h]])
        sc, bsc = self.wnext(f"ckv{l}")
        for h in range(2):
            ts = slice(h * HALF, (h + 1) * HALF)
            raws = []
            pst, bpst = self.psn("s")
            for m in range(2):
                ps, bps = self.psn("a")
                for k in range(8):
                    self.MM(ps[:], sc[:, k, m * 128:(m + 1) * 128], self.HT[:, k, ts], k == 0, k == 7, [bsc, self.bHT[k, h]], [bps])
                sq, bsq = self.tb()
                self.ACT(sq[:], ps[:], AF.Square, [bps], [bsq], scale=1.0 / 16.0)
                self.MM(pst[:], ones, sq[:], m == 0, m == 1, [self.bCB, bsq], [bpst])
                raws.append((ps, bps))
            r, br = self.rstd_from(pst, bpst)
            for m in range(2):
                ps, bps = raws[m]
                t, bt = self.tf()
                self.STT(t[:], ps[:], self.col(l, self.C_KVG + m), r[:], ALU.mult, ALU.mult, [bps, br, self.bCOLS], [bt])
                self.ACT(self.CKVN[:, m, ts], t[:], AF.Copy, [bt], [self.bCKVN[m, h]])
                if h == 0:
                    self.outs.append(self.DMA(self.o_ckv[l, m * 128:(m + 1) * 128, :], t[:], [bt], []))
            ps, bps = self.psn("a")
            for k in range(8):
                self.MM(ps[0:32, :], sc[:, k, 256:288], self.HT[:, k, ts], k == 0, k == 7, [bsc, self.bHT[k, h]], [bps])
            t, bt = self.tf()
            self.ACT(t[0:32, :], ps[0:32, :], AF.Copy, [bps], [bt])
            self.CP(self.KPE[0:32, ts], t[0:32, :], [bt], [self.bKPE[h]])
            if h == 0:
                self.outs.append(self.DMA(self.o_kpe[l], t[0:32, :], [bt], []))
        suq, bsuq = self.wnext(f"uq{l}")
        kmx = self.kmx_in[l].ap().rearrange("(h p) t -> p h t", p=96)
        kmg = self.COLS[0:96, l, self.C_KMG:self.C_KMG + 1]
        qmg = self.COLS[0:96, l, self.C_QMG:self.C_QMG + 1]
        ones96 = ones[0:96, 0:96]
        units = []
        for hd in range(8):
            for h in range(2):
                ts = slice(h * HALF, (h + 1) * HALF)

                def projq(hd=hd, h=h, ts=ts):
                    ps, bps = self.psn("d")
                    for k in range(3):
                        self.MM(ps[0:96, :], suq[:, k, hd * 96:(hd + 1) * 96], self.CQN[:, k, ts], k == 0, k == 2,
                                [bsuq, self.bCQN[k, h]], [bps])
                    return ps, bps

                def postq(norm, hd=hd, h=h, ts=ts):
                    if h == 0:
                        norm(self.QMT[0:96, hd, ts], [self.bQMT[hd, 0]])
                    else:
                        tb_, btb = self.tb()
                        norm(tb_[0:96, :], [btb])
                        self.rope_from(tb_, btb, 96, Pm, self.ROPE[0:96, 2, :], self.ROPE[0:96, 3, :], self.QMT[0:96, hd, ts], [self.bQMT[hd, 1]])
                units.append(dict(proj=projq, npart=96, ones=ones96, isn=96.0 ** -0.5, gcol=qmg, bgs=[self.bCOLS], post=postq))
        for hd in range(8):
            for seg in range(3):
                n = HALF if seg < 2 else 256

                def projk(hd=hd, seg=seg, n=n):
                    ps, bps = self.psn("d")
                    for k in range(2):
                        if seg < 2:
                            rhs, br = self.CKVN[:, k, seg * HALF:(seg + 1) * HALF], self.bCKVN[k, seg]
                        else:
                            rhs, br = self.CC[:, k, :], self.bCC
                        self.MM(ps[0:96, 0:n], self.WK[:, k, hd, :], rhs, k == 0, False, [self.bWK, br], [bps])
                    if seg < 2:
                        rhs, br = self.KPE[0:32, seg * HALF:(seg + 1) * HALF], self.bKPE[seg]
                    else:
                        rhs, br = self.CKPE[0:32, :], self.bCKPE
                    self.MM(ps[0:96, 0:n], EXT, rhs, False, True, [self.bCB, br], [bps])
                    return ps, bps

                def postk(norm, hd=hd, seg=seg):
                    if seg == 0:
                        norm(self.KMTP[0:96, hd, :], [self.bKMTP[hd]])
                    elif seg == 1:
                        tb_, btb = self.tb()
                        norm(tb_[0:96, :], [btb])
                        st, bst = self.tb()
                        self.rope_from(tb_, btb, 96, Pm, self.ROPE[0:96, 2, :], self.ROPE[0:96, 3, :], st[0:96, :], [bst])
                        self.DMA(kmx[:, hd, :], st[0:96, :], [bst], [self.b_x["kmx"][l]])
                    else:
                        norm(self.KMTC[0:96, hd, :], [self.bKMTC[hd]])
                units.append(dict(proj=projk, npart=96, ones=ones96, isn=96.0 ** -0.5, gcol=kmg, bgs=[self.bCOLS], n=n, post=postk))
        self.bn_run(units)
        vmx = self.vmx_in[l].ap().rearrange("(n p) e -> p n e", p=128)
        for tt in range(10):
            ps, bps = self.psn("d")
            for k in range(2):
                if tt < 8:
                    lhsT, br = self.CKVN[:, k, tt * 128:(tt + 1) * 128], self.bCKVN[k, tt // 4]
                else:
                    lhsT, br = self.CC[:, k, (tt - 8) * 128:(tt - 7) * 128], self.bCC
                self.MM(ps[:], lhsT, self.WV[:, k, :], k == 0, k == 1, [self.bWV, br], [bps])
            if tt < 4:
                self.ACT(self.VMP[:, tt, :], ps[:], AF.Copy, [bps], [self.bVMP[tt]])
            elif tt < 8:
                st, bst = self.tb()
                self.ACT(st[:], ps[:], AF.Copy, [bps], [bst])
                self.DMA(vmx[:, tt - 4, :], st[:], [bst], [self.b_x["vmx"][l]])
            else:
                self.ACT(self.VMC[:, tt - 8, :], ps[:], AF.Copy, [bps], [self.bVMC[tt - 8]])
        self.allgather("kmx", l)
        self.allgather("vmx", l)

    def attn_unit(self, qap, bq, keytiles, scale, e, po, bpo, pz, bpz, nq):
        nt = len(keytiles)
        for i, (kT, bk, v, bv) in enumerate(keytiles):
            psc, bpsc = self.psn("a")
            self.MM(psc[:, 0:nq], kT, qap, True, True, [bk, bq], [bpsc])
            pt, bpt = self.tb()
            self.ACT(pt[:, 0:nq], psc[:, 0:nq], AF.Exp, [bpsc], [bpt], scale=scale)
            self.MM(po, v, pt[:, 0:nq], i == 0, i == nt - 1, [bv, bpt], [bpo])
            self.MM(pz, self.CB[:, 1, 0:e], pt[:, 0:nq], i == 0, i == nt - 1, [self.bCB, bpt], [bpz])

    def diff_fin(self, l, j, h, q0, nq):
        P = self.PS
        b = self.b_PS
        DA, bDA = self.DA, self.b_DA
        r0, br0 = self.tf()
        self.ACT(r0[:, 0:nq], P[5][:, 0:nq], AF.Ln, [b[5]], [br0])
        self.ACT(r0[:, 0:nq], r0[:, 0:nq], AF.Exp, [br0], [br0], scale=-1.0)
        self.TT(DA[0][:, 0:nq], P[4][:, 0:nq], r0[:, 0:nq], ALU.mult, [b[4], br0], [bDA[0]])
        r1, br1 = self.tf()
        self.ACT(r1[:, 0:nq], P[7][:, 0:nq], AF.Ln, [b[7]], [br1])
        self.ACT(r1[:, 0:nq], r1[:, 0:nq], AF.Exp, [br1], [br1], scale=-1.0)
        self.TT(DA[1][:, 0:nq], P[6][:, 0:nq], r1[:, 0:nq], ALU.mult, [b[6], br1], [bDA[1]])

        def tail():
            o = DA[0]
            self.STT(o[:, 0:nq], DA[1][:, 0:nq], self.LAM_[:, l, 5:6], DA[0][:, 0:nq], ALU.mult, ALU.add,
                     [bDA[1], bDA[0], self.bLAMg[l]], [bDA[0]])
            sq, bsq = self.tb()
            self.ACT(sq[:, 0:nq], o[:, 0:nq], AF.Square, [bDA[0]], [bsq], scale=128.0 ** -0.5)
            pst, bpst = self.psn("a")
            self.MM(pst[:, 0:nq], self.ones(), sq[:, 0:nq], True, True, [self.bCB, bsq], [bpst])
            r, br = self.rstd_from(pst, bpst, 128, nq)
            self.STT(self.ODT[:, j, q0:q0 + nq], o[:, 0:nq], self.LAM_[:, l, 6:7], r[:, 0:nq], ALU.mult, ALU.mult,
                     [bDA[0], br, self.bLAMg[l]], [self.bODT[j, h]])
        return tail

    def mla_fin(self, hd, h, q0, nq, ba=None):
        ba0 = ba

        def tail():
            P = self.PS
            b = self.b_PS
            ba = (4 + 2 * (hd % 2)) if ba0 is None else ba0
            pr = slice((hd % 2) * 64, (hd % 2) * 64 + 64)
            r0, br0 = self.tf()
            self.ACT(r0[pr, 0:nq], P[ba + 1][pr, 0:nq], AF.Ln, [b[ba + 1]], [br0])
            self.ACT(r0[pr, 0:nq], r0[pr, 0:nq], AF.Exp, [br0], [br0], scale=-1.0)
            self.TT(self.OMT[pr, hd // 2, q0:q0 + nq], P[ba][pr, 0:nq], r0[pr, 0:nq], ALU.mult, [b[ba], br0], [self.bOMT[hd, h]])
        return tail

    def attn_stream(self, units, defer, bg=None):
        pending = []
        for u in units:
            tiles = u["tiles"]
            nt = len(tiles)
            nq, e = u["nq"], u["e"]
            if u.get("pre") is not None:
                u["pre"]()
            wbanks = set()
            for L in (u.get("lanes") or [u]):
                wbanks.add(L["bpo"])
                wbanks.add(L["bpz"])
            for p in list(pending):
                if any(bb in wbanks for bb in p[2]):
                    pending.remove(p)
                    p[1]()
            lanes = u.get("lanes") or [u]
            for i in range(nt):
                scs = []
                for L in lanes:
                    kT, bk, v, bv = L["tiles"][i]
                    psc, bpsc = self.psn(u.get("spool", "a"))
                    self.MM(psc[:, 0:nq], kT, L["q"], True, True, [bk, L["bq"]], [bpsc])
                    scs.append((psc, bpsc))
                pts = []
                for L, (psc, bpsc) in zip(lanes, scs):
                    pt, bpt = self.tb()
                    self.ACT(pt[:, 0:nq], psc[:, 0:nq], AF.Exp, [bpsc], [bpt], scale=u["scale"])
                    pts.append((pt, bpt))
                for L, (pt, bpt) in zip(lanes, pts):
                    kT, bk, v, bv = L["tiles"][i]
                    self.MM(L["po"], v, pt[:, 0:nq], i == 0, i == nt - 1, [bv, bpt], [L["bpo"]])
                    self.MM(L["pz"], self.CB[:, 1, 0:e], pt[:, 0:nq], i == 0, i == nt - 1, [self.bCB, bpt], [L["bpz"]])
                for p in pending:
                    p[0] -= 1
                while pending and pending[0][0] <= 0:
                    pending.pop(0)[1]()
                if bg:
                    bg.pop(0)()
            if u.get("fin") is not None:
                t = u["fin"]()
                if t is not None:
                    pending.append([defer, t, u.get("fin_banks", ())])
        for p in pending:
            p[1]()
        while bg:
            bg.pop(0)()

    def stage_attn_P(self, l):
        P, b = self.PS, self.b_PS
        units = []
        for s in range(2):
            q0 = s * 256
            for j in range(4):
                for c in range(2):
                    pr = slice(c * 64, (c + 1) * 64)
                    kts = [(self.KTP[pr, j, q0 + i * 128:q0 + (i + 1) * 128], self.bKTP[j],
                            self.VP[:, 2 * s + i, j * 128:(j + 1) * 128], self.bVP[2 * s + i]) for i in range(2)]
                    units.append(dict(tiles=kts, q=self.QT[pr, j, q0:q0 + 256], bq=self.bQT[j, 0], scale=0.125, e=128,
                                      po=P[4 + 2 * c][:, 0:256], bpo=b[4 + 2 * c], pz=P[5 + 2 * c][:, 0:256], bpz=b[5 + 2 * c], nq=256,
                                      fin=(lambda j=j, q0=q0: self.diff_fin(l, j, 0, q0, 256)) if c == 1 else None))
            for hd in range(8):
                kts = [(self.KMTP[0:96, hd, q0 + i * 128:q0 + (i + 1) * 128], self.bKMTP[hd],
                        self.VMP[:, 2 * s + i, hd * 64:(hd + 1) * 64], self.bVMP[2 * s + i]) for i in range(2)]
                pr = slice((hd % 2) * 64, (hd % 2) * 64 + 64)
                ba = 4 + 2 * (hd % 2)
                units.append(dict(tiles=kts, q=self.QMT[0:96, hd, q0:q0 + 256], bq=self.bQMT[hd, 0], scale=96.0 ** -0.5, e=64,
                                  po=P[ba][pr, 0:256], bpo=b[ba], pz=P[ba + 1][pr, 0:256], bpz=b[ba + 1], nq=256,
                                  fin=(lambda hd=hd, q0=q0: self.mla_fin(hd, 0, q0, 256)), fin_banks=(b[ba], b[ba + 1])))
        self.attn_stream(units, 3)

    def stage_attn_S(self, l):
        P, b = self.PS, self.b_PS
        q0 = HALF
        units = []
        slot = [0]

        import os
        PAIR = os.environ.get("KPAIR", "0") == "1"
        for j in range(4):
            s = slot[0] % 2
            slot[0] += 1
            lanes = []
            for c in range(2):
                pr = slice(c * 64, (c + 1) * 64)
                kts = [(self.KC[pr, j, i * 128:(i + 1) * 128], self.bKC, self.VC[:, i, j * 128:(j + 1) * 128], self.bVC) for i in range(2)]
                kts += [(self.KS[s][pr, i * 128:(i + 1) * 128], self.bKS[s], self.VS[s][:, i, :], self.bVS[s]) for i in range(16)]
                lanes.append(dict(tiles=kts, q=self.QT[pr, j, q0:q0 + HALF], bq=self.bQT[j, 1], scale=0.125, e=128,
                                  po=P[4 + 2 * c][:, :], bpo=b[4 + 2 * c], pz=P[5 + 2 * c][:, :], bpz=b[5 + 2 * c], nq=HALF))
            if PAIR:
                u = dict(lanes[0])
                u["lanes"] = lanes
                u["fin"] = (lambda j=j: self.diff_fin(l, j, 1, q0, HALF))
                u["load"] = ("d", j, s)
                units.append(u)
            else:
                lanes[0]["load"] = ("d", j, s)
                lanes[0]["fin"] = None
                lanes[1]["fin"] = (lambda j=j: self.diff_fin(l, j, 1, q0, HALF))
                units += lanes
        for hd in range(8):
            s = slot[0] % 2
            slot[0] += 1
            pr = slice((hd % 2) * 64, (hd % 2) * 64 + 64)
            ba = 4
            kts = [(self.KMTC[0:96, hd, i * 128:(i + 1) * 128], self.bKMTC[hd], self.VMC[:, i, hd * 64:(hd + 1) * 64], self.bVMC[i]) for i in range(2)]
            kts += [(self.KS[s][0:96, i * 128:(i + 1) * 128], self.bKS[s], self.VS[s][:, i, 0:64], self.bVS[s]) for i in range(16)]
            units.append(dict(tiles=kts, q=self.QMT[0:96, hd, q0:q0 + HALF], bq=self.bQMT[hd, 1], scale=96.0 ** -0.5, e=64,
                              po=P[ba][pr, :], bpo=b[ba], pz=P[ba + 1][pr, :], bpz=b[ba + 1], nq=HALF,
                              fin=(lambda hd=hd: self.mla_fin(hd, 1, q0, HALF, ba=4)), fin_banks=(b[ba], b[ba + 1]), load=("m", hd, s),
                              spool="a6" if hd > 0 else "a"))
        kxo = self.kx_out[l].ap().rearrange("(r f) t -> f r t", r=4)
        vxo = self.vx_out[l].ap().rearrange("(n p) e -> p n e", p=128)
        kmo = self.kmx_out[l].ap().rearrange("(r f) t -> f r t", r=4)
        vmo = self.vmx_out[l].ap().rearrange("(n p) e -> p n e", p=128)

        def do_load(ld):
            kind, idx, s = ld
            if kind == "d":
                self.DMA(self.KS[s].rearrange("p (r t) -> p r t", r=4), kxo[idx * 128:(idx + 1) * 128], [self.b_xo["kx"][l]], [self.bKS[s]])
                self.DMA(self.VS[s], vxo[:, :, idx * 128:(idx + 1) * 128], [self.b_xo["vx"][l]], [self.bVS[s]])
            else:
                self.DMA(self.KS[s][0:96, :].rearrange("p (r t) -> p r t", r=4), kmo[idx * 96:(idx + 1) * 96], [self.b_xo["kmx"][l]], [self.bKS[s]])
                self.DMA(self.VS[s][:, :, 0:64], vmo[:, :, idx * 64:(idx + 1) * 64], [self.b_xo["vmx"][l]], [self.bVS[s]])
        for u in units:
            if u.get("load") is not None:
                u["pre"] = (lambda ld=u["load"]: do_load(ld))
        bg = self.conv_bg(l)
        step = len(bg) // 9
        for n_, j in enumerate(range(4, 12)):
            bg.insert((n_ + 1) * step + n_, (lambda j=j: self.mod_block(l, j, pool="a")))
        self.attn_stream(units, 4, bg=bg)

    def conv_bg(self, l):
        ops = []
        hbo = self.hb_out[l].ap().rearrange("(r j p) c -> p j r c", r=4, p=128)

        def halo():
            for j in range(4):
                self.DMA(self.HL[:, j], hbo[:, j], [self.b_xo["hb"][l]], [self.bHL])
            for side, c0, dst0 in ((0, 17, 0), (1, 0, 527)):
                for r in range(4):
                    if r == 0:
                        self.TS(self.HLF[:, :, side, 0:15], self.HL[:, :, r, c0:c0 + 15], self.SEL[:, side * 4 + r:side * 4 + r + 1], None,
                                ALU.mult, None, [self.bHL, self.bSEL], [self.bHLF])
                    else:
                        self.STT(self.HLF[:, :, side, 0:15], self.HL[:, :, r, c0:c0 + 15], self.SEL[:, side * 4 + r:side * 4 + r + 1],
                                 self.HLF[:, :, side, 0:15], ALU.mult, ALU.add, [self.bHL, self.bSEL, self.bHLF], [self.bHLF])
                self.CP(self.US[:, :, dst0:dst0 + 15], self.HLF[:, :, side, 0:15], [self.bHLF], [self.bUS[j] for j in range(4)])
        ops.append(halo)
        accP = self.CACC[:, 0:HALF].rearrange("p (s t) -> p s t", s=2)
        accS = self.CACC[:, HALF:T]
        for j in range(4):
            for kk in range(31):
                dwc = self.col(l, self.C_DW + j * 31 + kk)

                def tapP(j=j, kk=kk, dwc=dwc):
                    src = self.UP[:, j, :, kk:kk + 256]
                    if kk == 0:
                        self.TS(accP, src, dwc, None, ALU.mult, None, [self.bUP[j], self.bCOLS], [self.bACC[0]])
                    else:
                        self.STT(accP, src, dwc, accP, ALU.mult, ALU.add, [self.bUP[j], self.bCOLS, self.bACC[0]], [self.bACC[0]])

                def tapS(j=j, kk=kk, dwc=dwc):
                    src = self.US[:, j, kk:kk + HALF]
                    if kk == 0:
                        self.TS(accS, src, dwc, None, ALU.mult, None, [self.bUS[j], self.bCOLS], [self.bACC[1]])
                    else:
                        self.STT(accS, src, dwc, accS, ALU.mult, ALU.add, [self.bUS[j], self.bCOLS, self.bACC[1]], [self.bACC[1]])
                ops.append(tapP)
                ops.append(tapS)

            def fin(j=j):
                cb = self.col(l, self.C_CB + j)
                self.TS(self.UP[:, j, :, 15:271], accP, cb, None, ALU.add, None, [self.bACC[0], self.bCOLS], [self.bUP[j]])
                self.TS(self.US[:, j, 15:527], accS, cb, None, ALU.add, None, [self.bACC[1], self.bCOLS], [self.bUS[j]])
            ops.append(fin)
        return ops

    def stage_conv(self, l):
        P, b = self.PS, self.b_PS
        stats = [(P[6], b[6]), (P[7], b[7])]

        def ysrc(j, h):
            if h == 0:
                return self.UP[:, j, :, 15:271], self.bUP[j]
            return self.US[:, j, 15:527], self.bUS[j]
        for j in range(4):
            for h in range(2):
                y, by = ysrc(j, h)
                sq, bsq = self.tb()
                sqv = sq[:].rearrange("p (s t) -> p s t", s=2) if h == 0 else sq[:]
                self.ACT(sqv, y, AF.Square, [by], [bsq], scale=512.0 ** -0.5)
                pst, bpst = stats[h]
                self.MM(pst[:], self.ones(), sq[:], j == 0, j == 3, [self.bCB, bsq], [bpst])
        for h in range(2):
            ts = slice(h * HALF, (h + 1) * HALF)
            r, br = self.rstd_from(stats[h][0], stats[h][1])
            for j in range(4):
                y, by = ysrc(j, h)
                z, bz = self.tf()
                zv = z[:].rearrange("p (s t) -> p s t", s=2) if h == 0 else z[:]
                rv = r[:].rearrange("p (s t) -> p s t", s=2) if h == 0 else r[:]
                self.STT(zv, y, self.col(l, self.C_CNG + j), rv, ALU.mult, ALU.mult, [by, self.bCOLS, br], [bz])
                self.ACT(self.ST[:, j, ts], z[:], AF.Silu, [bz], [self.bST[j, h]])

    def stage_merge(self, l):
        self.DMA(self.WCO, self.w_conv_out[l].rearrange("(k p) c -> p k c", p=128), (), [self.bWCO], q="pool")
        self.DMA(self.WDO, self.w_diff_out[l].rearrange("(k p) c -> p k c", p=128), (), [self.bWDO], q="pool")
        self.DMA(self.WMO, self.w_mla_out[l].rearrange("(k p) c -> p k c", p=128), (), [self.bWMO], q="pool")
        branches = ((self.WCO, self.bWCO, self.ST, self.bST), (self.WDO, self.bWDO, self.ODT, self.bODT),
                    (self.WMO, self.bWMO, self.OMT, self.bOMT))
        for j in range(8):
            sg, bsg = self.wnext(f"mg{l}_{j}")
            for h in range(2):
                ts = slice(h * HALF, (h + 1) * HALF)
                acc = None
                for bi, (wt, bwt, src, bsrc) in enumerate(branches):
                    pg, bpg = self.psn("d")
                    for k in range(8):
                        self.MM(pg[:], sg[:, k, bi, :], self.HT[:, k, ts], k == 0, k == 7, [bsg, self.bHT[k, h]], [bpg])
                    po, bpo = self.psn("d")
                    for k in range(4):
                        if bi == 2:
                            rb = [self.bOMT[2 * k, h], self.bOMT[2 * k + 1, h]]
                        else:
                            rb = [bsrc[k, h]]
                        self.MM(po[:], wt[:, k, j * 128:(j + 1) * 128], src[:, k, ts], k == 0, k == 3, [bwt] + rb, [bpo])
                    sgm, bsgm = self.tf()
                    self.ACT(sgm[:], pg[:], AF.Sigmoid, [bpg], [bsgm])
                    if bi == 0:
                        acc, bacc = self.tf()
                        self.TT(acc[:], po[:], sgm[:], ALU.mult, [bpo, bsgm], [bacc])
                    elif bi == 1:
                        t1, bt1 = self.tf()
                        self.TT(t1[:], po[:], sgm[:], ALU.mult, [bpo, bsgm], [bt1])
                        self.TT(acc[:], acc[:], t1[:], ALU.add, [bacc, bt1], [bacc])
                    else:
                        t1, bt1 = self.tf()
                        self.TT(t1[:], po[:], sgm[:], ALU.mult, [bpo, bsgm], [bt1])
                        self.TT(self.MT[:, j, ts], acc[:], t1[:], ALU.add, [bacc, bt1], [self.bMT[j, h]])
        for blk in range(2):
            so, bso = self.wnext(f"wo{l}_{blk}")
            for jj in range(4):
                j = blk * 4 + jj
                for h in range(2):
                    ts = slice(h * HALF, (h + 1) * HALF)
                    po, bpo = self.proj(so, bso, jj * 128, 128, h, src=self.MT, bsrc=self.bMT)
                    self.STT(self.X[:, j, ts], po[:], self.modcol(l, 2, j, h), self.X[:, j, ts], ALU.mult, ALU.add,
                             [bpo, self.bX[j, h]] + self.bmod(l, 2), [self.bX[j, h]])

    def stage_mlp(self, l):
        for q in range(4):
            for blk in range(2):
                su, bsu = self.wnext(f"up{l}_{q}_{blk}")
                for jj in range(4):
                    jq = blk * 4 + jj
                    for h in range(2):
                        ts = slice(h * HALF, (h + 1) * HALF)
                        pu, bpu = self.proj(su, bsu, jj * 128, 128, h)
                        t, bt = self.tf()
                        self.ACT(t[:], pu[:], AF.Relu, [bpu], [bt])
                        self.TT(self.AT[:, jq, ts], t[:], t[:], ALU.mult, [bt], [self.bAT[jq, h]])
            for blk in range(2):
                sd, bsd = self.wnext(f"dn{l}_{q}_{blk}")
                for jj in range(4):
                    j = blk * 4 + jj
                    for h in range(2):
                        ts = slice(h * HALF, (h + 1) * HALF)
                        pd, bpd = self.proj(sd, bsd, jj * 128, 128, h, src=self.AT, bsrc=self.bAT)
                        self.STT(self.X[:, j, ts], pd[:], self.modcol(l, 5, j, h), self.X[:, j, ts], ALU.mult, ALU.add,
                                 [bpd, self.bX[j, h]] + self.bmod(l, 5), [self.bX[j, h]])
            if l + 1 < DEPTH and q < 2:
                self.mod_block(l + 1, 2 * q)
                self.mod_block(l + 1, 2 * q + 1)

    def epilogue(self):
        fin = []
        for k in range(8):
            fin.append(self.DMA(self.yT[k * 128:(k + 1) * 128, :], self.X[:, k, :], [self.bX[k, 0], self.bX[k, 1]], []))
        return fin


def _rope_tables(rank):
    t = np.arange(rank * 512, rank * 512 + 512)
    pos = [(t // 64).astype(np.float64), (t % 64).astype(np.float64)]
    out = np.zeros((128, 4, 512), np.float64)
    inv = 10000.0 ** (-np.arange(0, 32, 2, dtype=np.float64) / 32.0)
    for c in range(2):
        for a in range(2):
            for p in range(2):
                for i in range(16):
                    d = c * 64 + a * 32 + p * 16 + i
                    ang = pos[a] * inv[i]
                    out[d, 0] = np.cos(ang)
                    out[d, 1] = np.sin(ang) * (-1.0 if p == 0 else 1.0)
    inv = 10000.0 ** (-np.arange(0, 16, 2, dtype=np.float64) / 16.0)
    out[0:64, 2] = 1.0
    for a in range(2):
        for p in range(2):
            for i in range(8):
                d = 64 + a * 16 + p * 8 + i
                ang = pos[a] * inv[i]
                out[d, 2] = np.cos(ang)
                out[d, 3] = np.sin(ang) * (-1.0 if p == 0 else 1.0)
    return out.astype(np.float32)


def _const_mats():
    m = np.zeros((128, 6, 128), np.float32)
    m[:, 0, :] = np.eye(128)
    m[:, 1, :] = 1.0
    m[0:64, 2, 0:64] = 1.0
    m[64:128, 2, 64:128] = 1.0
    for c in range(2):
        for a in range(2):
            for p in range(2):
                for i in range(16):
                    d = c * 64 + a * 32 + p * 16 + i
                    ds = c * 64 + a * 32 + (1 - p) * 16 + i
                    m[ds, 3, d] = 1.0
    for d in range(64):
        m[d, 4, d] = 1.0
    for a in range(2):
        for p in range(2):
            for i in range(8):
                d = 64 + a * 16 + p * 8 + i
                ds = 64 + a * 16 + (1 - p) * 8 + i
                m[ds, 4, d] = 1.0
    for i in range(32):
        m[i, 5, 64 + i] = 1.0
    return m


def _pack_cols(inp):
    cols = np.zeros((128, DEPTH, NCOLS), np.float32)

    def colform(v, nchunk):
        return np.ascontiguousarray(v.reshape(nchunk, 128).T)
    for l in range(DEPTH):
        cols[:, l, 0:96] = np.repeat(colform(inp["mod_b"][l], 48)[:, :, None], 2, axis=2).reshape(128, 96)
        cols[:, l, 96:112] = np.repeat(colform(inp["norm1_g"][l], 8)[:, :, None], 2, axis=2).reshape(128, 16)
        cols[:, l, 112:128] = np.repeat(colform(inp["norm2_g"][l], 8)[:, :, None], 2, axis=2).reshape(128, 16)
        dw = inp["conv_dw"][l]
        cols[:, l, 128:252] = dw.T.reshape(4, 128, 31).transpose(1, 0, 2).reshape(128, 124)
        cols[:, l, 252:256] = colform(inp["conv_b"][l], 4)
        cols[:, l, 256:260] = colform(inp["conv_norm_g"][l], 4)
        cols[:, l, 260] = np.tile(inp["diff_q_norm"][l], 2)
        cols[:, l, 261] = np.tile(inp["diff_k_norm"][l], 2)
        cols[:, l, 262] = inp["diff_subln"][l]
        cols[:, l, 263:266] = colform(inp["mla_q_a_norm"][l], 3)
        cols[:, l, 266:268] = colform(inp["mla_kv_a_norm"][l], 2)
        cols[0:96, l, 268] = inp["mla_q_norm"][l]
        cols[0:96, l, 269] = inp["mla_k_norm"][l]
        cols[:, l, 270:526] = np.broadcast_to(inp["diff_lambda"][l].reshape(1, 256), (128, 256))
    return cols


_NC_CACHE = {}


def get_nc(dbg=()):
    key = tuple(dbg)
    if key not in _NC_CACHE:
        kb = KB(dbg)
        kb.build()
        _NC_CACHE[key] = kb
    return _NC_CACHE[key]


def make_in_maps(inp):
    inp = {k: np.asarray(v) for k, v in inp.items()}
    cols = _pack_cols(inp)
    constm = _const_mats()
    shared = {k: np.ascontiguousarray(inp[k], dtype=np.float32) for k in
              ("mod_w", "w_in", "w_conv_out", "w_diff_out", "w_uq", "w_ukv", "w_mla_out", "w_out", "w_up", "w_down")}
    maps = []
    for c in range(8):
        g, r = c // 4, c % 4
        xp = inp["x_prompt"][2 * c:2 * c + 2].reshape(512, D)
        xs = inp["x_sample"][g, r * 512:(r + 1) * 512]
        xT = np.ascontiguousarray(np.concatenate([xp, xs], 0).T)
        cond = np.stack([inp["c_ctx"].reshape(8, 128).T, inp["c"][g].reshape(8, 128).T], axis=2)
        sel = np.zeros((128, 8), np.float32)
        if r > 0:
            sel[:, r - 1] = 1.0
        if r < 3:
            sel[:, 4 + r + 1] = 1.0
        m = dict(shared)
        m.update({
            "xT": xT, "cond": np.ascontiguousarray(cond, dtype=np.float32), "cols": cols, "constm": constm,
            "rope": _rope_tables(r), "sel": sel,
            "ckT": np.ascontiguousarray(inp["cache_diff_k"][g].reshape(DEPTH, 256, 512).transpose(0, 2, 1)),
            "cv": np.ascontiguousarray(inp["cache_diff_v"][g].reshape(DEPTH, 256, 512)),
            "cckvT": np.ascontiguousarray(inp["cache_mla_ckv"][g].transpose(0, 2, 1)),
            "ckpeT": np.ascontiguousarray(inp["cache_mla_kpe"][g].transpose(0, 2, 1)),
        })
        maps.append(m)
    return maps


def assemble(results):
    y_prompt = np.zeros((16, 256, D), np.float32)
    y_sample = np.zeros((2, 2048, D), np.float32)
    ndk = np.zeros((16, DEPTH, 256, 4, 128), np.float32)
    ndv = np.zeros((16, DEPTH, 256, 4, 128), np.float32)
    nckv = np.zeros((16, DEPTH, 256, 256), np.float32)
    nkpe = np.zeros((16, DEPTH, 256, 32), np.float32)
    for c in range(8):
        g, r = c // 4, c % 4
        res = results[c]
        yT = res["yT"]
        y_prompt[2 * c:2 * c + 2] = yT[:, 0:512].T.reshape(2, 256, D)
        y_sample[g, r * 512:(r + 1) * 512] = yT[:, 512:].T
        for l in range(DEPTH):
            ndk[2 * c:2 * c + 2, l] = res["o_dk"][l].T.reshape(2, 256, 4, 128)
            ndv[2 * c:2 * c + 2, l] = res["o_dv"][l].reshape(2, 256, 4, 128)
            nckv[2 * c:2 * c + 2, l] = res["o_ckv"][l].T.reshape(2, 256, 256)
            nkpe[2 * c:2 * c + 2, l] = res["o_kpe"][l].T.reshape(2, 256, 32)
    return (y_prompt, y_sample, ndk, ndv, nckv, nkpe)


def kernel(**inputs):
    kb = get_nc()
    maps = make_in_maps(inputs)
    res = run_bass_kernel_spmd(kb.nc, maps, core_ids=list(range(8)))
    return assemble(res.results)
```

```python
import contextlib
import math
import numpy as np
import concourse.bass as bass
import concourse.mybir as mybir
from concourse.bass_utils import run_bass_kernel_spmd

F32 = mybir.dt.float32
BF16 = mybir.dt.bfloat16
AF = mybir.ActivationFunctionType
ALU = mybir.AluOpType

D = 1024
DEPTH = 2
T = 1024
HALF = 512
EPS = 1e-6
IN_COLS = 6304
NCOLS = 526
GATE0 = 3232


class Buf:
    __slots__ = ("name", "w", "r", "rd", "rpe")

    def __init__(self, name):
        self.name = name
        self.w = None
        self.r = {}
        self.rd = []
        self.rpe = []


class Op:
    __slots__ = ("eng", "kind", "fn", "deps", "gdeps", "flag", "sem", "val", "pidx", "ape", "T", "wb", "cols", "pos")

    def __init__(self, eng, kind, fn):
        self.eng = eng
        self.kind = kind
        self.fn = fn
        self.deps = []
        self.gdeps = []
        self.flag = False
        self.sem = None
        self.val = 0
        self.pidx = -1
        self.ape = -1
        self.T = None
        self.wb = ()
        self.cols = 512
        self.pos = -1


class Sched:
    ENGS = ("pe", "act", "dve", "pool", "sp")
    HND = {"pe": "tensor", "act": "scalar", "dve": "vector", "pool": "gpsimd", "sp": "sync"}
    REORDER = True

    def __init__(self, nc, n_dma_sems=14):
        self.nc = nc
        self.ops = {e: [] for e in self.ENGS}
        self.n_dma_sems = n_dma_sems
        self.npe = 0
        self.last_ape = {}

    def op(self, eng, fn, reads=(), writes=(), kind="c", cols=512):
        o = Op(eng, kind, fn)
        o.cols = cols
        is_pe = (eng == "pe" and kind == "c")
        deps = []
        for b in reads:
            if b.w is not None:
                deps.append(b.w)
        for b in writes:
            if b.w is not None:
                deps.append(b.w)
            deps.extend(b.r.values())
            deps.extend(b.rd)
            if b.rpe and not is_pe:
                o.gdeps.append(b.rpe)
        seen = set()
        ape = -1
        for d in deps:
            if d is o or id(d) in seen:
                continue
            seen.add(id(d))
            if is_pe and d.eng == "pe" and d.kind == "c":
                continue
            o.deps.append(d)
            ape = max(ape, d.pidx if (d.eng == "pe" and d.kind == "c") else d.ape)
        for g in o.gdeps:
            ape = max(ape, g[-1].pidx)
        if is_pe:
            o.pidx = self.npe
            self.npe += 1
            o.wb = tuple(writes)
        else:
            ape = max(ape, self.last_ape.get(eng, -1))
            self.last_ape[eng] = ape
        o.ape = ape
        for b in writes:
            b.w = o
            b.r = {}
            b.rd = []
            b.rpe = []
        for b in reads:
            if b.w is o:
                continue
            if is_pe:
                b.rpe.append(o)
            elif kind == "c":
                b.r[eng] = o
            else:
                b.rd.append(o)
        self.ops[eng].append(o)
        return o

    def reorder_pe(self):
        ops = self.ops["pe"]
        import os
        COST = {"c": float(os.environ.get("KCOST", "2.0")), "d": 4.0, "x": 30.0}
        fin = {}
        out = []
        D = []
        now = [0.0]
        inD = set()

        def Tof(o):
            if o.T is not None:
                return o.T
            stack = [o]
            while stack:
                x = stack[-1]
                if x.T is not None:
                    stack.pop()
                    continue
                t = 0.0
                pending = False
                for d in x.deps:
                    if d.eng == "pe" and d.kind == "c":
                        t = max(t, fin[d.pidx])
                    elif d.T is None:
                        stack.append(d)
                        pending = True
                    else:
                        t = max(t, d.T)
                if pending:
                    continue
                for g in x.gdeps:
                    t = max(t, max(fin[m.pidx] for m in g))
                x.T = t + COST[x.kind]
                stack.pop()
            return o.T

        def ready_time(y):
            t = 0.0
            for d in y.deps:
                t = max(t, Tof(d))
            return t

        def emit(y, forced=False):
            r = ready_time(y)
            now[0] = max(now[0], r) + max(0.035, y.cols / 1900.0 + 0.016)
            fin[y.pidx] = now[0]
            y.pos = len(out)
            out.append(y)

        def can_go(idx):
            x = D[idx]
            for e in D[:idx]:
                if x.ape >= e.pidx:
                    return False
                for b in x.wb:
                    if b in e.wb:
                        return False
            return True

        def drain(limit):
            while D:
                best = None
                cand = None
                for idx in range(len(D)):
                    if not can_go(idx):
                        continue
                    r = ready_time(D[idx])
                    if r <= now[0] + 0.05:
                        best = idx
                        break
                    if cand is None or r < cand[0]:
                        cand = (r, idx)
                if best is None:
                    if len(D) <= limit:
                        return
                    best = cand[1]
                emit(D.pop(best))

        CAP = int(os.environ.get("KCAP", "14")) if self.REORDER else 0
        for y in ops:
            D.append(y)
            drain(CAP)
        drain(0)
        assert len(out) == len(ops)
        self.ops["pe"] = out

    def emit(self, final_waits=()):
        nc = self.nc
        self.reorder_pe()
        for e in self.ENGS:
            for o in self.ops[e]:
                for g in o.gdeps:
                    o.deps.append(max(g, key=lambda m: m.pos))
                o.gdeps = []
        for e in self.ENGS:
            for o in self.ops[e]:
                for d in o.deps:
                    d.flag = True
        for o in final_waits:
            o.flag = True
        with contextlib.ExitStack() as es:
            esem = {e: es.enter_context(nc.semaphore("s_" + e)) for e in self.ENGS}
            dsem = {e: [es.enter_context(nc.semaphore(f"d_{e}{i}")) for i in range(self.n_dma_sems)]
                    for e in ("sp", "pool")}
            ncc = sum(1 for e in self.ENGS for o in self.ops[e] if o.kind == "x")
            csem = [es.enter_context(nc.semaphore(f"cc{i}")) for i in range(ncc)]
            ci = 0
            for e in self.ENGS:
                cnt = 0
                dcnt = [0] * self.n_dma_sems
                dlast = [None] * self.n_dma_sems
                di = 0
                for o in self.ops[e]:
                    if o.kind == "c":
                        if o.flag:
                            cnt += 1
                            o.sem = esem[e]
                            o.val = cnt
                    elif o.kind == "d":
                        k = di % self.n_dma_sems
                        di += 1
                        dcnt[k] += 16
                        o.sem = dsem[e][k]
                        o.val = dcnt[k]
                        if dlast[k] is not None:
                            o.deps.append(dlast[k])
                        dlast[k] = o
                    else:
                        o.sem = csem[ci]
                        o.val = 1
                        ci += 1
            block = es.enter_context(nc.Block())

            def make(e):
                def body(eng):
                    seen = {}
                    for o in self.ops[e]:
                        need = {}
                        for d in o.deps:
                            key = id(d.sem)
                            if seen.get(key, 0) >= d.val:
                                continue
                            if key not in need or need[key][1] < d.val:
                                need[key] = (d.sem, d.val)
                        for key, (s, v) in need.items():
                            eng.wait_ge(s, v)
                            seen[key] = v
                        ins = o.fn(eng)
                        if o.kind == "c":
                            if o.flag:
                                ins.then_inc(o.sem, 1)
                        elif o.kind == "d":
                            ins.then_inc(o.sem, 16)
                        else:
                            ins.then_inc(o.sem)
                    if e == "sp":
                        for o in final_waits:
                            if seen.get(id(o.sem), 0) < o.val:
                                eng.wait_ge(o.sem, o.val)
                                seen[id(o.sem)] = o.val
                return body

            for e in self.ENGS:
                getattr(block, self.HND[e])(make(e))


class Grid:
    def __init__(self, name):
        self.name = name
        self.d = {}

    def __getitem__(self, key):
        b = self.d.get(key)
        if b is None:
            b = self.d[key] = Buf(f"{self.name}{key}")
        return b

    def all(self):
        return list(self.d.values())


class KB:
    def __init__(self, dbg=()):
        self.dbg = set(dbg)
        self.nc = bass.Bass("TRN2", target_bir_lowering=False)
        self.S = Sched(self.nc)
        self.es = contextlib.ExitStack()
        self.outs = []

    def MM(self, out, lhsT, rhs, st, sp, R, W, **kw):
        return self.S.op("pe", lambda e: e.matmul(out, lhsT=lhsT, rhs=rhs, start=st, stop=sp, **kw), R, W, cols=int(rhs.shape[-1]))

    def ACT(self, out, in_, func, R, W, bias=None, scale=None):
        kw = {}
        if bias is not None:
            kw["bias"] = bias
        if scale is not None:
            kw["scale"] = scale
        return self.S.op("act", lambda e: e.activation(out=out, in_=in_, func=func, **kw), R, W)

    def TS(self, out, in0, s1, s2, op0, op1, R, W, eng="dve"):
        if s2 is None:
            return self.S.op(eng, lambda e: e.tensor_scalar(out=out, in0=in0, scalar1=s1, scalar2=None, op0=op0), R, W)
        return self.S.op(eng, lambda e: e.tensor_scalar(out=out, in0=in0, scalar1=s1, scalar2=s2, op0=op0, op1=op1), R, W)

    def TT(self, out, in0, in1, op, R, W, eng="dve"):
        return self.S.op(eng, lambda e: e.tensor_tensor(out=out, in0=in0, in1=in1, op=op), R, W)

    def STT(self, out, in0, scalar, in1, op0, op1, R, W, eng="dve"):
        return self.S.op(eng, lambda e: e.scalar_tensor_tensor(out=out, in0=in0, scalar=scalar, in1=in1, op0=op0, op1=op1), R, W)

    def CP(self, out, in_, R, W, eng="dve"):
        return self.S.op(eng, lambda e: e.tensor_copy(out=out, in_=in_), R, W)

    def MS(self, ap, val, W, eng="dve"):
        return self.S.op(eng, lambda e: e.memset(ap, val), (), W)

    def DMA(self, out, in_, R, W, q="sp"):
        return self.S.op(q, lambda e: e.dma_start(out=out, in_=in_), R, W, kind="d")

    def dump(self, name, ap, R, dt=F32):
        import os
        if os.environ.get("KDUMP", "") == "":
            return
        shape = list(ap.shape)
        t = self.nc.dram_tensor("dbg_" + name, shape, dt, kind="ExternalOutput").ap()
        self.outs.append(self.DMA(t, ap, R, []))
        self.dbg_names = getattr(self, "dbg_names", []) + ["dbg_" + name]

    def FENCE(self, R, W):
        sc = self.scratch
        return self.S.op("dve", lambda e: e.memset(sc[0:1, 0:1], 0.0), (), list(R) + list(W) + [self.b_scratch])

    def sb(self, name, shape, dt):
        return self.es.enter_context(self.nc.sbuf_tensor(name, shape, dt))

    def dram_in(self, name, shape, dt=F32):
        return self.nc.dram_tensor(name, list(shape), dt, kind="ExternalInput").ap()

    def dram_out(self, name, shape, dt=F32):
        return self.nc.dram_tensor(name, list(shape), dt, kind="ExternalOutput").ap()

    def tf(self):
        i = self._tfi % len(self.TF)
        self._tfi += 1
        return self.TF[i], self.b_TF[i]

    def tb(self):
        i = self._tbi % len(self.TB)
        self._tbi += 1
        return self.TB[i], self.b_TB[i]

    def psn(self, pool):
        lst = self.pspools[pool]
        i = self._psi.get(pool, 0)
        self._psi[pool] = i + 1
        b = lst[i % len(lst)]
        return self.PS[b], self.b_PS[b]

    def wnext(self, name, hold=0):
        i = self._wi
        self._wi += 1
        spec = self.wlist[i]
        assert spec[0] == name, (spec[0], name)
        nslot = len(self.WS)
        while self._wissued < min(len(self.wlist), i + nslot - hold):
            j = self._wissued
            nm, src, view = self.wlist[j]
            s = j % nslot
            if isinstance(src, list):
                for bi, sp_ in enumerate(src):
                    self.DMA(view(self.WS[s])[:, :, bi, :], sp_, (), [self.b_WS[s]], q="pool")
            else:
                self.DMA(view(self.WS[s]), src, (), [self.b_WS[s]], q="pool")
            self._wissued += 1
        s = i % nslot
        return spec[2](self.WS[s]), self.b_WS[s]

    def build(self):
        nc = self.nc
        self.xT = self.dram_in("xT", [D, T])
        self.cond = self.dram_in("cond", [128, 8, 2])
        self.cols_d = self.dram_in("cols", [128, DEPTH, NCOLS])
        self.constm = self.dram_in("constm", [128, 6, 128])
        self.rope = self.dram_in("rope", [128, 4, HALF])
        self.sel_d = self.dram_in("sel", [128, 8])
        self.ckT = self.dram_in("ckT", [DEPTH, 512, 256])
        self.cv = self.dram_in("cv", [DEPTH, 256, 512])
        self.cckvT = self.dram_in("cckvT", [DEPTH, 256, 256])
        self.ckpeT = self.dram_in("ckpeT", [DEPTH, 32, 256])
        self.mod_w = self.dram_in("mod_w", [DEPTH, D, 6 * D])
        self.w_in = self.dram_in("w_in", [DEPTH, D, IN_COLS])
        self.w_conv_out = self.dram_in("w_conv_out", [DEPTH, 512, D])
        self.w_diff_out = self.dram_in("w_diff_out", [DEPTH, 512, D])
        self.w_uq = self.dram_in("w_uq", [DEPTH, 384, 768])
        self.w_ukv = self.dram_in("w_ukv", [DEPTH, 256, 1024])
        self.w_mla_out = self.dram_in("w_mla_out", [DEPTH, 512, D])
        self.w_out = self.dram_in("w_out", [DEPTH, D, D])
        self.w_up = self.dram_in("w_up", [DEPTH, D, 4 * D])
        self.w_down = self.dram_in("w_down", [DEPTH, 4 * D, D])

        self.yT = self.dram_out("yT", [D, T])
        self.o_dk = self.dram_out("o_dk", [DEPTH, 512, HALF])
        self.o_dv = self.dram_out("o_dv", [DEPTH, HALF, 512])
        self.o_ckv = self.dram_out("o_ckv", [DEPTH, 256, HALF])
        self.o_kpe = self.dram_out("o_kpe", [DEPTH, 32, HALF])

        def dint(name, shape):
            return nc.dram_tensor(name, list(shape), BF16)
        self.kx_in = [dint(f"kx_in{l}", [512, 512]) for l in range(DEPTH)]
        self.kx_out = [dint(f"kx_out{l}", [2048, 512]) for l in range(DEPTH)]
        self.vx_in = [dint(f"vx_in{l}", [512, 512]) for l in range(DEPTH)]
        self.vx_out = [dint(f"vx_out{l}", [2048, 512]) for l in range(DEPTH)]
        self.kmx_in = [dint(f"kmx_in{l}", [768, 512]) for l in range(DEPTH)]
        self.kmx_out = [dint(f"kmx_out{l}", [3072, 512]) for l in range(DEPTH)]
        self.vmx_in = [dint(f"vmx_in{l}", [512, 512]) for l in range(DEPTH)]
        self.vmx_out = [dint(f"vmx_out{l}", [2048, 512]) for l in range(DEPTH)]
        self.hb_in = [dint(f"hb_in{l}", [512, 32]) for l in range(DEPTH)]
        self.hb_out = [dint(f"hb_out{l}", [2048, 32]) for l in range(DEPTH)]
        self.b_x = {n: [Buf(f"{n}_in{l}") for l in range(DEPTH)] for n in ("kx", "vx", "kmx", "vmx", "hb")}
        self.b_xo = {n: [Buf(f"{n}_out{l}") for l in range(DEPTH)] for n in ("kx", "vx", "kmx", "vmx", "hb")}

        sb = self.sb
        self.X = sb("X", [128, 8, T], F32)
        self.bX = Grid("X")
        self.HT = sb("HT", [128, 8, T], BF16)
        self.bHT = Grid("HT")
        self.RB = sb("RB", [128, 8192], BF16)
        RB = self.RB
        self.ODT = RB[:, 0:4096].rearrange("p (c t) -> p c t", c=4)
        self.OMT = RB[:, 4096:8192].rearrange("p (c t) -> p c t", c=4)
        self.bODT = Grid("ODT")
        self.bOMT = Grid("OMT")
        NSLOT = 3
        self.WS = [sb(f"WS{i}", [128, 4096], BF16) for i in range(NSLOT)]
        self.b_WS = [Buf(f"WS{i}") for i in range(NSLOT)]
        self.TF = [sb(f"TF{i}", [128, HALF], F32) for i in range(7)]
        self.b_TF = [Buf(f"TF{i}") for i in range(7)]
        self.TB = [sb(f"TB{i}", [128, HALF], BF16) for i in range(7)]
        self.b_TB = [Buf(f"TB{i}") for i in range(7)]
        self._tfi = self._tbi = 0
        self.CACC = sb("CACC", [128, T], F32)
        self.bACC = [Buf("ACC0"), Buf("ACC1")]
        self.DA = [sb(f"DA{i}", [128, HALF], F32) for i in range(2)]
        self.b_DA = [Buf(f"DA{i}") for i in range(2)]
        self.RS = [sb(f"RS{i}", [128, HALF], F32) for i in range(2)]
        self.b_RS = [Buf(f"RS{i}") for i in range(2)]
        self.CB = sb("CB", [128, 6, 128], BF16)
        self.bCB = Buf("CB")
        self.ROPE = sb("ROPE", [128, 4, HALF], F32)
        self.bROPE = Buf("ROPE")
        self.SEL = sb("SEL", [128, 8], F32)
        self.bSEL = Buf("SEL")
        self.COLS = sb("COLS", [128, DEPTH, NCOLS], F32)
        self.bCOLS = Buf("COLS")
        self.CONDF = sb("CONDF", [128, 8, 2], F32)
        self.CONDB = sb("CONDB", [128, 8, 2], BF16)
        self.bCOND = Buf("COND")
        self.bCONDB = Buf("CONDB")
        self.MODC_ = sb("MODC", [128, DEPTH, 48, 2], F32)
        self.bMC = Grid("MC")
        self.MODA_ = sb("MODA", [128, DEPTH, 2, 8, 2], F32)
        self.bMA = Grid("MA")
        self.LAM_ = sb("LAM", [128, DEPTH, 8], F32)
        self.bLAMg = Grid("LAM")
        self.LTMP = sb("LTMP", [128, 2, 64], F32)
        self.bLT = Buf("LTMP")
        self.scratch = sb("scr", [128, 2], F32)
        self.b_scratch = Buf("scr")
        self.WK = sb("WK", [128, 2, 8, 96], BF16)
        self.WV = sb("WV", [128, 2, 512], BF16)
        self.bWK = Buf("WK")
        self.bWV = Buf("WV")
        self.UP = sb("UP", [128, 4, 2, 286], BF16)
        self.US = sb("US", [128, 4, 542], BF16)
        self.bUP = Grid("UP")
        self.bUS = Grid("US")
        self.HL = sb("HL", [128, 4, 4, 32], BF16)
        self.bHL = Buf("HL")
        self.HLF = sb("HLF", [128, 4, 2, 16], F32)
        self.bHLF = Buf("HLF")

        o = 0
        self.CQN = RB[:, o:o + 3072].rearrange("p (c t) -> p c t", c=3); o += 3072
        self.CKVN = RB[:, o:o + 2048].rearrange("p (c t) -> p c t", c=2); o += 2048
        self.KPE = RB[:, o:o + 1024]; o += 1024
        self.CC = RB[:, o:o + 512].rearrange("p (c t) -> p c t", c=2); o += 512
        self.CKPE = RB[:, o:o + 256]; o += 256
        RA_N = 28544
        self.RA = sb("RA", [128, RA_N], BF16)
        RA = self.RA
        off = [0]

        def carve(n):
            a = off[0]
            off[0] += n
            return a
        o = carve(4096); self.QT = RA[:, o:o + 4096].rearrange("p (c t) -> p c t", c=4)
        o = carve(8192); self.QMT = RA[:, o:o + 8192].rearrange("p (c t) -> p c t", c=8)
        o = carve(2048); self.KMTC = RA[:, o:o + 2048].rearrange("p (c t) -> p c t", c=8)
        o = carve(1024); self.VMC = RA[:, o:o + 1024].rearrange("p (c t) -> p c t", c=2)
        o = carve(1024); self.KC = RA[:, o:o + 1024].rearrange("p (c t) -> p c t", c=4)
        o = carve(1024); self.VC = RA[:, o:o + 1024].rearrange("p (c t) -> p c t", c=2)
        un = carve(10240)
        o = un
        self.KTP = RA[:, o:o + 2048].rearrange("p (c t) -> p c t", c=4); o += 2048
        self.VP = RA[:, o:o + 2048].rearrange("p (c t) -> p c t", c=4); o += 2048
        self.KMTP = RA[:, o:o + 4096].rearrange("p (c t) -> p c t", c=8); o += 4096
        self.VMP = RA[:, o:o + 2048].rearrange("p (c t) -> p c t", c=4); o += 2048
        self.KS = [RA[:, un + i * 2048: un + (i + 1) * 2048] for i in range(2)]
        self.VS = [RA[:, un + 4096 + i * 2048: un + 4096 + (i + 1) * 2048].rearrange("p (n e) -> p n e", n=16) for i in range(2)]
        life1_end = off[0]
        o2 = 0
        self.ST = RA[:, o2:o2 + 4096].rearrange("p (c t) -> p c t", c=4); o2 += 4096
        self.DG = RA[:, o2:o2 + 3968].rearrange("p (k c) -> p k c", k=31); o2 += 3968
        self.MT = RA[:, o2:o2 + 8192].rearrange("p (c t) -> p c t", c=8); o2 += 8192
        self.AT = RA[:, o2:o2 + 8192].rearrange("p (c t) -> p c t", c=8)
        self.WCO = RA[:, o2:o2 + 4096].rearrange("p (k c) -> p k c", k=4); o2 += 4096
        self.WDO = RA[:, o2:o2 + 4096].rearrange("p (k c) -> p k c", k=4); o2 += 4096
        self.WMO = RA[:, o2:o2 + 4096].rearrange("p (k c) -> p k c", k=4); o2 += 4096
        assert o2 <= RA_N and life1_end <= RA_N, (o2, life1_end)
        self.bQT = Grid("QT"); self.bKTP = Grid("KTP"); self.bVP = Grid("VP"); self.bQMT = Grid("QMT")
        self.bKMTP = Grid("KMTP"); self.bVMP = Grid("VMP"); self.bKMTC = Grid("KMTC"); self.bVMC = Grid("VMC")
        self.bKC = Buf("KC"); self.bVC = Buf("VC")
        self.bKS = [Buf("KS0"), Buf("KS1")]; self.bVS = [Buf("VS0"), Buf("VS1")]
        self.bCQN = Grid("CQN"); self.bCKVN = Grid("CKVN"); self.bKPE = Grid("KPE"); self.bCC = Buf("CC"); self.bCKPE = Buf("CKPE")
        self.bST = Grid("ST"); self.bDG = Buf("DG"); self.bMT = Grid("MT"); self.bAT = Grid("AT")
        self.bWCO = Buf("WCO"); self.bWDO = Buf("WDO"); self.bWMO = Buf("WMO")
        for g, n1, n2 in ((self.bQT, 4, 2), (self.bQMT, 8, 2), (self.bODT, 4, 2), (self.bOMT, 8, 2), (self.bCQN, 3, 2), (self.bCKVN, 2, 2),
                          (self.bST, 4, 2), (self.bMT, 8, 2), (self.bAT, 8, 2)):
            for a in range(n1):
                for b_ in range(n2):
                    g[a, b_]
        for g, n1 in ((self.bKTP, 4), (self.bVP, 4), (self.bKMTP, 8), (self.bVMP, 4), (self.bKMTC, 8), (self.bVMC, 2), (self.bKPE, 2)):
            for a in range(n1):
                g[a]

        self.PS = [self.es.enter_context(nc.psum_tensor(f"ps{i}", [128, HALF], F32)) for i in range(8)]
        self.b_PS = [Buf(f"ps{i}") for i in range(8)]
        self.pspools = {"d": [0, 1, 2, 3, 4, 5], "s": [6, 7], "a": [0, 1, 2, 3], "a6": [0, 1, 2, 3, 6, 7]}
        self._psi = {}

        self.wlist = []

        def v3(nk, ncol):
            return lambda slot: slot[:, 0:nk * ncol].rearrange("p (k c) -> p k c", k=nk)
        specs = {}
        for l in range(DEPTH):
            mw = self.mod_w[l].rearrange("(k p) c -> p k c", p=128)
            wi = self.w_in[l].rearrange("(k p) c -> p k c", p=128)
            for j in range(12):
                specs[f"mod{l}_{j}"] = (mw[:, :, 512 * j:512 * (j + 1)], v3(8, 512))
            for nm, a, b in (("ua", 0, 512), ("ug", 512, 1024), ("dq", 1024, 1536), ("dk", 1536, 2048),
                             ("dv", 2048, 2560), ("cq", 2560, 2944), ("ckv", 2944, 3232)):
                specs[f"{nm}{l}"] = (wi[:, :, a:b], v3(8, b - a))
            specs[f"uq{l}"] = (self.w_uq[l].rearrange("(k p) c -> p k c", p=128), v3(3, 768))
            wg = self.w_in[l][:, GATE0:IN_COLS].rearrange("(k p) (b j c) -> p k b j c", p=128, b=3, j=8)
            for j in range(8):
                specs[f"mg{l}_{j}"] = ([wg[:, :, b_, j, :] for b_ in range(3)],
                                       lambda slot: slot[:, 0:8 * 384].rearrange("p (k b c) -> p k b c", k=8, b=3))
            wo = self.w_out[l].rearrange("(k p) c -> p k c", p=128)
            for j in range(2):
                specs[f"wo{l}_{j}"] = (wo[:, :, 512 * j:512 * (j + 1)], v3(8, 512))
            wu = self.w_up[l].rearrange("(k p) c -> p k c", p=128)
            wd = self.w_down[l].rearrange("(q k p) c -> q p k c", q=4, p=128)
            for q in range(4):
                for j in range(2):
                    specs[f"up{l}_{q}_{j}"] = (wu[:, :, 1024 * q + 512 * j: 1024 * q + 512 * (j + 1)], v3(8, 512))
                for j in range(2):
                    specs[f"dn{l}_{q}_{j}"] = (wd[q][:, :, 512 * j:512 * (j + 1)], v3(8, 512))
        order = []
        for l in range(DEPTH):
            if l == 0:
                order += [f"mod0_{j}" for j in range(4)]
            order += [f"ua{l}", f"ug{l}", f"dq{l}", f"dk{l}", f"dv{l}", f"cq{l}", f"ckv{l}", f"uq{l}"]
            order += [f"mod{l}_{j}" for j in range(4, 12)]
            order += [f"mg{l}_{j}" for j in range(8)] + [f"wo{l}_0", f"wo{l}_1"]
            for q in range(4):
                order += [f"up{l}_{q}_0", f"up{l}_{q}_1", f"dn{l}_{q}_0", f"dn{l}_{q}_1"]
                if l + 1 < DEPTH and q < 2:
                    order += [f"mod{l + 1}_{2 * q}", f"mod{l + 1}_{2 * q + 1}"]
        self.wlist = [(nm,) + specs[nm] for nm in order]
        self._wi = 0
        self._wissued = 0

        self.prologue()
        try:
            for l in range(DEPTH):
                self.layer(l)
        except StopIteration:
            pass
        finals = self.epilogue()
        self.S.emit(final_waits=finals + self.outs)
        return nc

    def prologue(self):
        self.DMA(self.CONDF[:], self.cond, (), [self.bCOND])
        self.DMA(self.COLS[:], self.cols_d, (), [self.bCOLS])
        for k in range(8):
            for h in range(2):
                self.DMA(self.X[:, k, h * HALF:(h + 1) * HALF], self.xT[k * 128:(k + 1) * 128, h * HALF:(h + 1) * HALF],
                         (), [self.bX[k, h]])
        self.DMA(self.ROPE[:], self.rope, (), [self.bROPE])
        self.DMA(self.SEL[:], self.sel_d, (), [self.bSEL])
        self.DMA(self.CB[:], self.constm, (), [self.bCB], q="pool")
        self.ACT(self.CONDB[:], self.CONDF[:], AF.Silu, [self.bCOND], [self.bCONDB])
        for j in range(4):
            self.MS(self.UP[:, j], 0.0, [self.bUP[j]], eng="pool")
        self.MS(self.WK[:], 0.0, [self.bWK], eng="pool")
        self.MS(self.HLF[:], 0.0, [self.bHLF], eng="pool")

    def col(self, l, a, n=1):
        return self.COLS[:, l, a:a + n]

    def layer(self, l):
        mla_t = self.bCQN.all() + self.bCKVN.all() + self.bKPE.all() + [self.bCC, self.bCKPE]
        ponly = self.bKTP.all() + self.bVP.all() + self.bKMTP.all() + self.bVMP.all()
        life1 = (self.bQT.all() + self.bQMT.all() + self.bKMTC.all() + self.bVMC.all() + [self.bKC, self.bVC]
                 + self.bKS + self.bVS + ponly)
        wo3 = [self.bWCO, self.bWDO, self.bWMO]
        life2 = self.bST.all() + [self.bDG] + self.bMT.all() + wo3
        obr = self.bODT.all() + self.bOMT.all()
        import os
        stop = os.environ.get("KSTOP", "")

        def chk(name):
            if stop == f"{name}{l}":
                raise StopIteration
        self.chk = chk
        self.stage_lam(l)
        if l == 0:
            for j in range(4):
                self.mod_block(0, j)
        chk("mod")
        self.stage_norm(l, 0)
        if l == 0:
            self.dump("ht", self.HT[:].rearrange("p k t -> p (k t)"), self.bHT.all(), BF16)
        chk("norm")
        self.stage_convproj(l)
        if l == 0:
            self.dump("us", self.US[:, :, 15:527], [self.bUS[j] for j in range(4)], BF16)
            self.dump("up", self.UP[:, :, :, 15:271], [self.bUP[j] for j in range(4)], BF16)
        chk("convproj")
        self.stage_diffproj(l)
        if l == 0:
            self.dump("qt", self.QT, self.bQT.all(), BF16)
            self.dump("ktp", self.KTP, self.bKTP.all(), BF16)
            self.dump("vp", self.VP, self.bVP.all(), BF16)
            self.dump("kxin", self.kx_in[l].ap(), [self.b_x["kx"][l]], BF16)
            self.dump("vxin", self.vx_in[l].ap(), [self.b_x["vx"][l]], BF16)
        chk("diffproj")
        self.stage_mlaproj(l)
        if l == 0:
            self.dump("qmt", self.QMT[0:96], self.bQMT.all(), BF16)
            self.dump("kmtp", self.KMTP[0:96], self.bKMTP.all(), BF16)
            self.dump("vmp", self.VMP, self.bVMP.all(), BF16)
            self.dump("cqn", self.CQN, self.bCQN.all(), BF16)
            self.dump("kmxin", self.kmx_in[l].ap(), [self.b_x["kmx"][l]], BF16)
            self.dump("vmxin", self.vmx_in[l].ap(), [self.b_x["vmx"][l]], BF16)
            self.dump("kmtc", self.KMTC[0:96], self.bKMTC.all(), BF16)
            self.dump("vmc", self.VMC, self.bVMC.all(), BF16)
        chk("mlaproj")
        self.FENCE(mla_t, obr)
        self.stage_attn_P(l); chk("attnP")
        self.FENCE(ponly, self.bKS + self.bVS)
        self.stage_attn_S(l)
        if l == 0:
            self.dump("odt", self.ODT, self.bODT.all(), BF16)
            self.dump("omt", self.OMT, self.bOMT.all(), BF16)
        chk("attnS")
        self.FENCE(life1, life2)
        self.stage_conv(l)
        if l == 0:
            self.dump("st", self.ST, self.bST.all(), BF16)
        chk("conv")
        self.stage_merge(l)
        if l == 0:
            self.dump("mt", self.MT, self.bMT.all(), BF16)
            self.dump("xmid", self.X[:], self.bX.all())
        chk("merge")
        self.FENCE(wo3, self.bAT.all())
        self.stage_norm(l, 1)
        self.stage_mlp(l); chk("mlp")
        if l + 1 < DEPTH:
            self.FENCE(life2 + self.bAT.all() + obr, life1 + mla_t)

    C_MODB = 0
    C_N1G = 96
    C_N2G = 112
    C_DW = 128
    C_CB = 252
    C_CNG = 256
    C_QG = 260
    C_KG = 261
    C_SLG = 262
    C_CQG = 263
    C_KVG = 266
    C_QMG = 268
    C_KMG = 269
    C_LAM = 270

    def mod_block(self, l, j, pool="s"):
        pm, bpm = self.psn(pool)
        slot, bs = self.wnext(f"mod{l}_{j}")
        for mi in range(4):
            for k in range(8):
                self.MM(pm[:, 2 * mi:2 * mi + 2], slot[:, k, mi * 128:(mi + 1) * 128], self.CONDB[:, k, :],
                        k == 0, k == 7, [bs, self.bCONDB], [bpm])
        self.TT(self.MODC_[:, l, 4 * j:4 * j + 4, :].rearrange("p m c -> p (m c)"), pm[:, 0:8],
                self.COLS[:, l, self.C_MODB + 8 * j:self.C_MODB + 8 * j + 8], ALU.add, [bpm, self.bCOLS], [self.bMC[l, j]])
        for i, (sc0, gcol, last) in enumerate(((8, self.C_N1G, 3), (32, self.C_N2G, 9))):
            if j == last:
                self.STT(self.MODA_[:, l, i].rearrange("p k c -> p (k c)"),
                         self.MODC_[:, l, sc0:sc0 + 8, :].rearrange("p k c -> p (k c)"), 1.0,
                         self.COLS[:, l, gcol:gcol + 16], ALU.add, ALU.mult, [self.bMC[l, j - 1], self.bMC[l, j], self.bCOLS], [self.bMA[l, i]])

    def stage_lam(self, l):
        LAM = self.LAM_[:, l, :]
        bL = self.bLAMg[l]
        lam_init = 0.8 - 0.6 * math.exp(-0.3 * l)
        lp = self.COLS[:, l, self.C_LAM:self.C_LAM + 256].rearrange("p (r d) -> p r d", r=4)
        self.TT(self.LTMP[:, 0, :], lp[:, 0, :], lp[:, 1, :], ALU.mult, [self.bCOLS], [self.bLT])
        self.TT(self.LTMP[:, 1, :], lp[:, 2, :], lp[:, 3, :], ALU.mult, [self.bCOLS, self.bLT], [self.bLT])
        self.S.op("dve", lambda e: e.tensor_reduce(out=LAM[:, 0:2], in_=self.LTMP[:], axis=mybir.AxisListType.X, op=ALU.add),
                  [self.bLT], [bL])
        self.ACT(LAM[:, 2:4], LAM[:, 0:2], AF.Exp, [bL], [bL])
        self.TT(LAM[:, 4:5], LAM[:, 3:4], LAM[:, 2:3], ALU.subtract, [bL], [bL])
        self.TS(LAM[:, 5:6], LAM[:, 4:5], -lam_init, None, ALU.add, None, [bL], [bL])
        self.TS(LAM[:, 6:7], self.col(l, self.C_SLG), 1.0 - lam_init, None, ALU.mult, None, [bL, self.bCOLS], [bL])

    def modcol(self, l, part, k, h):
        return self.MODC_[:, l, part * 8 + k, h:h + 1]

    def bmod(self, l, part):
        return [self.bMC[l, 2 * part], self.bMC[l, 2 * part + 1]]

    def rstd_from(self, stats_ps, bps, npart=128, n=HALF, dst=None):
        r, br = dst if dst is not None else self.tf()
        self.ACT(r[0:npart, 0:n], stats_ps[0:npart, 0:n], AF.Ln, [bps], [br], bias=EPS)
        self.ACT(r[0:npart, 0:n], r[0:npart, 0:n], AF.Exp, [br], [br], scale=-0.5)
        return r, br

    def ident(self):
        return self.CB[:, 0, :]

    def ones(self):
        return self.CB[:, 1, :]

    def stage_norm(self, l, which):
        part_sh, part_sc = (0, 1) if which == 0 else (3, 4)
        for h in range(2):
            ts = slice(h * HALF, (h + 1) * HALF)
            pst, bpst = self.psn("s")
            for k in range(8):
                sq, bsq = self.tb()
                self.ACT(sq[:], self.X[:, k, ts], AF.Square, [self.bX[k, h]], [bsq], scale=1.0 / 32.0)
                self.MM(pst[:], self.ones(), sq[:], k == 0, k == 7, [self.bCB, bsq], [bpst])
            r, br = self.rstd_from(pst, bpst, dst=(self.RS[h], self.b_RS[h]))
            for k in range(8):
                t, bt = self.tf()
                self.STT(t[:], self.X[:, k, ts], self.MODA_[:, l, which, k, h:h + 1], r[:], ALU.mult, ALU.mult,
                         [self.bX[k, h], self.bMA[l, which], br], [bt])
                self.ACT(self.HT[:, k, ts], t[:], AF.Identity, [bt] + self.bmod(l, part_sh), [self.bHT[k, h]],
                         bias=self.modcol(l, part_sh, k, h))

    def proj(self, slot, bs, c0, m, h, pool="d", nk=8, src=None, bsrc=None):
        ps, bps = self.psn(pool)
        ts = slice(h * HALF, (h + 1) * HALF)
        for k in range(nk):
            if src is None:
                rhs, br = self.HT[:, k, ts], self.bHT[k, h]
            else:
                rhs, br = src[:, k, ts], bsrc[k, h]
            self.MM(ps[0:m, :], slot[:, k, c0:c0 + m], rhs, k == 0, k == nk - 1, [bs, br], [bps])
        return ps, bps

    def stage_convproj(self, l):
        sa, bsa = self.wnext(f"ua{l}")
        sg, bsg = self.wnext(f"ug{l}", hold=1)
        for j in range(4):
            for h in range(2):
                pa, bpa = self.proj(sa, bsa, j * 128, 128, h)
                pg, bpg = self.proj(sg, bsg, j * 128, 128, h)
                sgm, bsgm = self.tf()
                self.ACT(sgm[:], pg[:], AF.Sigmoid, [bpg], [bsgm])
                if h == 0:
                    for s in range(2):
                        self.TT(self.UP[:, j, s, 15:271], pa[:, s * 256:(s + 1) * 256], sgm[:, s * 256:(s + 1) * 256], ALU.mult,
                                [bpa, bsgm], [self.bUP[j]])
                else:
                    self.TT(self.US[:, j, 15:527], pa[:], sgm[:], ALU.mult, [bpa, bsgm], [self.bUS[j]])
        hb = self.hb_in[l].ap().rearrange("(j p) c -> p j c", p=128)
        o1 = self.DMA(hb[:, :, 0:16], self.US[:, :, 15:31], [self.bUS[j] for j in range(4)], [self.b_x["hb"][l]])
        o2 = self.DMA(hb[:, :, 16:32], self.US[:, :, 511:527], [self.bUS[j] for j in range(4)], [self.b_x["hb"][l]])
        self.allgather("hb", l)

    def allgather(self, name, l):
        src = getattr(self, name + "_in")[l]
        dst = getattr(self, name + "_out")[l]
        self.S.op("pool", lambda e: e.collective_compute("AllGather", ALU.bypass, replica_groups=[[0, 1, 2, 3], [4, 5, 6, 7]],
                                                         ins=[src.ap().opt()], outs=[dst.ap().opt()]),
                  [self.b_x[name][l]], [self.b_xo[name][l]], kind="x")

    def blocknorm(self, ps, bps, npart, ones_ap, inv_sqrt_n, gcol, bgs, n=HALF):
        sq, bsq = self.tb()
        self.ACT(sq[0:npart, 0:n], ps[0:npart, 0:n], AF.Square, [bps], [bsq], scale=inv_sqrt_n)
        pst, bpst = self.psn("s")
        self.MM(pst[0:npart, 0:n], ones_ap, sq[0:npart, 0:n], True, True, [self.bCB, bsq], [bpst])
        r, br = self.rstd_from(pst, bpst, npart, n)
        t, bt = self.tf()
        self.STT(t[0:npart, 0:n], ps[0:npart, 0:n], gcol, r[0:npart, 0:n], ALU.mult, ALU.mult, [bps, br] + bgs, [bt])
        return t, bt

    def rope_from(self, tb_, btb, npart, perm_ap, ctab, stab, out_ap, bout):
        psw, bpsw = self.psn("d")
        self.MM(psw[0:npart, :], perm_ap, tb_[0:npart, :], True, True, [self.bCB, btb], [bpsw])
        t1, bt1 = self.tf()
        self.TT(t1[0:npart, :], tb_[0:npart, :], ctab, ALU.mult, [btb, self.bROPE], [bt1])
        t2, bt2 = self.tf()
        self.TT(t2[0:npart, :], psw[0:npart, :], stab, ALU.mult, [bpsw, self.bROPE], [bt2])
        self.TT(out_ap, t1[0:npart, :], t2[0:npart, :], ALU.add, [bt1, bt2], bout)

    def bn_run(self, units):
        def A(u):
            ps, bps = u["proj"]()
            npart, n = u["npart"], u.get("n", HALF)
            sq, bsq = self.tb()
            self.ACT(sq[0:npart, 0:n], ps[0:npart, 0:n], AF.Square, [bps], [bsq], scale=u["isn"])
            u["c"] = (ps, bps, sq, bsq)

        def B(u):
            ps, bps, sq, bsq = u["c"]
            npart, n = u["npart"], u.get("n", HALF)
            pst, bpst = self.psn("s")
            self.MM(pst[0:npart, 0:n], u["ones"], sq[0:npart, 0:n], True, True, [self.bCB, bsq], [bpst])
            u["r"] = self.rstd_from(pst, bpst, npart, n)

        def C(u):
            ps, bps, sq, bsq = u["c"]
            r, br = u["r"]
            npart, n = u["npart"], u.get("n", HALF)

            def norm(dst, bdst):
                self.STT(dst, ps[0:npart, 0:n], u["gcol"], r[0:npart, 0:n], ALU.mult, ALU.mult, [bps, br] + u["bgs"], bdst)
            u["post"](norm)
        nu = len(units)
        for i in range(nu + 2):
            if i < nu:
                A(units[i])
            if 0 <= i - 1 < nu:
                B(units[i - 1])
            if 0 <= i - 2 < nu:
                C(units[i - 2])

    def stage_diffproj(self, l):
        blk64 = self.CB[:, 2, :]
        Pd = self.CB[:, 3, :]
        sq_, bsq_ = self.wnext(f"dq{l}")
        sk, bsk = self.wnext(f"dk{l}", hold=1)
        kx = self.kx_in[l].ap().rearrange("(j p) t -> p j t", p=128)
        units = []
        for which in range(2):
            for j in range(4):
                for h in range(2):
                    def post(norm, which=which, j=j, h=h):
                        if which == 0:
                            if h == 0:
                                norm(self.QT[:, j, 0:HALF], [self.bQT[j, 0]])
                            else:
                                tb_, btb = self.tb()
                                norm(tb_[:], [btb])
                                self.rope_from(tb_, btb, 128, Pd, self.ROPE[:, 0, :], self.ROPE[:, 1, :], self.QT[:, j, HALF:T], [self.bQT[j, 1]])
                        else:
                            if h == 0:
                                t, bt = self.tf()
                                norm(t[:], [bt])
                                self.CP(self.KTP[:, j, :], t[:], [bt], [self.bKTP[j]])
                                self.outs.append(self.DMA(self.o_dk[l, j * 128:(j + 1) * 128, :], t[:], [bt], []))
                            else:
                                tb_, btb = self.tb()
                                norm(tb_[:], [btb])
                                st, bst = self.tb()
                                self.rope_from(tb_, btb, 128, Pd, self.ROPE[:, 0, :], self.ROPE[:, 1, :], st[:], [bst])
                                self.DMA(kx[:, j, :], st[:], [bst], [self.b_x["kx"][l]])
                    slot, bslot = (sq_, bsq_) if which == 0 else (sk, bsk)
                    units.append(dict(proj=(lambda slot=slot, bslot=bslot, j=j, h=h: self.proj(slot, bslot, j * 128, 128, h)),
                                      npart=128, ones=blk64, isn=0.125,
                                      gcol=self.col(l, self.C_QG if which == 0 else self.C_KG), bgs=[self.bCOLS], post=post))
        self.bn_run(units)
        self.chk("dq")
        self.chk("dk")
        sv, bsv = self.wnext(f"dv{l}")
        vx = self.vx_in[l].ap().rearrange("(n p) e -> p n e", p=128)
        for tt in range(8):
            ps, bps = self.psn("d")
            h = tt // 4
            for k in range(8):
                self.MM(ps[:], self.HT[:, k, tt * 128:(tt + 1) * 128], sv[:, k, :], k == 0, k == 7, [bsv, self.bHT[k, h]], [bps])
            if tt < 4:
                vf, bvf = self.tf()
                self.ACT(vf[:], ps[:], AF.Copy, [bps], [bvf])
                self.outs.append(self.DMA(self.o_dv[l, tt * 128:(tt + 1) * 128, :], vf[:], [bvf], []))
                self.CP(self.VP[:, tt, :], vf[:], [bvf], [self.bVP[tt]])
            else:
                st, bst = self.tb()
                self.ACT(st[:], ps[:], AF.Copy, [bps], [bst])
                self.DMA(vx[:, tt - 4, :], st[:], [bst], [self.b_x["vx"][l]])
        self.chk("dv")
        self.allgather("kx", l)
        self.allgather("vx", l)
        self.chk("dag")
        self.DMA(self.KC, self.ckT[l].rearrange("(j p) t -> p j t", p=128), (), [self.bKC], q="pool")
        self.DMA(self.VC, self.cv[l].rearrange("(n p) e -> p n e", p=128), (), [self.bVC], q="pool")

    def stage_mlaproj(self, l):
        ones = self.ones()
        Pm = self.CB[0:96, 4, 0:96]
        EXT = self.CB[0:32, 5, 0:96]
        wk_src = self.w_ukv[l].rearrange("(k p) (h c) -> p k h c", p=128, h=8)
        for k in range(2):
            self.DMA(self.WK[:, k, :, 0:64], wk_src[:, k, :, 0:64], (), [self.bWK], q="pool")
            self.DMA(self.WV[:, k, :].rearrange("p (h c) -> p h c", h=8), wk_src[:, k, :, 64:128], (), [self.bWV], q="pool")
        self.DMA(self.CC, self.cckvT[l].rearrange("(k p) t -> p k t", p=128), (), [self.bCC], q="pool")
        self.DMA(self.CKPE[0:32, :], self.ckpeT[l], (), [self.bCKPE], q="pool")
        scq, bscq = self.wnext(f"cq{l}")
        for h in range(2):
            ts = slice(h * HALF, (h + 1) * HALF)
            raws = []
            pst, bpst = self.psn("s")
            for m in range(3):
                ps, bps = self.psn("a")
                for k in range(8):
                    self.MM(ps[:], scq[:, k, m * 128:(m + 1) * 128], self.HT[:, k, ts], k == 0, k == 7, [bscq, self.bHT[k, h]], [bps])
                sq, bsq = self.tb()
                self.ACT(sq[:], ps[:], AF.Square, [bps], [bsq], scale=384.0 ** -0.5)
                self.MM(pst[:], ones, sq[:], m == 0, m == 2, [self.bCB, bsq], [bpst])
                raws.append((ps, bps))
            r, br = self.rstd_from(pst, bpst)
            for m in range(3):
                ps, bps = raws[m]
                self.STT(self.CQN[:, m, ts], ps[:], self.col(l, self.C_CQG + m), r[:], ALU.mult, ALU.mult,
                         [bps, br, self.bCOLS], [self.bCQN[m, h]])
        sc, bsc = self.wnext(f"ckv{l}")
        for h in range(2):
            ts = slice(h * HALF, (h + 1) * HALF)
            raws = []
            pst, bpst = self.psn("s")
            for m in range(2):
                ps, bps = self.psn("a")
                for k in range(8):
                    self.MM(ps[:], sc[:, k, m * 128:(m + 1) * 128], self.HT[:, k, ts], k == 0, k == 7, [bsc, self.bHT[k, h]], [bps])
                sq, bsq = self.tb()
                self.ACT(sq[:], ps[:], AF.Square, [bps], [bsq], scale=1.0 / 16.0)
                self.MM(pst[:], ones, sq[:], m == 0, m == 1, [self.bCB, bsq], [bpst])
                raws.append((ps, bps))
            r, br = self.rstd_from(pst, bpst)
            for m in range(2):
                ps, bps = raws[m]
                t, bt = self.tf()
                self.STT(t[:], ps[:], self.col(l, self.C_KVG + m), r[:], ALU.mult, ALU.mult, [bps, br, self.bCOLS], [bt])
                self.ACT(self.CKVN[:, m, ts], t[:], AF.Copy, [bt], [self.bCKVN[m, h]])
                if h == 0:
                    self.outs.append(self.DMA(self.o_ckv[l, m * 128:(m + 1) * 128, :], t[:], [bt], []))
            ps, bps = self.psn("a")
            for k in range(8):
                self.MM(ps[0:32, :], sc[:, k, 256:288], self.HT[:, k, ts], k == 0, k == 7, [bsc, self.bHT[k, h]], [bps])
            t, bt = self.tf()
            self.ACT(t[0:32, :], ps[0:32, :], AF.Copy, [bps], [bt])
            self.CP(self.KPE[0:32, ts], t[0:32, :], [bt], [self.bKPE[h]])
            if h == 0:
                self.outs.append(self.DMA(self.o_kpe[l], t[0:32, :], [bt], []))
        suq, bsuq = self.wnext(f"uq{l}")
        kmx = self.kmx_in[l].ap().rearrange("(h p) t -> p h t", p=96)
        kmg = self.COLS[0:96, l, self.C_KMG:self.C_KMG + 1]
        qmg = self.COLS[0:96, l, self.C_QMG:self.C_QMG + 1]
        ones96 = ones[0:96, 0:96]
        units = []
        for hd in range(8):
            for h in range(2):
                ts = slice(h * HALF, (h + 1) * HALF)

                def projq(hd=hd, h=h, ts=ts):
                    ps, bps = self.psn("d")
                    for k in range(3):
                        self.MM(ps[0:96, :], suq[:, k, hd * 96:(hd + 1) * 96], self.CQN[:, k, ts], k == 0, k == 2,
                                [bsuq, self.bCQN[k, h]], [bps])
                    return ps, bps

                def postq(norm, hd=hd, h=h, ts=ts):
                    if h == 0:
                        norm(self.QMT[0:96, hd, ts], [self.bQMT[hd, 0]])
                    else:
                        tb_, btb = self.tb()
                        norm(tb_[0:96, :], [btb])
                        self.rope_from(tb_, btb, 96, Pm, self.ROPE[0:96, 2, :], self.ROPE[0:96, 3, :], self.QMT[0:96, hd, ts], [self.bQMT[hd, 1]])
                units.append(dict(proj=projq, npart=96, ones=ones96, isn=96.0 ** -0.5, gcol=qmg, bgs=[self.bCOLS], post=postq))
        for hd in range(8):
            for seg in range(3):
                n = HALF if seg < 2 else 256

                def projk(hd=hd, seg=seg, n=n):
                    ps, bps = self.psn("d")
                    for k in range(2):
                        if seg < 2:
                            rhs, br = self.CKVN[:, k, seg * HALF:(seg + 1) * HALF], self.bCKVN[k, seg]
                        else:
                            rhs, br = self.CC[:, k, :], self.bCC
                        self.MM(ps[0:96, 0:n], self.WK[:, k, hd, :], rhs, k == 0, False, [self.bWK, br], [bps])
                    if seg < 2:
                        rhs, br = self.KPE[0:32, seg * HALF:(seg + 1) * HALF], self.bKPE[seg]
                    else:
                        rhs, br = self.CKPE[0:32, :], self.bCKPE
                    self.MM(ps[0:96, 0:n], EXT, rhs, False, True, [self.bCB, br], [bps])
                    return ps, bps

                def postk(norm, hd=hd, seg=seg):
                    if seg == 0:
                        norm(self.KMTP[0:96, hd, :], [self.bKMTP[hd]])
                    elif seg == 1:
                        tb_, btb = self.tb()
                        norm(tb_[0:96, :], [btb])
                        st, bst = self.tb()
                        self.rope_from(tb_, btb, 96, Pm, self.ROPE[0:96, 2, :], self.ROPE[0:96, 3, :], st[0:96, :], [bst])
                        self.DMA(kmx[:, hd, :], st[0:96, :], [bst], [self.b_x["kmx"][l]])
                    else:
                        norm(self.KMTC[0:96, hd, :], [self.bKMTC[hd]])
                units.append(dict(proj=projk, npart=96, ones=ones96, isn=96.0 ** -0.5, gcol=kmg, bgs=[self.bCOLS], n=n, post=postk))
        self.bn_run(units)
        vmx = self.vmx_in[l].ap().rearrange("(n p) e -> p n e", p=128)
        for tt in range(10):
            ps, bps = self.psn("d")
            for k in range(2):
                if tt < 8:
                    lhsT, br = self.CKVN[:, k, tt * 128:(tt + 1) * 128], self.bCKVN[k, tt // 4]
                else:
                    lhsT, br = self.CC[:, k, (tt - 8) * 128:(tt - 7) * 128], self.bCC
                self.MM(ps[:], lhsT, self.WV[:, k, :], k == 0, k == 1, [self.bWV, br], [bps])
            if tt < 4:
                self.ACT(self.VMP[:, tt, :], ps[:], AF.Copy, [bps], [self.bVMP[tt]])
            elif tt < 8:
                st, bst = self.tb()
                self.ACT(st[:], ps[:], AF.Copy, [bps], [bst])
                self.DMA(vmx[:, tt - 4, :], st[:], [bst], [self.b_x["vmx"][l]])
            else:
                self.ACT(self.VMC[:, tt - 8, :], ps[:], AF.Copy, [bps], [self.bVMC[tt - 8]])
        self.allgather("kmx", l)
        self.allgather("vmx", l)

    def attn_unit(self, qap, bq, keytiles, scale, e, po, bpo, pz, bpz, nq):
        nt = len(keytiles)
        for i, (kT, bk, v, bv) in enumerate(keytiles):
            psc, bpsc = self.psn("a")
            self.MM(psc[:, 0:nq], kT, qap, True, True, [bk, bq], [bpsc])
            pt, bpt = self.tb()
            self.ACT(pt[:, 0:nq], psc[:, 0:nq], AF.Exp, [bpsc], [bpt], scale=scale)
            self.MM(po, v, pt[:, 0:nq], i == 0, i == nt - 1, [bv, bpt], [bpo])
            self.MM(pz, self.CB[:, 1, 0:e], pt[:, 0:nq], i == 0, i == nt - 1, [self.bCB, bpt], [bpz])

    def diff_fin(self, l, j, h, q0, nq):
        P = self.PS
        b = self.b_PS
        DA, bDA = self.DA, self.b_DA
        r0, br0 = self.tf()
        self.ACT(r0[:, 0:nq], P[5][:, 0:nq], AF.Ln, [b[5]], [br0])
        self.ACT(r0[:, 0:nq], r0[:, 0:nq], AF.Exp, [br0], [br0], scale=-1.0)
        self.TT(DA[0][:, 0:nq], P[4][:, 0:nq], r0[:, 0:nq], ALU.mult, [b[4], br0], [bDA[0]])
        r1, br1 = self.tf()
        self.ACT(r1[:, 0:nq], P[7][:, 0:nq], AF.Ln, [b[7]], [br1])
        self.ACT(r1[:, 0:nq], r1[:, 0:nq], AF.Exp, [br1], [br1], scale=-1.0)
        self.TT(DA[1][:, 0:nq], P[6][:, 0:nq], r1[:, 0:nq], ALU.mult, [b[6], br1], [bDA[1]])

        def tail():
            o = DA[0]
            self.STT(o[:, 0:nq], DA[1][:, 0:nq], self.LAM_[:, l, 5:6], DA[0][:, 0:nq], ALU.mult, ALU.add,
                     [bDA[1], bDA[0], self.bLAMg[l]], [bDA[0]])
            sq, bsq = self.tb()
            self.ACT(sq[:, 0:nq], o[:, 0:nq], AF.Square, [bDA[0]], [bsq], scale=128.0 ** -0.5)
            pst, bpst = self.psn("a")
            self.MM(pst[:, 0:nq], self.ones(), sq[:, 0:nq], True, True, [self.bCB, bsq], [bpst])
            r, br = self.rstd_from(pst, bpst, 128, nq)
            self.STT(self.ODT[:, j, q0:q0 + nq], o[:, 0:nq], self.LAM_[:, l, 6:7], r[:, 0:nq], ALU.mult, ALU.mult,
                     [bDA[0], br, self.bLAMg[l]], [self.bODT[j, h]])
        return tail

    def mla_fin(self, hd, h, q0, nq, ba=None):
        ba0 = ba

        def tail():
            P = self.PS
            b = self.b_PS
            ba = (4 + 2 * (hd % 2)) if ba0 is None else ba0
            pr = slice((hd % 2) * 64, (hd % 2) * 64 + 64)
            r0, br0 = self.tf()
            self.ACT(r0[pr, 0:nq], P[ba + 1][pr, 0:nq], AF.Ln, [b[ba + 1]], [br0])
            self.ACT(r0[pr, 0:nq], r0[pr, 0:nq], AF.Exp, [br0], [br0], scale=-1.0)
            self.TT(self.OMT[pr, hd // 2, q0:q0 + nq], P[ba][pr, 0:nq], r0[pr, 0:nq], ALU.mult, [b[ba], br0], [self.bOMT[hd, h]])
        return tail

    def attn_stream(self, units, defer, bg=None):
        pending = []
        for u in units:
            tiles = u["tiles"]
            nt = len(tiles)
            nq, e = u["nq"], u["e"]
            if u.get("pre") is not None:
                u["pre"]()
            wbanks = set()
            for L in (u.get("lanes") or [u]):
                wbanks.add(L["bpo"])
                wbanks.add(L["bpz"])
            for p in list(pending):
                if any(bb in wbanks for bb in p[2]):
                    pending.remove(p)
                    p[1]()
            lanes = u.get("lanes") or [u]
            for i in range(nt):
                scs = []
                for L in lanes:
                    kT, bk, v, bv = L["tiles"][i]
                    psc, bpsc = self.psn(u.get("spool", "a"))
                    self.MM(psc[:, 0:nq], kT, L["q"], True, True, [bk, L["bq"]], [bpsc])
                    scs.append((psc, bpsc))
                pts = []
                for L, (psc, bpsc) in zip(lanes, scs):
                    pt, bpt = self.tb()
                    self.ACT(pt[:, 0:nq], psc[:, 0:nq], AF.Exp, [bpsc], [bpt], scale=u["scale"])
                    pts.append((pt, bpt))
                for L, (pt, bpt) in zip(lanes, pts):
                    kT, bk, v, bv = L["tiles"][i]
                    self.MM(L["po"], v, pt[:, 0:nq], i == 0, i == nt - 1, [bv, bpt], [L["bpo"]])
                    self.MM(L["pz"], self.CB[:, 1, 0:e], pt[:, 0:nq], i == 0, i == nt - 1, [self.bCB, bpt], [L["bpz"]])
                for p in pending:
                    p[0] -= 1
                while pending and pending[0][0] <= 0:
                    pending.pop(0)[1]()
                if bg:
                    bg.pop(0)()
            if u.get("fin") is not None:
                t = u["fin"]()
                if t is not None:
                    pending.append([defer, t, u.get("fin_banks", ())])
        for p in pending:
            p[1]()
        while bg:
            bg.pop(0)()

    def stage_attn_P(self, l):
        P, b = self.PS, self.b_PS
        units = []
        for s in range(2):
            q0 = s * 256
            for j in range(4):
                for c in range(2):
                    pr = slice(c * 64, (c + 1) * 64)
                    kts = [(self.KTP[pr, j, q0 + i * 128:q0 + (i + 1) * 128], self.bKTP[j],
                            self.VP[:, 2 * s + i, j * 128:(j + 1) * 128], self.bVP[2 * s + i]) for i in range(2)]
                    units.append(dict(tiles=kts, q=self.QT[pr, j, q0:q0 + 256], bq=self.bQT[j, 0], scale=0.125, e=128,
                                      po=P[4 + 2 * c][:, 0:256], bpo=b[4 + 2 * c], pz=P[5 + 2 * c][:, 0:256], bpz=b[5 + 2 * c], nq=256,
                                      fin=(lambda j=j, q0=q0: self.diff_fin(l, j, 0, q0, 256)) if c == 1 else None))
            for hd in range(8):
                kts = [(self.KMTP[0:96, hd, q0 + i * 128:q0 + (i + 1) * 128], self.bKMTP[hd],
                        self.VMP[:, 2 * s + i, hd * 64:(hd + 1) * 64], self.bVMP[2 * s + i]) for i in range(2)]
                pr = slice((hd % 2) * 64, (hd % 2) * 64 + 64)
                ba = 4 + 2 * (hd % 2)
                units.append(dict(tiles=kts, q=self.QMT[0:96, hd, q0:q0 + 256], bq=self.bQMT[hd, 0], scale=96.0 ** -0.5, e=64,
                                  po=P[ba][pr, 0:256], bpo=b[ba], pz=P[ba + 1][pr, 0:256], bpz=b[ba + 1], nq=256,
                                  fin=(lambda hd=hd, q0=q0: self.mla_fin(hd, 0, q0, 256)), fin_banks=(b[ba], b[ba + 1])))
        self.attn_stream(units, 3)

    def stage_attn_S(self, l):
        P, b = self.PS, self.b_PS
        q0 = HALF
        units = []
        slot = [0]

        import os
        PAIR = os.environ.get("KPAIR", "0") == "1"
        for j in range(4):
            s = slot[0] % 2
            slot[0] += 1
            lanes = []
            for c in range(2):
                pr = slice(c * 64, (c + 1) * 64)
                kts = [(self.KC[pr, j, i * 128:(i + 1) * 128], self.bKC, self.VC[:, i, j * 128:(j + 1) * 128], self.bVC) for i in range(2)]
                kts += [(self.KS[s][pr, i * 128:(i + 1) * 128], self.bKS[s], self.VS[s][:, i, :], self.bVS[s]) for i in range(16)]
                lanes.append(dict(tiles=kts, q=self.QT[pr, j, q0:q0 + HALF], bq=self.bQT[j, 1], scale=0.125, e=128,
                                  po=P[4 + 2 * c][:, :], bpo=b[4 + 2 * c], pz=P[5 + 2 * c][:, :], bpz=b[5 + 2 * c], nq=HALF))
            if PAIR:
                u = dict(lanes[0])
                u["lanes"] = lanes
                u["fin"] = (lambda j=j: self.diff_fin(l, j, 1, q0, HALF))
                u["load"] = ("d", j, s)
                units.append(u)
            else:
                lanes[0]["load"] = ("d", j, s)
                lanes[0]["fin"] = None
                lanes[1]["fin"] = (lambda j=j: self.diff_fin(l, j, 1, q0, HALF))
                units += lanes
        for hd in range(8):
            s = slot[0] % 2
            slot[0] += 1
            pr = slice((hd % 2) * 64, (hd % 2) * 64 + 64)
            ba = 4
            kts = [(self.KMTC[0:96, hd, i * 128:(i + 1) * 128], self.bKMTC[hd], self.VMC[:, i, hd * 64:(hd + 1) * 64], self.bVMC[i]) for i in range(2)]
            kts += [(self.KS[s][0:96, i * 128:(i + 1) * 128], self.bKS[s], self.VS[s][:, i, 0:64], self.bVS[s]) for i in range(16)]
            units.append(dict(tiles=kts, q=self.QMT[0:96, hd, q0:q0 + HALF], bq=self.bQMT[hd, 1], scale=96.0 ** -0.5, e=64,
                              po=P[ba][pr, :], bpo=b[ba], pz=P[ba + 1][pr, :], bpz=b[ba + 1], nq=HALF,
                              fin=(lambda hd=hd: self.mla_fin(hd, 1, q0, HALF, ba=4)), fin_banks=(b[ba], b[ba + 1]), load=("m", hd, s),
                              spool="a6" if hd > 0 else "a"))
        kxo = self.kx_out[l].ap().rearrange("(r f) t -> f r t", r=4)
        vxo = self.vx_out[l].ap().rearrange("(n p) e -> p n e", p=128)
        kmo = self.kmx_out[l].ap().rearrange("(r f) t -> f r t", r=4)
        vmo = self.vmx_out[l].ap().rearrange("(n p) e -> p n e", p=128)

        def do_load(ld):
            kind, idx, s = ld
            if kind == "d":
                self.DMA(self.KS[s].rearrange("p (r t) -> p r t", r=4), kxo[idx * 128:(idx + 1) * 128], [self.b_xo["kx"][l]], [self.bKS[s]])
                self.DMA(self.VS[s], vxo[:, :, idx * 128:(idx + 1) * 128], [self.b_xo["vx"][l]], [self.bVS[s]])
            else:
                self.DMA(self.KS[s][0:96, :].rearrange("p (r t) -> p r t", r=4), kmo[idx * 96:(idx + 1) * 96], [self.b_xo["kmx"][l]], [self.bKS[s]])
                self.DMA(self.VS[s][:, :, 0:64], vmo[:, :, idx * 64:(idx + 1) * 64], [self.b_xo["vmx"][l]], [self.bVS[s]])
        for u in units:
            if u.get("load") is not None:
                u["pre"] = (lambda ld=u["load"]: do_load(ld))
        bg = self.conv_bg(l)
        step = len(bg) // 9
        for n_, j in enumerate(range(4, 12)):
            bg.insert((n_ + 1) * step + n_, (lambda j=j: self.mod_block(l, j, pool="a")))
        self.attn_stream(units, 4, bg=bg)

    def conv_bg(self, l):
        ops = []
        hbo = self.hb_out[l].ap().rearrange("(r j p) c -> p j r c", r=4, p=128)

        def halo():
            for j in range(4):
                self.DMA(self.HL[:, j], hbo[:, j], [self.b_xo["hb"][l]], [self.bHL])
            for side, c0, dst0 in ((0, 17, 0), (1, 0, 527)):
                for r in range(4):
                    if r == 0:
                        self.TS(self.HLF[:, :, side, 0:15], self.HL[:, :, r, c0:c0 + 15], self.SEL[:, side * 4 + r:side * 4 + r + 1], None,
                                ALU.mult, None, [self.bHL, self.bSEL], [self.bHLF])
                    else:
                        self.STT(self.HLF[:, :, side, 0:15], self.HL[:, :, r, c0:c0 + 15], self.SEL[:, side * 4 + r:side * 4 + r + 1],
                                 self.HLF[:, :, side, 0:15], ALU.mult, ALU.add, [self.bHL, self.bSEL, self.bHLF], [self.bHLF])
                self.CP(self.US[:, :, dst0:dst0 + 15], self.HLF[:, :, side, 0:15], [self.bHLF], [self.bUS[j] for j in range(4)])
        ops.append(halo)
        accP = self.CACC[:, 0:HALF].rearrange("p (s t) -> p s t", s=2)
        accS = self.CACC[:, HALF:T]
        for j in range(4):
            for kk in range(31):
                dwc = self.col(l, self.C_DW + j * 31 + kk)

                def tapP(j=j, kk=kk, dwc=dwc):
                    src = self.UP[:, j, :, kk:kk + 256]
                    if kk == 0:
                        self.TS(accP, src, dwc, None, ALU.mult, None, [self.bUP[j], self.bCOLS], [self.bACC[0]])
                    else:
                        self.STT(accP, src, dwc, accP, ALU.mult, ALU.add, [self.bUP[j], self.bCOLS, self.bACC[0]], [self.bACC[0]])

                def tapS(j=j, kk=kk, dwc=dwc):
                    src = self.US[:, j, kk:kk + HALF]
                    if kk == 0:
                        self.TS(accS, src, dwc, None, ALU.mult, None, [self.bUS[j], self.bCOLS], [self.bACC[1]])
                    else:
                        self.STT(accS, src, dwc, accS, ALU.mult, ALU.add, [self.bUS[j], self.bCOLS, self.bACC[1]], [self.bACC[1]])
                ops.append(tapP)
                ops.append(tapS)

            def fin(j=j):
                cb = self.col(l, self.C_CB + j)
                self.TS(self.UP[:, j, :, 15:271], accP, cb, None, ALU.add, None, [self.bACC[0], self.bCOLS], [self.bUP[j]])
                self.TS(self.US[:, j, 15:527], accS, cb, None, ALU.add, None, [self.bACC[1], self.bCOLS], [self.bUS[j]])
            ops.append(fin)
        return ops

    def stage_conv(self, l):
        P, b = self.PS, self.b_PS
        stats = [(P[6], b[6]), (P[7], b[7])]

        def ysrc(j, h):
            if h == 0:
                return self.UP[:, j, :, 15:271], self.bUP[j]
            return self.US[:, j, 15:527], self.bUS[j]
        for j in range(4):
            for h in range(2):
                y, by = ysrc(j, h)
                sq, bsq = self.tb()
                sqv = sq[:].rearrange("p (s t) -> p s t", s=2) if h == 0 else sq[:]
                self.ACT(sqv, y, AF.Square, [by], [bsq], scale=512.0 ** -0.5)
                pst, bpst = stats[h]
                self.MM(pst[:], self.ones(), sq[:], j == 0, j == 3, [self.bCB, bsq], [bpst])
        for h in range(2):
            ts = slice(h * HALF, (h + 1) * HALF)
            r, br = self.rstd_from(stats[h][0], stats[h][1])
            for j in range(4):
                y, by = ysrc(j, h)
                z, bz = self.tf()
                zv = z[:].rearrange("p (s t) -> p s t", s=2) if h == 0 else z[:]
                rv = r[:].rearrange("p (s t) -> p s t", s=2) if h == 0 else r[:]
                self.STT(zv, y, self.col(l, self.C_CNG + j), rv, ALU.mult, ALU.mult, [by, self.bCOLS, br], [bz])
                self.ACT(self.ST[:, j, ts], z[:], AF.Silu, [bz], [self.bST[j, h]])

    def stage_merge(self, l):
        self.DMA(self.WCO, self.w_conv_out[l].rearrange("(k p) c -> p k c", p=128), (), [self.bWCO], q="pool")
        self.DMA(self.WDO, self.w_diff_out[l].rearrange("(k p) c -> p k c", p=128), (), [self.bWDO], q="pool")
        self.DMA(self.WMO, self.w_mla_out[l].rearrange("(k p) c -> p k c", p=128), (), [self.bWMO], q="pool")
        branches = ((self.WCO, self.bWCO, self.ST, self.bST), (self.WDO, self.bWDO, self.ODT, self.bODT),
                    (self.WMO, self.bWMO, self.OMT, self.bOMT))
        for j in range(8):
            sg, bsg = self.wnext(f"mg{l}_{j}")
            for h in range(2):
                ts = slice(h * HALF, (h + 1) * HALF)
                acc = None
                for bi, (wt, bwt, src, bsrc) in enumerate(branches):
                    pg, bpg = self.psn("d")
                    for k in range(8):
                        self.MM(pg[:], sg[:, k, bi, :], self.HT[:, k, ts], k == 0, k == 7, [bsg, self.bHT[k, h]], [bpg])
                    po, bpo = self.psn("d")
                    for k in range(4):
                        if bi == 2:
                            rb = [self.bOMT[2 * k, h], self.bOMT[2 * k + 1, h]]
                        else:
                            rb = [bsrc[k, h]]
                        self.MM(po[:], wt[:, k, j * 128:(j + 1) * 128], src[:, k, ts], k == 0, k == 3, [bwt] + rb, [bpo])
                    sgm, bsgm = self.tf()
                    self.ACT(sgm[:], pg[:], AF.Sigmoid, [bpg], [bsgm])
                    if bi == 0:
                        acc, bacc = self.tf()
                        self.TT(acc[:], po[:], sgm[:], ALU.mult, [bpo, bsgm], [bacc])
                    elif bi == 1:
                        t1, bt1 = self.tf()
                        self.TT(t1[:], po[:], sgm[:], ALU.mult, [bpo, bsgm], [bt1])
                        self.TT(acc[:], acc[:], t1[:], ALU.add, [bacc, bt1], [bacc])
                    else:
                        t1, bt1 = self.tf()
                        self.TT(t1[:], po[:], sgm[:], ALU.mult, [bpo, bsgm], [bt1])
                        self.TT(self.MT[:, j, ts], acc[:], t1[:], ALU.add, [bacc, bt1], [self.bMT[j, h]])
        for blk in range(2):
            so, bso = self.wnext(f"wo{l}_{blk}")
            for jj in range(4):
                j = blk * 4 + jj
                for h in range(2):
                    ts = slice(h * HALF, (h + 1) * HALF)
                    po, bpo = self.proj(so, bso, jj * 128, 128, h, src=self.MT, bsrc=self.bMT)
                    self.STT(self.X[:, j, ts], po[:], self.modcol(l, 2, j, h), self.X[:, j, ts], ALU.mult, ALU.add,
                             [bpo, self.bX[j, h]] + self.bmod(l, 2), [self.bX[j, h]])

    def stage_mlp(self, l):
        for q in range(4):
            for blk in range(2):
                su, bsu = self.wnext(f"up{l}_{q}_{blk}")
                for jj in range(4):
                    jq = blk * 4 + jj
                    for h in range(2):
                        ts = slice(h * HALF, (h + 1) * HALF)
                        pu, bpu = self.proj(su, bsu, jj * 128, 128, h)
                        t, bt = self.tf()
                        self.ACT(t[:], pu[:], AF.Relu, [bpu], [bt])
                        self.TT(self.AT[:, jq, ts], t[:], t[:], ALU.mult, [bt], [self.bAT[jq, h]])
            for blk in range(2):
                sd, bsd = self.wnext(f"dn{l}_{q}_{blk}")
                for jj in range(4):
                    j = blk * 4 + jj
                    for h in range(2):
                        ts = slice(h * HALF, (h + 1) * HALF)
                        pd, bpd = self.proj(sd, bsd, jj * 128, 128, h, src=self.AT, bsrc=self.bAT)
                        self.STT(self.X[:, j, ts], pd[:], self.modcol(l, 5, j, h), self.X[:, j, ts], ALU.mult, ALU.add,
                                 [bpd, self.bX[j, h]] + self.bmod(l, 5), [self.bX[j, h]])
            if l + 1 < DEPTH and q < 2:
                self.mod_block(l + 1, 2 * q)
                self.mod_block(l + 1, 2 * q + 1)

    def epilogue(self):
        fin = []
        for k in range(8):
            fin.append(self.DMA(self.yT[k * 128:(k + 1) * 128, :], self.X[:, k, :], [self.bX[k, 0], self.bX[k, 1]], []))
        return fin


def _rope_tables(rank):
    t = np.arange(rank * 512, rank * 512 + 512)
    pos = [(t // 64).astype(np.float64), (t % 64).astype(np.float64)]
    out = np.zeros((128, 4, 512), np.float64)
    inv = 10000.0 ** (-np.arange(0, 32, 2, dtype=np.float64) / 32.0)
    for c in range(2):
        for a in range(2):
            for p in range(2):
                for i in range(16):
                    d = c * 64 + a * 32 + p * 16 + i
                    ang = pos[a] * inv[i]
                    out[d, 0] = np.cos(ang)
                    out[d, 1] = np.sin(ang) * (-1.0 if p == 0 else 1.0)
    inv = 10000.0 ** (-np.arange(0, 16, 2, dtype=np.float64) / 16.0)
    out[0:64, 2] = 1.0
    for a in range(2):
        for p in range(2):
            for i in range(8):
                d = 64 + a * 16 + p * 8 + i
                ang = pos[a] * inv[i]
                out[d, 2] = np.cos(ang)
                out[d, 3] = np.sin(ang) * (-1.0 if p == 0 else 1.0)
    return out.astype(np.float32)


def _const_mats():
    m = np.zeros((128, 6, 128), np.float32)
    m[:, 0, :] = np.eye(128)
    m[:, 1, :] = 1.0
    m[0:64, 2, 0:64] = 1.0
    m[64:128, 2, 64:128] = 1.0
    for c in range(2):
        for a in range(2):
            for p in range(2):
                for i in range(16):
                    d = c * 64 + a * 32 + p * 16 + i
                    ds = c * 64 + a * 32 + (1 - p) * 16 + i
                    m[ds, 3, d] = 1.0
    for d in range(64):
        m[d, 4, d] = 1.0
    for a in range(2):
        for p in range(2):
            for i in range(8):
                d = 64 + a * 16 + p * 8 + i
                ds = 64 + a * 16 + (1 - p) * 8 + i
                m[ds, 4, d] = 1.0
    for i in range(32):
        m[i, 5, 64 + i] = 1.0
    return m


def _pack_cols(inp):
    cols = np.zeros((128, DEPTH, NCOLS), np.float32)

    def colform(v, nchunk):
        return np.ascontiguousarray(v.reshape(nchunk, 128).T)
    for l in range(DEPTH):
        cols[:, l, 0:96] = np.repeat(colform(inp["mod_b"][l], 48)[:, :, None], 2, axis=2).reshape(128, 96)
        cols[:, l, 96:112] = np.repeat(colform(inp["norm1_g"][l], 8)[:, :, None], 2, axis=2).reshape(128, 16)
        cols[:, l, 112:128] = np.repeat(colform(inp["norm2_g"][l], 8)[:, :, None], 2, axis=2).reshape(128, 16)
        dw = inp["conv_dw"][l]
        cols[:, l, 128:252] = dw.T.reshape(4, 128, 31).transpose(1, 0, 2).reshape(128, 124)
        cols[:, l, 252:256] = colform(inp["conv_b"][l], 4)
        cols[:, l, 256:260] = colform(inp["conv_norm_g"][l], 4)
        cols[:, l, 260] = np.tile(inp["diff_q_norm"][l], 2)
        cols[:, l, 261] = np.tile(inp["diff_k_norm"][l], 2)
        cols[:, l, 262] = inp["diff_subln"][l]
        cols[:, l, 263:266] = colform(inp["mla_q_a_norm"][l], 3)
        cols[:, l, 266:268] = colform(inp["mla_kv_a_norm"][l], 2)
        cols[0:96, l, 268] = inp["mla_q_norm"][l]
        cols[0:96, l, 269] = inp["mla_k_norm"][l]
        cols[:, l, 270:526] = np.broadcast_to(inp["diff_lambda"][l].reshape(1, 256), (128, 256))
    return cols


_NC_CACHE = {}


def get_nc(dbg=()):
    key = tuple(dbg)
    if key not in _NC_CACHE:
        kb = KB(dbg)
        kb.build()
        _NC_CACHE[key] = kb
    return _NC_CACHE[key]


def make_in_maps(inp):
    inp = {k: np.asarray(v) for k, v in inp.items()}
    cols = _pack_cols(inp)
    constm = _const_mats()
    shared = {k: np.ascontiguousarray(inp[k], dtype=np.float32) for k in
              ("mod_w", "w_in", "w_conv_out", "w_diff_out", "w_uq", "w_ukv", "w_mla_out", "w_out", "w_up", "w_down")}
    maps = []
    for c in range(8):
        g, r = c // 4, c % 4
        xp = inp["x_prompt"][2 * c:2 * c + 2].reshape(512, D)
        xs = inp["x_sample"][g, r * 512:(r + 1) * 512]
        xT = np.ascontiguousarray(np.concatenate([xp, xs], 0).T)
        cond = np.stack([inp["c_ctx"].reshape(8, 128).T, inp["c"][g].reshape(8, 128).T], axis=2)
        sel = np.zeros((128, 8), np.float32)
        if r > 0:
            sel[:, r - 1] = 1.0
        if r < 3:
            sel[:, 4 + r + 1] = 1.0
        m = dict(shared)
        m.update({
            "xT": xT, "cond": np.ascontiguousarray(cond, dtype=np.float32), "cols": cols, "constm": constm,
            "rope": _rope_tables(r), "sel": sel,
            "ckT": np.ascontiguousarray(inp["cache_diff_k"][g].reshape(DEPTH, 256, 512).transpose(0, 2, 1)),
            "cv": np.ascontiguousarray(inp["cache_diff_v"][g].reshape(DEPTH, 256, 512)),
            "cckvT": np.ascontiguousarray(inp["cache_mla_ckv"][g].transpose(0, 2, 1)),
            "ckpeT": np.ascontiguousarray(inp["cache_mla_kpe"][g].transpose(0, 2, 1)),
        })
        maps.append(m)
    return maps


def assemble(results):
    y_prompt = np.zeros((16, 256, D), np.float32)
    y_sample = np.zeros((2, 2048, D), np.float32)
    ndk = np.zeros((16, DEPTH, 256, 4, 128), np.float32)
    ndv = np.zeros((16, DEPTH, 256, 4, 128), np.float32)
    nckv = np.zeros((16, DEPTH, 256, 256), np.float32)
    nkpe = np.zeros((16, DEPTH, 256, 32), np.float32)
    for c in range(8):
        g, r = c // 4, c % 4
        res = results[c]
        yT = res["yT"]
        y_prompt[2 * c:2 * c + 2] = yT[:, 0:512].T.reshape(2, 256, D)
        y_sample[g, r * 512:(r + 1) * 512] = yT[:, 512:].T
        for l in range(DEPTH):
            ndk[2 * c:2 * c + 2, l] = res["o_dk"][l].T.reshape(2, 256, 4, 128)
            ndv[2 * c:2 * c + 2, l] = res["o_dv"][l].reshape(2, 256, 4, 128)
            nckv[2 * c:2 * c + 2, l] = res["o_ckv"][l].T.reshape(2, 256, 256)
            nkpe[2 * c:2 * c + 2, l] = res["o_kpe"][l].T.reshape(2, 256, 32)
    return (y_prompt, y_sample, ndk, ndv, nckv, nkpe)


def kernel(**inputs):
    kb = get_nc()
    maps = make_in_maps(inputs)
    res = run_bass_kernel_spmd(kb.nc, maps, core_ids=list(range(8)))
    return assemble(res.results)
```
